# Optimizing a Trainium2 kernel written in Bass

```python
import math
import jax, jax.numpy as jnp
from jax import lax
import numpy as np

D_MODEL = 1024
BATCH = 8
SEQ = 2048
DEPTH = 1
DEC_BATCH = 128
DEC_SEQ = 8
PAST_LEN = 16384
PAGE_SIZE = 128

N_MEM = 256
D_MIX = D_MODEL
D_RET = D_MIX // 2
D_LRU = D_MIX - D_RET
RET_HEADS = 4
RET_HEAD_DIM = D_RET // RET_HEADS
RET_CHUNK = 128
ROPE_BASE = 10000.0
LRU_BLOCKS = 8
LRU_BLOCK_DIM = D_LRU // LRU_BLOCKS
CONV_WIDTH = 4
LRU_C = 8.0
XATTN_HEADS = 4
XATTN_HEAD_DIM = D_MODEL // XATTN_HEADS
D_FF = ((8 * D_MODEL // 3) + 127) // 128 * 128
D_IN = 4 * D_RET + 2 * D_LRU
EPS = 1e-6

kernel_name = 'hybrid_retention_rglru_macaron_decoder'


def rmsnorm(x, g):
    xf = x.astype(jnp.float32)
    y = xf * lax.rsqrt(jnp.mean(xf * xf, axis=-1, keepdims=True) + EPS)
    return (y * g.astype(jnp.float32)).astype(x.dtype)


def swiglu(x, wg, wu, wd):
    return (jax.nn.silu(x @ wg) * (x @ wu)) @ wd


def rotary(x, pos):
    half = x.shape[-1] // 2
    inv = ROPE_BASE ** (-jnp.arange(half, dtype=jnp.float32) / half)
    ang = pos.astype(jnp.float32)[:, None] * inv[None, :]
    cos = jnp.cos(ang)[None, :, None, :]
    sin = jnp.sin(ang)[None, :, None, :]
    x1, x2 = x[..., :half], x[..., half:]
    return jnp.concatenate([x1 * cos - x2 * sin, x1 * sin + x2 * cos], axis=-1)


def retention(q, k, v, s0):
    b, t, h, d = q.shape
    c = t if t <= RET_CHUNK else RET_CHUNK
    n_chunks = t // c
    lg = jnp.log(1.0 - 2.0 ** (-5.0 - jnp.arange(h, dtype=jnp.float32)))
    idx = jnp.arange(c, dtype=jnp.float32)
    diff = idx[:, None] - idx[None, :]
    dmask = jnp.where(diff[None] >= 0, jnp.exp(lg[:, None, None] * jnp.maximum(diff, 0.0)[None]), 0.0)
    q_dec = jnp.exp(lg[None, :] * (idx[:, None] + 1.0))
    k_dec = jnp.exp(lg[None, :] * (c - 1.0 - idx[:, None]))
    chunk_dec = jnp.exp(lg * c)

    def step(s, qkv):
        qc, kc, vc = qkv
        sc = jnp.einsum('bihd,bjhd->bhij', qc, kc) * dmask[None]
        o = jnp.einsum('bhij,bjhe->bihe', sc, vc) + jnp.einsum('bihd,bhde->bihe', qc, s) * q_dec[None, :, :, None]
        s = chunk_dec[None, :, None, None] * s + jnp.einsum('bjhd,bjhe->bhde', kc * k_dec[None, :, :, None], vc)
        return s, o

    to_chunks = lambda a: a.reshape(b, n_chunks, c, h, d).transpose(1, 0, 2, 3, 4)
    s_last, o = lax.scan(step, s0, (to_chunks(q), to_chunks(k), to_chunks(v)))
    o = o.transpose(1, 0, 2, 3, 4).reshape(b, t, h, d)
    return o, s_last


def causal_conv(u, buf, w, bias):
    t = u.shape[1]
    full = jnp.concatenate([buf.astype(u.dtype), u], axis=1)
    out = full[:, 0:t] * w[0]
    for j in range(1, CONV_WIDTH):
        out = out + full[:, j:j + t] * w[j]
    return out + bias, full[:, t:]


def rglru(xc, h0, wa, ba, wx, bx, lam):
    b, t, w = xc.shape
    xb = xc.reshape(b, t, LRU_BLOCKS, LRU_BLOCK_DIM)
    r = jax.nn.sigmoid(jnp.einsum('btnk,nkj->btnj', xb, wa.astype(jnp.float32)).reshape(b, t, w) + ba)
    i = jax.nn.sigmoid(jnp.einsum('btnk,nkj->btnj', xb, wx.astype(jnp.float32)).reshape(b, t, w) + bx)
    log_a = -LRU_C * r * jax.nn.softplus(-lam.astype(jnp.float32))
    a = jnp.exp(log_a)
    gx = jnp.sqrt(-jnp.expm1(2.0 * log_a)) * (i * xc)

    def step(h, ab):
        a_t, g_t = ab
        h = a_t * h + g_t
        return h, h

    h_last, hs = lax.scan(step, h0, (a.transpose(1, 0, 2), gx.transpose(1, 0, 2)))
    return hs.transpose(1, 0, 2), h_last


def mem_kv(mem, wk, wv):
    b, m, _ = mem.shape
    k = (mem @ wk).reshape(b, m, XATTN_HEADS, XATTN_HEAD_DIM)
    v = (mem @ wv).reshape(b, m, XATTN_HEADS, XATTN_HEAD_DIM)
    return k, v


def cross_attn(h, mk, mv, wq, wo):
    b, t, _ = h.shape
    q = (h @ wq).reshape(b, t, XATTN_HEADS, XATTN_HEAD_DIM)
    sc = jnp.einsum('bthd,bmhd->bhtm', q.astype(jnp.float32), mk.astype(jnp.float32)) * (XATTN_HEAD_DIM ** -0.5)
    pr = jax.nn.softmax(sc, axis=-1)
    o = jnp.einsum('bhtm,bmhd->bthd', pr, mv.astype(jnp.float32)).reshape(b, t, D_MODEL)
    return o.astype(h.dtype) @ wo


def layer(x, pos, s0, h0, conv0, mk, mv, p, l):
    b, t, _ = x.shape
    x = x + 0.5 * swiglu(rmsnorm(x, p['ffn1_norm'][l]), p['ffn1_wg'][l], p['ffn1_wu'][l], p['ffn1_wd'][l])
    hn = rmsnorm(x, p['mix_norm'][l])
    z = hn @ p['w_in'][l]
    q, k, v, g = (z[..., j * D_RET:(j + 1) * D_RET] for j in range(4))
    u = z[..., 4 * D_RET:4 * D_RET + D_LRU]
    gate = z[..., 4 * D_RET + D_LRU:]
    hd = (b, t, RET_HEADS, RET_HEAD_DIM)
    qf = rotary(q.reshape(hd).astype(jnp.float32), pos)
    kf = rotary(k.reshape(hd).astype(jnp.float32), pos) * (RET_HEAD_DIM ** -0.5)
    vf = v.reshape(hd).astype(jnp.float32)
    o, s_new = retention(qf, kf, vf, s0.astype(jnp.float32))
    mu = jnp.mean(o, axis=-1, keepdims=True)
    var = jnp.mean((o - mu) ** 2, axis=-1, keepdims=True)
    o = ((o - mu) * lax.rsqrt(var + EPS)).reshape(b, t, D_RET) * p['ret_gn_gain'][l].astype(jnp.float32)
    ret_out = (jax.nn.silu(g.astype(jnp.float32)) * o).astype(x.dtype)
    uc, conv_new = causal_conv(u, conv0, p['conv_w'][l], p['conv_b'][l])
    hs, h_new = rglru(uc.astype(jnp.float32), h0.astype(jnp.float32), p['lru_wa'][l], p['lru_ba'][l],
                      p['lru_wx'][l], p['lru_bx'][l], p['lru_lambda'][l])
    lru_out = (rmsnorm(hs, p['lru_norm'][l]) * jax.nn.gelu(gate.astype(jnp.float32))).astype(x.dtype)
    x = x + jnp.concatenate([ret_out, lru_out], axis=-1) @ p['w_out'][l]
    x = x + cross_attn(rmsnorm(x, p['xattn_norm'][l]), mk, mv, p['xattn_wq'][l], p['xattn_wo'][l])
    x = x + 0.5 * swiglu(rmsnorm(x, p['ffn2_norm'][l]), p['ffn2_wg'][l], p['ffn2_wu'][l], p['ffn2_wd'][l])
    return x, s_new, h_new, conv_new


def setup_inputs(seed: int = 0) -> dict:
    key = jax.random.key(seed)
    ks = jax.random.split(key, 40)
    nrm = lambda k, shape, scale: jax.random.normal(k, shape, jnp.float32) * scale
    gain = lambda k, shape: 1.0 + nrm(k, shape, 0.02)
    L = DEPTH
    u = jax.random.uniform(ks[20], (L, D_LRU), jnp.float32, 0.9, 0.999)
    s = u ** (1.0 / LRU_C)
    return {
        'x_prompt': nrm(ks[0], (BATCH, SEQ, D_MODEL), 1.0),
        'x_sample': nrm(ks[1], (DEC_BATCH, DEC_SEQ, D_MODEL), 1.0),
        'state_ret': nrm(ks[2], (L, DEC_BATCH, RET_HEADS, RET_HEAD_DIM, RET_HEAD_DIM), 0.1),
        'state_lru_h': nrm(ks[3], (L, DEC_BATCH, D_LRU), 0.5),
        'state_lru_conv': nrm(ks[4], (L, DEC_BATCH, CONV_WIDTH - 1, D_LRU), 1.0),
        'cache_mem_k': nrm(ks[5], (L, DEC_BATCH, N_MEM, XATTN_HEADS, XATTN_HEAD_DIM), 1.0),
        'cache_mem_v': nrm(ks[6], (L, DEC_BATCH, N_MEM, XATTN_HEADS, XATTN_HEAD_DIM), 1.0),
        'mem_prompt': nrm(ks[7], (BATCH, N_MEM, D_MODEL), 1.0),
        'ffn1_norm': gain(ks[8], (L, D_MODEL)),
        'ffn1_wg': nrm(ks[9], (L, D_MODEL, D_FF), D_MODEL ** -0.5),
        'ffn1_wu': nrm(ks[10], (L, D_MODEL, D_FF), D_MODEL ** -0.5),
        'ffn1_wd': nrm(ks[11], (L, D_FF, D_MODEL), D_FF ** -0.5),
        'mix_norm': gain(ks[12], (L, D_MODEL)),
        'w_in': nrm(ks[13], (L, D_MODEL, D_IN), D_MODEL ** -0.5),
        'ret_gn_gain': gain(ks[14], (L, D_RET)),
        'conv_w': nrm(ks[15], (L, CONV_WIDTH, D_LRU), CONV_WIDTH ** -0.5),
        'conv_b': nrm(ks[16], (L, D_LRU), 0.02),
        'lru_wa': nrm(ks[17], (L, LRU_BLOCKS, LRU_BLOCK_DIM, LRU_BLOCK_DIM), LRU_BLOCK_DIM ** -0.5),
        'lru_ba': nrm(ks[18], (L, D_LRU), 0.02),
        'lru_wx': nrm(ks[19], (L, LRU_BLOCKS, LRU_BLOCK_DIM, LRU_BLOCK_DIM), LRU_BLOCK_DIM ** -0.5),
        'lru_bx': nrm(ks[21], (L, D_LRU), 0.02),
        'lru_lambda': jnp.log(s) - jnp.log1p(-s),
        'lru_norm': gain(ks[22], (L, D_LRU)),
        'w_out': nrm(ks[23], (L, D_MIX, D_MODEL), D_MIX ** -0.5),
        'xattn_norm': gain(ks[24], (L, D_MODEL)),
        'xattn_wq': nrm(ks[25], (L, D_MODEL, D_MODEL), D_MODEL ** -0.5),
        'xattn_wk': nrm(ks[26], (L, D_MODEL, D_MODEL), D_MODEL ** -0.5),
        'xattn_wv': nrm(ks[27], (L, D_MODEL, D_MODEL), D_MODEL ** -0.5),
        'xattn_wo': nrm(ks[28], (L, D_MODEL, D_MODEL), D_MODEL ** -0.5),
        'ffn2_norm': gain(ks[29], (L, D_MODEL)),
        'ffn2_wg': nrm(ks[30], (L, D_MODEL, D_FF), D_MODEL ** -0.5),
        'ffn2_wu': nrm(ks[31], (L, D_MODEL, D_FF), D_MODEL ** -0.5),
        'ffn2_wd': nrm(ks[32], (L, D_FF, D_MODEL), D_FF ** -0.5),
        'final_norm': gain(ks[33], (D_MODEL,)),
    }


def reference(x_prompt, x_sample, state_ret, state_lru_h, state_lru_conv, cache_mem_k, cache_mem_v, mem_prompt,
              ffn1_norm, ffn1_wg, ffn1_wu, ffn1_wd, mix_norm, w_in, ret_gn_gain, conv_w, conv_b,
              lru_wa, lru_ba, lru_wx, lru_bx, lru_lambda, lru_norm, w_out,
              xattn_norm, xattn_wq, xattn_wk, xattn_wv, xattn_wo,
              ffn2_norm, ffn2_wg, ffn2_wu, ffn2_wd, final_norm):
    p = {'ffn1_norm': ffn1_norm, 'ffn1_wg': ffn1_wg, 'ffn1_wu': ffn1_wu, 'ffn1_wd': ffn1_wd,
         'mix_norm': mix_norm, 'w_in': w_in, 'ret_gn_gain': ret_gn_gain, 'conv_w': conv_w, 'conv_b': conv_b,
         'lru_wa': lru_wa, 'lru_ba': lru_ba, 'lru_wx': lru_wx, 'lru_bx': lru_bx, 'lru_lambda': lru_lambda,
         'lru_norm': lru_norm, 'w_out': w_out, 'xattn_norm': xattn_norm, 'xattn_wq': xattn_wq,
         'xattn_wo': xattn_wo, 'ffn2_norm': ffn2_norm, 'ffn2_wg': ffn2_wg, 'ffn2_wu': ffn2_wu, 'ffn2_wd': ffn2_wd}
    bp, tp, _ = x_prompt.shape
    ts = x_sample.shape[1]
    pos_p = jnp.arange(tp, dtype=jnp.int32)
    pos_s = PAST_LEN + jnp.arange(ts, dtype=jnp.int32)
    s0_p = jnp.zeros((bp, RET_HEADS, RET_HEAD_DIM, RET_HEAD_DIM), jnp.float32)
    h0_p = jnp.zeros((bp, D_LRU), jnp.float32)
    c0_p = jnp.zeros((bp, CONV_WIDTH - 1, D_LRU), x_prompt.dtype)
    yp, ys = x_prompt, x_sample
    p_ret, p_h, p_conv, p_mk, p_mv, s_ret, s_h, s_conv = [], [], [], [], [], [], [], []
    for l in range(DEPTH):
        mk, mv = mem_kv(mem_prompt, xattn_wk[l], xattn_wv[l])
        yp, sp, hp, cp = layer(yp, pos_p, s0_p, h0_p, c0_p, mk, mv, p, l)
        ys, ss, hs, cs = layer(ys, pos_s, state_ret[l], state_lru_h[l], state_lru_conv[l],
                               cache_mem_k[l], cache_mem_v[l], p, l)
        p_ret.append(sp); p_h.append(hp); p_conv.append(cp); p_mk.append(mk); p_mv.append(mv)
        s_ret.append(ss); s_h.append(hs); s_conv.append(cs)
    y_prompt = rmsnorm(yp, final_norm)
    y_sample = rmsnorm(ys, final_norm)
    return (y_prompt, y_sample, jnp.stack(p_ret), jnp.stack(p_h), jnp.stack(p_conv), jnp.stack(p_mk),
            jnp.stack(p_mv), jnp.stack(s_ret), jnp.stack(s_h), jnp.stack(s_conv))
```

```python
import contextlib
import math
import numpy as np
import ml_dtypes
import concourse.bass as bass
import concourse.mybir as mybir
from concourse.bass_utils import run_bass_kernel_spmd

F32 = mybir.dt.float32
BF16 = mybir.dt.bfloat16
ALU = mybir.AluOpType
AF = mybir.ActivationFunctionType
AX = mybir.AxisListType

NCORES = 8
D = 1024
NT = 17
TOK = NT * 128
FF = 2816
NFC = FF // 128
EPS = 1e-6
BLOCKS = [(0, 512), (512, 512), (1024, 512), (1536, 512), (2048, 128)]
FF_PARTS = [(0, 8), (8, 16), (16, 22)]
ENGS = ("pe", "act", "dve", "pool", "sp")
AUTO_TRACK = True
PE_FILL = True
STRICT_WAR = False


class Op:
    __slots__ = ("idx", "kind", "eng", "call", "deps", "key", "val", "needed", "seq", "dur", "seg", "pos",
                 "nbytes", "waits", "cp", "tab", "fill", "waits_pre")

    def __init__(self, kind, eng):
        self.kind = kind
        self.eng = eng
        self.fill = 0
        self.waits_pre = []
        self.deps = []
        self.needed = False
        self.seq = None
        self.key = None
        self.val = 0
        self.pos = -1


class Res:
    __slots__ = ("name", "w", "r", "excl")

    def __init__(self, name="", excl=False):
        self.name = name
        self.w = None
        self.r = []
        self.excl = excl


class _Rec:
    def __init__(self):
        self.call = None

    def __getattr__(self, name):
        def f(*a, **kw):
            self.call = (name, a, kw)
            return self
        return f


def _free_elems(ap):
    n = 1
    for d in ap.shape[1:]:
        n *= int(d)
    return n


def _est_dur(eng, call):
    name, a, kw = call
    if name == "matmul":
        rhs = a[2] if len(a) > 2 else kw["rhs"]
        return 0.02 + _free_elems(rhs) / 2300.0
    if name == "transpose":
        return 0.08
    out = kw.get("out", a[0] if a else None)
    n = _free_elems(out) if out is not None else 64
    if eng == "act":
        return 0.12 + n * 0.00100
    if eng == "dve":
        return 0.12 + n * 0.00105
    if name == "tensor_scalar":
        return 0.35 + n * 0.0008
    if name == "tensor_tensor" and kw.get("op") == ALU.pow:
        return 0.5 + n * 0.17
    return 0.2 + n * 0.0021


class Prog:
    def __init__(self, nc):
        self.nc = nc
        self.all = []
        self.dma_cnt = {}
        self.dma_last = {}
        self.seg = 0
        self.pages = {}
        self.filler_ap = None

    def barrier(self):
        if not AUTO_TRACK:
            self.seg += 1

    PAGE = 1024

    def _ap_range(self, ap):
        try:
            if ap.tensor.name != "arena":
                return None
        except Exception:
            return None
        sz = 4 if ap.dtype == F32 else 2
        dims = list(ap.ap)
        pstride = dims[0][0]
        lo = ap.offset % pstride if pstride else ap.offset
        ext = 1
        for st_, cnt in dims[1:]:
            ext += abs(st_) * (cnt - 1)
        return (lo * sz, (lo + ext) * sz)

    def _track(self, op, call):
        if not AUTO_TRACK:
            return
        if call[0] == "dma":
            accs = [(call[1], True), (call[2], False)]
        else:
            name, a, kw = call
            accs = []
            items = list(kw.items()) + [("arg%d" % i, v) for i, v in enumerate(a)]
            for k, v in items:
                if not hasattr(v, "ap") or not hasattr(v, "tensor"):
                    continue
                is_out = k in ("out", "accum_out", "arg0")
                accs.append((v, is_out))
        seen = set(id(d) for d, _ in op.deps)
        pages = self.pages
        for ap, is_w in accs:
            r = self._ap_range(ap)
            if r is None:
                continue
            lo, hi = r
            for pg in range(lo // self.PAGE, (hi - 1) // self.PAGE + 1):
                lst = pages.setdefault(pg, [])
                keep = []
                for rec in lst:
                    rlo, rhi, rop, rw = rec
                    if rlo < hi and lo < rhi and rop is not op:
                        if is_w or rw:
                            if id(rop) not in seen:
                                seen.add(id(rop))
                                op.deps.append((rop, not rw))
                        if is_w and lo <= max(rlo, pg * self.PAGE) and min(rhi, (pg + 1) * self.PAGE) <= hi:
                            continue
                    keep.append(rec)
                keep.append((lo, hi, op, is_w))
                pages[pg] = keep

    def _edges(self, op, reads, writes, deps):
        for d in deps:
            if d is not None:
                op.deps.append((d, False))
        for R in reads:
            if R.w is not None:
                op.deps.append((R.w, False))
            if R.excl:
                for d in R.r:
                    if d.eng != op.eng:
                        op.deps.append((d, True))
        for R in writes:
            if R.w is not None:
                op.deps.append((R.w, False))
            for d in R.r:
                op.deps.append((d, True))
        for R in reads:
            R.r.append(op)
        for R in writes:
            R.w = op
            R.r = []

    def op(self, eng, fn, reads=(), writes=(), deps=()):
        o = Op("eng", eng)
        rec = _Rec()
        fn(rec)
        o.call = rec.call
        o.dur = _est_dur(eng, o.call)
        o.tab = None
        if eng == "act" and o.call[0] == "activation":
            fnc = o.call[2].get("func")
            if fnc in (AF.Exp, AF.Tanh):
                o.tab = "exp"
            elif fnc == AF.Sqrt:
                o.tab = "sqrt"
            elif fnc == AF.Ln:
                o.tab = "ln"
            elif fnc == AF.Silu:
                o.tab = "silu"
        o.seg = self.seg
        o.idx = len(self.all)
        self._edges(o, reads, writes, deps)
        self._track(o, o.call)
        self.all.append(o)
        return o

    def dma(self, eng, out, in_, key, reads=(), writes=(), deps=(), **kw):
        o = Op("dma", eng)
        o.tab = None
        o.call = ("dma", out, in_, kw)
        o.key = key
        self.dma_cnt[key] = self.dma_cnt.get(key, 0) + 16
        o.val = self.dma_cnt[key]
        deps = list(deps)
        if key in self.dma_last:
            deps.append(self.dma_last[key])
        self.dma_last[key] = o
        sz = 4 if out.dtype == F32 else 2
        o.nbytes = _free_elems(out) * int(out.shape[0]) * sz
        o.dur = 1.0 if eng == "pool" else 0.15
        o.seg = self.seg
        o.idx = len(self.all)
        self._edges(o, reads, writes, deps)
        self._track(o, o.call)
        self.all.append(o)
        return o

    def schedule(self):
        LAT = 0.3
        DMA_BW = 150e3
        WINDOW = 512
        order = {e: [] for e in ENGS}
        fin = {}
        nseg = self.seg + 1
        segs = [[] for _ in range(nseg)]
        for o in self.all:
            segs[o.seg].append(o)
        succ_cp = [0.0] * len(self.all)
        for o in reversed(self.all):
            mine = succ_cp[o.idx] + (o.dur if o.kind == "eng" else 2.5)
            o.cp = mine
            for d, _w in o.deps:
                if mine > succ_cp[d.idx]:
                    succ_cp[d.idx] = mine
        tnow = 0.0
        dma_clock = 0.0
        cur_tab = [None]
        for sg in segs:
            rem = {e: [o for o in sg if o.eng == e] for e in ENGS}
            clk = {e: tnow for e in ENGS}
            dma_clock = max(dma_clock, tnow)
            nleft = len(sg)
            cand = {e: None for e in ENGS}
            dirty = set(ENGS)
            while nleft:
                for e in list(dirty):
                    best = None
                    bestkey = None
                    lst = rem[e]
                    for j in range(min(WINDOW, len(lst))):
                        o = lst[j]
                        r = clk[e]
                        ok = True
                        for d, _w in o.deps:
                            f = fin.get(d.idx)
                            if f is None:
                                ok = False
                                break
                            if d.eng != e or d.kind == "dma":
                                f += LAT
                            if f > r:
                                r = f
                        if not ok:
                            continue
                        if r <= clk[e] + 1e-9:
                            key = (0 if (e == "act" and o.tab is not None and o.tab != cur_tab[0]) else 1, o.cp)
                            if best is None or best[0] > clk[e] + 1e-9 or key > bestkey:
                                best = (r, j, o)
                                bestkey = key
                        elif best is None or r < best[0] - 1e-9:
                            best = (r, j, o)
                    cand[e] = best
                dirty.clear()
                be = None
                for e in ENGS:
                    c = cand[e]
                    if c is not None and (be is None or c[0] < cand[be][0]):
                        be = e
                assert be is not None, "scheduler deadlock"
                r, j, o = cand[be]
                rem[be].pop(j)
                if o.kind == "dma":
                    clk[be] = r + o.dur
                    st = max(r, dma_clock)
                    dma_clock = st + o.nbytes / DMA_BW
                    fin[o.idx] = dma_clock + 2.0
                else:
                    if (be == "pe" and PE_FILL and o.call[0] == "matmul" and o.call[2].get("start")
                            and len(order["pe"]) > 64):
                        r_war = clk[be]
                        for d, w_ in o.deps:
                            if w_ and d.eng != "pe":
                                r_war = max(r_war, fin[d.idx] + LAT)
                        gap = r - r_war - 0.4
                        if gap > 0.6 and _free_elems(o.call[1][0]) >= 128:
                            o.fill = min(int(0.7 * gap / 0.07), 64)
                    extra = 0.0
                    if be == "act" and o.tab is not None and o.tab != cur_tab[0]:
                        extra = 1.3
                        cur_tab[0] = o.tab
                    clk[be] = r + o.dur + extra
                    fin[o.idx] = clk[be]
                o.pos = len(order[be])
                order[be].append(o)
                nleft -= 1
                dirty = set(ENGS)
            tnow = max([tnow] + [fin[o.idx] for o in sg])
        self.est_total = tnow
        return order

    def emit(self):
        nc = self.nc
        order = self.schedule()
        last_by_seg = {}
        waits = {}
        for e in ENGS:
            prev_seg = -1
            for o in order[e]:
                extra = []
                if o.seg != prev_seg:
                    for e2 in ENGS:
                        if e2 == e:
                            continue
                        cands = [x for x in order[e2] if x.seg < o.seg and x.kind == "eng"]
                        if cands:
                            extra.append(cands[-1])
                    keys = {}
                    for x in self.all:
                        if x.kind == "dma" and x.seg < o.seg:
                            keys[x.key] = x
                    extra.extend(keys.values())
                    prev_seg = o.seg
                per_eng = {}
                per_eng_war = {}
                dmas = {}
                for d, war in [(x, False) for x in extra] + o.deps:
                    if d.kind == "dma":
                        if d.key not in dmas or dmas[d.key].val < d.val:
                            dmas[d.key] = d
                        continue
                    if d.eng == e:
                        if e == "pe":
                            continue
                        if war and not STRICT_WAR:
                            continue
                    tgt = per_eng_war if (war and o.fill) else per_eng
                    if d.eng not in tgt or tgt[d.eng].pos < d.pos:
                        tgt[d.eng] = d
                o.waits_pre = list(per_eng_war.values())
                o.waits = list(per_eng.values()) + list(dmas.values())
                for d in list(per_eng.values()) + list(per_eng_war.values()):
                    d.needed = True
        for e in ENGS:
            c = 0
            for o in order[e]:
                if o.kind == "eng" and o.needed:
                    c += 1
                    o.seq = c
        with contextlib.ExitStack() as st:
            esem = {e: st.enter_context(nc.semaphore("s_" + e)) for e in ENGS}
            dsem = {k: st.enter_context(nc.semaphore("d_%s" % k)) for k in self.dma_cnt}
            block = st.enter_context(nc.Block())

            def run(e, engobj):
                waited = {}
                for o in order[e]:
                    if o.fill:
                        for d in o.waits_pre:
                            sem, val, k = esem[d.eng], d.seq, ("e", d.eng)
                            if waited.get(k, 0) >= val:
                                continue
                            waited[k] = val
                            engobj.wait_ge(sem, val)
                        fout = o.call[1][0][:, 0:128]
                        for _ in range(o.fill):
                            engobj.matmul(fout, self.filler_ap, self.filler_ap, start=True, stop=True)
                    for d in o.waits:
                        if d.kind == "eng":
                            sem, val, k = esem[d.eng], d.seq, ("e", d.eng)
                        else:
                            sem, val, k = dsem[d.key], d.val, ("d", d.key)
                        if waited.get(k, 0) >= val:
                            continue
                        waited[k] = val
                        engobj.wait_ge(sem, val)
                    fn = o.call
                    if fn[0] == "dma":
                        _, out, in_, kw = fn
                        engobj.dma_start(out=out, in_=in_, **kw).then_inc(dsem[o.key], 16)
                    else:
                        ins = getattr(engobj, fn[0])(*fn[1], **fn[2])
                        if o.needed:
                            ins.then_inc(esem[e], 1)

            @block.tensor
            def _(eng):
                run("pe", eng)

            @block.scalar
            def _(eng):
                run("act", eng)

            @block.vector
            def _(eng):
                run("dve", eng)

            @block.gpsimd
            def _(eng):
                run("pool", eng)

            @block.sync
            def _(eng):
                run("sp", eng)
                for k, v in self.dma_cnt.items():
                    eng.wait_ge(dsem[k], v)


class Arena:
    def __init__(self, t, nwords):
        self.t = t
        self.n = nwords
        self.off = 0

    def alloc(self, free, dtype=F32):
        n = int(np.prod(free))
        sz = 4 if dtype == F32 else 2
        words = (n * sz + 3) // 4
        words = (words + 1) // 2 * 2
        assert self.off + words <= self.n, ("arena overflow", self.off, words, self.n)
        ap = self.t[:, self.off:self.off + words]
        self.off += words
        if dtype != F32:
            ap = ap.bitcast(dtype)
        ap = ap[:, 0:n]
        if len(free) == 2:
            ap = ap.rearrange("p (a b) -> p a b", a=free[0])
        elif len(free) == 3:
            ap = ap.rearrange("p (a b c) -> p a b c", a=free[0], b=free[1])
        elif len(free) == 4:
            ap = ap.rearrange("p (a b c d) -> p a b c d", a=free[0], b=free[1], c=free[2])
        return ap

    def mark(self):
        return self.off

    def release(self, m):
        self.off = m


def build_program(stop_after=None):
    nc = bass.Bass("TRN2", target_bir_lowering=False)

    def din(name, shape, dt=F32):
        return nc.dram_tensor(name, list(shape), dt, kind="ExternalInput").ap()

    def dout(name, shape, dt=F32):
        return nc.dram_tensor(name, list(shape), dt, kind="ExternalOutput").ap()

    x_in = din("x", [TOK, D])
    st_ret = din("st_ret", [16, 4, 128, 128])
    st_h = din("st_h", [128, 4, 16])
    st_conv = din("st_conv", [128, 4, 16, 3])
    ck = din("ck", [16, 256, D])
    cv = din("cv", [16, 256, D])
    memp = din("memp", [256, D])
    gains = {n: din("g_" + n, [128, D]) for n in ("ffn1", "mix", "xattn", "ffn2", "final")}
    wts = {}
    for n in ("ffn1", "ffn2"):
        wts[n + "_wg"] = din(n + "_wg", [D, FF])
        wts[n + "_wu"] = din(n + "_wu", [D, FF])
        wts[n + "_wd"] = din(n + "_wd", [FF, D])
    w_in = din("w_in", [D, 3072])
    w_out = din("w_out", [D, D])
    wq = din("wq", [D, D])
    wk = din("wk", [D, D])
    wv = din("wv", [D, D])
    wo = din("wo", [D, D])
    gn_gain = din("gn_gain", [128, 512])
    lruvec = din("lruvec", [128, 4, 9])
    wabd = din("wabd", [128, 4, 128])
    wxbd = din("wxbd", [128, 4, 128])
    c_ident = din("c_ident", [128, 128], BF16)
    c_ones = din("c_ones", [128, 128])
    c_cos = din("c_cos", [128, TOK])
    c_sin = din("c_sin", [128, TOK])
    c_dmask = din("c_dmask", [2, 128, 4, 128])
    c_qdec = din("c_qdec", [2, 128, 4, 128])
    c_kdec = din("c_kdec", [2, 128, 4])
    c_bmask = din("c_bmask", [128, 16, 128], BF16)
    c_rmask = din("c_rmask", [128, 16])

    y_out = dout("y", [TOK, D])
    o_retp = dout("o_retp", [4, 128, 128])
    o_hp = dout("o_hp", [128, 4])
    o_convp = dout("o_convp", [128, 4, 3])
    o_mk = dout("o_mk", [256, D])
    o_mv = dout("o_mv", [256, D])
    o_rets = dout("o_rets", [16, 4, 128, 128])
    o_hs = dout("o_hs", [128, 4, 16])
    o_convs = dout("o_convs", [128, 4, 16, 3])

    GAMMA = [1.0 - 2.0 ** (-5.0 - h) for h in range(4)]

    NW = 212800 // 4
    with contextlib.ExitStack() as st:
        arena_t = st.enter_context(nc.sbuf_tensor("arena", [128, NW], F32))
        psum = st.enter_context(nc.psum_tensor("psum", [128, 4096], F32))
        p = Prog(nc)
        A = Arena(arena_t, NW)
        dbgn = [0]

        def dbg(name, ap, reads):
            shp = list(ap.shape)
            o = dout("dbg_" + name, shp, ap.dtype)
            dbgn[0] += 1
            (p.dma("sp", o, ap, key="dbg%d" % dbgn[0], reads=reads))

        bank = [psum[:, b * 512:(b + 1) * 512] for b in range(8)]
        bank_bf = [bank[b].bitcast(BF16) for b in range(8)]
        bres = [Res("bank%d" % b, excl=True) for b in range(8)]

        def bank2(b):
            return psum[:, b * 512:(b + 2) * 512]

        xres = A.alloc([NT, D], F32)
        xr = [Res("x%d" % i) for i in range(NT)]
        ident = A.alloc([128], BF16)
        r_ident = Res()
        ssq = A.alloc([NT], F32)
        rstd = A.alloc([NT], F32)
        tmp17 = A.alloc([NT], F32)
        cm05 = A.alloc([4], F32)
        cp05 = A.alloc([4], F32)
        r_cm = Res()
        gbc = A.alloc([D], F32)
        r_gbc = Res()
        xnb = [A.alloc([D], BF16) for _ in range(2)]
        r_xnb = [Res(), Res()]
        junk = A.alloc([D], BF16)
        r_junk = Res()
        r_ssq = [Res() for _ in range(NT)]
        base_mark = A.mark()

        xv_in = x_in.rearrange("(i p) d -> p i d", p=128)
        yv_out = y_out.rearrange("(i p) d -> p i d", p=128)

        p.dma("sp", ident, c_ident, key="c0", writes=[r_ident])
        p.filler_ap = ident
        p.op("dve", lambda e: e.memset(cm05, -0.5), writes=[r_cm])
        p.op("dve", lambda e: e.memset(cp05, 0.5), writes=[r_cm])
        for i in range(NT):
            p.dma("sp", xres[:, i, :], xv_in[:, i, :], key="xin%d" % (i % 4), writes=[xr[i]])

        kcnt = [0]

        def norm_tile(i, dst, dst_res, dst_col0, tpb):
            nb = i % 2
            p.op("act", lambda e: e.activation(out=junk, in_=xres[:, i, :], func=AF.Square,
                                               accum_out=ssq[:, i:i + 1]),
                 reads=[xr[i]], writes=[r_junk, r_ssq[i]])
            p.op("dve", lambda e: e.tensor_scalar(out=tmp17[:, i:i + 1], in0=ssq[:, i:i + 1], scalar1=1.0 / D,
                                                  scalar2=EPS, op0=ALU.mult, op1=ALU.add),
                 reads=[], writes=[r_ssq[i]])
            p.op("pool", lambda e: e.tensor_tensor(out=rstd[:, i:i + 1], in0=tmp17[:, i:i + 1],
                                                   in1=cm05[:, 0:1], op=ALU.pow),
                 reads=[r_cm], writes=[r_ssq[i]])
            p.op("dve", lambda e: e.scalar_tensor_tensor(out=xnb[nb], in0=xres[:, i, :], scalar=rstd[:, i:i + 1],
                                                         in1=gbc, op0=ALU.mult, op1=ALU.mult),
                 reads=[xr[i], r_gbc, r_ssq[i]], writes=[r_xnb[nb]])
            for k in range(8):
                p.op("pe", lambda e, k=k: e.transpose(bank_bf[tpb][:, k * 128:(k + 1) * 128],
                                                      xnb[nb][:, k * 128:(k + 1) * 128], ident),
                     reads=[r_xnb[nb], r_ident], writes=[bres[tpb]])
            p.op("act", lambda e: e.activation(out=dst[:, :, dst_col0:dst_col0 + 128],
                                               in_=bank_bf[tpb].rearrange("p (k n) -> p k n", k=8),
                                               func=AF.Copy),
                 reads=[bres[tpb]], writes=[dst_res])

        def load_gain(name):
            p.dma("sp", gbc, gains[name], key="gbc", writes=[r_gbc])

        def ffn(name):
            wgv = wts[name + "_wg"].rearrange("(k p) n -> p k n", p=128)
            wuv = wts[name + "_wu"].rearrange("(k p) n -> p k n", p=128)
            wdv = wts[name + "_wd"].rearrange("(f p) d -> p f d", p=128)
            m0 = A.mark()
            xnT = A.alloc([8, TOK], BF16)
            r_xnT = [Res() for _ in range(NT)]
            hT = A.alloc([8, TOK], BF16)
            r_hT = {}
            wd_sb = A.alloc([8, D], BF16)
            r_wd = Res()
            ring = [A.alloc([2, 8, 256], BF16) for _ in range(2)]
            r_rg = [Res(), Res()]
            r_ru = [Res(), Res()]
            sgb = [A.alloc([512], F32) for _ in range(2)]
            r_sgb = [Res(), Res()]
            load_gain(name)
            for i in range(NT):
                norm_tile(i, xnT, r_xnT[i], i * 128, 4 + 2 * (i % 2))
            gi = 0
            for (f0, f1) in FF_PARTS:
                nf = f1 - f0
                p.dma("pool", wd_sb[:, 0:nf, :], wdv[:, f0:f1, :], key="wd", writes=[r_wd])
                for g0 in range(f0, f1, 2):
                    sl = gi % 2
                    gi += 1
                    p.dma("pool", ring[sl][:, 0, :, :], wgv[:, :, g0 * 128:(g0 + 2) * 128], key="rg%d" % sl,
                          writes=[r_rg[sl]])
                    p.dma("pool", ring[sl][:, 1, :, :], wuv[:, :, g0 * 128:(g0 + 2) * 128], key="ru%d" % sl,
                          writes=[r_ru[sl]])
                    for fl in range(2):
                        f = g0 + fl
                        for bi, (t0, nb) in enumerate(BLOCKS):
                            par = kcnt[0] % 2
                            kcnt[0] += 1
                            gb, ub = par, 2 + par
                            tiles = range(t0 // 128, (t0 + nb) // 128)
                            for k in range(8):
                                p.op("pe", lambda e, k=k, fl=fl, sl=sl, gb=gb, t0=t0, nb=nb: e.matmul(
                                    bank[gb][:, 0:nb], ring[sl][:, 0, k, fl * 128:(fl + 1) * 128],
                                    xnT[:, k, t0:t0 + nb], start=(k == 0), stop=(k == 7)),
                                    reads=[r_rg[sl]] + [r_xnT[i] for i in tiles], writes=[bres[gb]])
                            for k in range(8):
                                p.op("pe", lambda e, k=k, fl=fl, sl=sl, ub=ub, t0=t0, nb=nb: e.matmul(
                                    bank[ub][:, 0:nb], ring[sl][:, 1, k, fl * 128:(fl + 1) * 128],
                                    xnT[:, k, t0:t0 + nb], start=(k == 0), stop=(k == 7)),
                                    reads=[r_ru[sl]] + [r_xnT[i] for i in tiles], writes=[bres[ub]])
                            p.op("act", lambda e, gb=gb, par=par, nb=nb: e.activation(
                                out=sgb[par][:, 0:nb], in_=bank[gb][:, 0:nb], func=AF.Silu),
                                reads=[bres[gb]], writes=[r_sgb[par]])
                            rh = r_hT.setdefault((f - f0, bi), Res())
                            p.op("dve", lambda e, ub=ub, par=par, nb=nb, t0=t0, f=f, f0=f0: e.tensor_tensor(
                                out=hT[:, f - f0, t0:t0 + nb], in0=sgb[par][:, 0:nb], in1=bank[ub][:, 0:nb],
                                op=ALU.mult),
                                reads=[r_sgb[par], bres[ub]], writes=[rh])
                for i in range(NT):
                    yb = 4 + 2 * (i % 2)
                    bi = min(i // 4, 4)
                    for f in range(nf):
                        for hc in range(2):
                            p.op("pe", lambda e, f=f, hc=hc, yb=yb, i=i: e.matmul(
                                bank[yb + hc], hT[:, f, i * 128:(i + 1) * 128],
                                wd_sb[:, f, hc * 512:(hc + 1) * 512], start=(f == 0), stop=(f == nf - 1)),
                                reads=[r_hT[(f, bi)], r_wd], writes=[bres[yb + hc]])
                    p.op("dve", lambda e, yb=yb, i=i: e.scalar_tensor_tensor(
                        out=xres[:, i, :], in0=bank2(yb), scalar=0.5, in1=xres[:, i, :],
                        op0=ALU.mult, op1=ALU.add),
                        reads=[bres[yb], bres[yb + 1]], writes=[xr[i]])
            A.release(m0)
            p.barrier()
            return None

        def interleave(*gens):
            gens = list(gens)
            while gens:
                for g in list(gens):
                    try:
                        next(g)
                    except StopIteration:
                        gens.remove(g)

        def norm_block(t0, nb, xnTb, r_xnTb):
            for tt in range(nb // 128):
                norm_tile(t0 // 128 + tt, xnTb, r_xnTb, tt * 128, 5)

        def mix_ret(retT, r_retT):
            winv = w_in.rearrange("(k p) n -> p k n", p=128)
            m0 = A.mark()
            wqk = A.alloc([8, 1024], BF16)
            wvg = A.alloc([8, 1024], BF16)
            r_wqk, r_wvg = Res(), Res()
            p.dma("pool", wqk, winv[:, :, 0:1024], key="wA", writes=[r_wqk])
            p.dma("pool", wvg, winv[:, :, 1024:2048], key="wB", writes=[r_wvg])
            dmask = A.alloc([2, 512], F32)
            qdec = A.alloc([2, 512], F32)
            kdec = A.alloc([2, 4], F32)
            gng = A.alloc([512], F32)
            bmask = A.alloc([16, 128], BF16)
            rmask = A.alloc([16], F32)
            r_c = Res()
            for v in range(2):
                p.dma("sp", dmask[:, v, :], c_dmask[v].rearrange("p h i -> p (h i)"), key="c0", writes=[r_c])
                p.dma("sp", qdec[:, v, :], c_qdec[v].rearrange("p h i -> p (h i)"), key="c0", writes=[r_c])
                p.dma("sp", kdec[:, v, :], c_kdec[v], key="c0", writes=[r_c])
            p.dma("sp", gng, gn_gain, key="c0", writes=[r_c])
            p.dma("sp", bmask, c_bmask, key="c0", writes=[r_c])
            p.dma("sp", rmask, c_rmask, key="c0", writes=[r_c])
            load_gain("mix")
            xnTb = A.alloc([8, 256], BF16)
            r_xnTb = Res()
            cosb = A.alloc([256], F32)
            sinb = A.alloc([256], F32)
            r_cs = Res()
            rt1s = [A.alloc([256], F32) for _ in range(2)]
            rt2s = [A.alloc([256], F32) for _ in range(2)]
            r_rt1s, r_rt2s = [Res(), Res()], [Res(), Res()]
            rt1, rt2, r_rt1, r_rt2 = rt1s[0], rt2s[0], r_rt1s[0], r_rt2s[0]
            qT = A.alloc([4, 256], BF16)
            kT = A.alloc([4, 256], BF16)
            qdT = A.alloc([4, 256], BF16)
            kdn = A.alloc([2, 4, 128], BF16)
            vsb = A.alloc([2, 512], BF16)
            sg = A.alloc([2, 512], F32)
            r_qT, r_kT, r_qdT = Res(), Res(), Res()
            r_kdn = [Res() for _ in range(4)]
            r_v = [Res() for _ in range(4)]
            r_sg = [Res() for _ in range(4)]
            tht = A.alloc([512], F32)
            r_tht = Res()
            state = A.alloc([4, 128], F32)
            state_bf = A.alloc([4, 128], BF16)
            r_state, r_statebf = Res(), Res()
            sms = [A.alloc([512], BF16) for _ in range(2)]
            r_sms = [Res(), Res()]
            ons = [A.alloc([512], F32) for _ in range(2)]
            r_ons = [Res(), Res()]
            ros = [A.alloc([512], BF16) for _ in range(2)]
            r_ros = [Res(), Res()]
            statss = [A.alloc([8, 4], F32) for _ in range(2)]
            r_statss = [Res(), Res()]
            sm, r_sm, on, r_on, ro, r_ro, stats, r_stats = sms[0], r_sms[0], ons[0], r_ons[0], ros[0], r_ros[0], statss[0], r_statss[0]
            p.op("dve", lambda e: e.memset(state, 0.0), writes=[r_state])
            p.op("dve", lambda e: e.memset(state_bf, 0.0), writes=[r_statebf])

            def ret_stream(blocks, Bf):
                (xnTb, r_xnTb, cosb, sinb, r_cs, rt1s, rt2s, r_rt1s, r_rt2s, qT, kT, qdT, r_qT, r_kT, r_qdT,
                 kdn, r_kdn, vsb, r_v, sg, r_sg) = (Bf[k] for k in (
                    "xnTb", "r_xnTb", "cosb", "sinb", "r_cs", "rt1s", "rt2s", "r_rt1s", "r_rt2s", "qT", "kT", "qdT",
                    "r_qT", "r_kT", "r_qdT", "kdn", "r_kdn", "vsb", "r_v", "sg", "r_sg"))
                rt1, rt2, r_rt1, r_rt2 = rt1s[0], rt2s[0], r_rt1s[0], r_rt2s[0]
                for (t0, nb) in blocks:
                    is_s = (t0 == 2048)
                    v = 1 if is_s else 0
                    nt = nb // 128
                    norm_block(t0, nb, xnTb, r_xnTb)
                    p.dma("sp", cosb[:, 0:nb], c_cos[:, t0:t0 + nb], key=("cs%d" % (1 if t0 == 2048 else 0)), writes=[r_cs])
                    p.dma("sp", sinb[:, 0:nb], c_sin[:, t0:t0 + nb], key=("cs%d" % (1 if t0 == 2048 else 0)), writes=[r_cs])
                    for c in range(8):
                        pb = c % 2
                        rt1, rt2, r_rt1, r_rt2 = rt1s[pb], rt2s[pb], r_rt1s[pb], r_rt2s[pb]
                        for k in range(8):
                            p.op("pe", lambda e, k=k, c=c, pb=pb: e.matmul(
                                bank[pb][:, 0:nb], wqk[:, k, c * 128:(c + 1) * 128], xnTb[:, k, 0:nb],
                                start=(k == 0), stop=(k == 7)),
                                reads=[r_wqk, r_xnTb], writes=[bres[pb]])
                        dst = qT if c < 4 else kT
                        r_dst = r_qT if c < 4 else r_kT
                        h = c % 4
                        p.op("dve", lambda e, pb=pb: e.tensor_tensor(out=rt1[:, 0:nb], in0=bank[pb][:, 0:nb],
                                                                      in1=cosb[:, 0:nb], op=ALU.mult),
                             reads=[bres[pb], r_cs], writes=[r_rt1])
                        p.op("dve", lambda e, pb=pb: e.tensor_tensor(out=rt2[0:64, 0:nb], in0=bank[pb][64:128, 0:nb],
                                                                      in1=sinb[0:64, 0:nb], op=ALU.mult),
                             reads=[bres[pb], r_cs], writes=[r_rt2])
                        p.op("dve", lambda e, pb=pb: e.tensor_tensor(out=rt2[64:128, 0:nb], in0=bank[pb][0:64, 0:nb],
                                                                      in1=sinb[64:128, 0:nb], op=ALU.mult),
                             reads=[bres[pb], r_cs], writes=[r_rt2])
                        p.op("pool", lambda e, dst=dst, h=h: e.tensor_tensor(out=dst[:, h, 0:nb], in0=rt1[:, 0:nb],
                                                                             in1=rt2[:, 0:nb], op=ALU.add),
                             reads=[r_rt1, r_rt2], writes=[r_dst])
                    for h in range(4):
                        p.op("pool", lambda e, h=h: e.tensor_tensor(
                            out=qdT[:, h, 0:nb].rearrange("p (t i) -> p t i", i=128),
                            in0=qT[:, h, 0:nb].rearrange("p (t i) -> p t i", i=128),
                            in1=qdec[:, v, h * 128:(h + 1) * 128].unsqueeze(1).to_broadcast([128, nt, 128]),
                            op=ALU.mult),
                            reads=[r_qT, r_c], writes=[r_qdT])
                    for tt in range(nt):
                        for h in range(4):
                            p.op("pe", lambda e, h=h, tt=tt: e.transpose(
                                bank_bf[5][:, h * 128:(h + 1) * 128], kT[:, h, tt * 128:(tt + 1) * 128], ident),
                                reads=[r_kT, r_ident], writes=[bres[5]])
                        p.op("dve", lambda e, tt=tt: e.tensor_tensor(
                            out=kdn[:, tt, :, :],
                            in0=bank_bf[5][:, 0:512].rearrange("p (h d) -> p h d", h=4),
                            in1=kdec[:, v, :].unsqueeze(2).to_broadcast([128, 4, 128]), op=ALU.mult),
                            reads=[bres[5], r_c], writes=[r_kdn[tt]])
                        for k in range(8):
                            p.op("pe", lambda e, k=k, tt=tt: e.matmul(
                                bank[2], xnTb[:, k, tt * 128:(tt + 1) * 128], wvg[:, k, 0:512],
                                start=(k == 0), stop=(k == 7)),
                                reads=[r_wvg, r_xnTb], writes=[bres[2]])
                        p.op("act", lambda e, tt=tt: e.activation(out=vsb[:, tt, :], in_=bank[2], func=AF.Copy),
                             reads=[bres[2]], writes=[r_v[tt]])
                        for k in range(8):
                            p.op("pe", lambda e, k=k, tt=tt: e.matmul(
                                bank[3], xnTb[:, k, tt * 128:(tt + 1) * 128], wvg[:, k, 512:1024],
                                start=(k == 0), stop=(k == 7)),
                                reads=[r_wvg, r_xnTb], writes=[bres[3]])
                        p.op("act", lambda e: e.activation(out=tht, in_=bank[3], func=AF.Tanh, scale=0.5),
                             reads=[bres[3]], writes=[r_tht])
                        p.op("dve", lambda e, tt=tt: e.scalar_tensor_tensor(
                            out=sg[:, tt, :], in0=tht, scalar=1.0, in1=bank[3], op0=ALU.add, op1=ALU.mult),
                            reads=[r_tht, bres[3]], writes=[r_sg[tt]])
                    if is_s:
                        m1 = A.mark()
                        st0 = [A.alloc([8, 128], F32)] * 2
                        st0b = [A.alloc([8, 128], BF16)] * 2
                        r_st0 = [Res()] * 2
                        r_st0b = [Res()] * 2
                        qdm = [A.alloc([4, 2, 128], BF16)] * 2
                        r_qdm = [Res()] * 2
                        kdm = [A.alloc([4, 128], BF16) for _ in range(2)]
                        r_kdm = [Res(), Res()]
                        nst = [A.alloc([4, 128], F32) for _ in range(2)]
                        r_nst = [Res(), Res()]
                        oacc = A.alloc([512], F32)
                        r_oacc = Res()
                    yield
                    for tt in range(nt):
                        tl = slice(tt * 128, (tt + 1) * 128)
                        gi = t0 // 128 + tt
                        pq = gi % 2
                        sm, r_sm, on, r_on, ro, r_ro, stats, r_stats = (sms[pq], r_sms[pq], ons[pq], r_ons[pq], ros[pq],
                                                                        r_ros[pq], statss[pq], r_statss[pq])
                        ob = 6 + pq
                        if is_s:
                            o_h = [oacc[:, h * 128:(h + 1) * 128] for h in range(4)]
                            o_all = oacc.rearrange("p (h d) -> p h d", h=4)
                            o_res = [r_oacc] * 4
                        else:
                            o_h = [bank[ob][:, h * 128:(h + 1) * 128] for h in range(4)]
                            o_all = bank[ob].rearrange("p (h d) -> p h d", h=4)
                            o_res = [bres[ob]] * 4
                        for h in range(4):
                            p.op("pe", lambda e, h=h, tl=tl: e.matmul(
                                bank[4][:, h * 128:(h + 1) * 128], kT[:, h, tl], qT[:, h, tl], start=True, stop=True),
                                reads=[r_kT, r_qT], writes=[bres[4]])
                        p.op("dve", lambda e: e.tensor_tensor(out=sm, in0=bank[4], in1=dmask[:, v, :], op=ALU.mult),
                             reads=[bres[4], r_c], writes=[r_sm])
                        if not is_s:
                            for h in range(4):
                                hs_ = slice(h * 128, (h + 1) * 128)
                                p.op("pe", lambda e, hs_=hs_, tt=tt: e.matmul(
                                    bank[ob][:, hs_], sm[:, hs_], vsb[:, tt, hs_], start=True, stop=False),
                                    reads=[r_sm, r_v[tt]], writes=[bres[ob]])
                                p.op("pe", lambda e, hs_=hs_, h=h, tl=tl: e.matmul(
                                    bank[ob][:, hs_], qdT[:, h, tl], state_bf[:, h, :], start=False, stop=True),
                                    reads=[r_qdT, r_statebf], writes=[bres[ob]])
                            for h in range(4):
                                hs_ = slice(h * 128, (h + 1) * 128)
                                p.op("pe", lambda e, hs_=hs_, h=h, tt=tt: e.matmul(
                                    bank[4][:, hs_], kdn[:, tt, h, :], vsb[:, tt, hs_], start=True, stop=True),
                                    reads=[r_kdn[tt], r_v[tt]], writes=[bres[4]])
                            for h in range(4):
                                hs_ = slice(h * 128, (h + 1) * 128)
                                p.op("dve", lambda e, hs_=hs_, h=h: e.scalar_tensor_tensor(
                                    out=state[:, h, :], in0=state[:, h, :], scalar=float(GAMMA[h] ** 128),
                                    in1=bank[4][:, hs_], op0=ALU.mult, op1=ALU.add),
                                    reads=[bres[4]], writes=[r_state])
                            p.op("act", lambda e: e.activation(out=state_bf, in_=state, func=AF.Copy),
                                 reads=[r_state], writes=[r_statebf])
                            if gi == 15:
                                (p.dma("sp", o_retp.rearrange("h d e -> d h e"), state, key="osm",
                                                     reads=[r_state]))
                        else:
                            for h in range(4):
                                hs_ = slice(h * 128, (h + 1) * 128)
                                p.op("pe", lambda e, hs_=hs_, h=h: e.matmul(
                                    bank[ob][:, hs_], sm[:, hs_], vsb[:, 0, hs_], start=True, stop=True),
                                    reads=[r_sm, r_v[0]], writes=[bres[ob]])
                            p.op("dve", lambda e: e.tensor_copy(out=oacc, in_=bank[ob]),
                                 reads=[bres[ob]], writes=[r_oacc])
                            for g in range(8):
                                yield
                                b2 = g % 2
                                p.dma("sp", st0[b2], st_ret[2 * g:2 * g + 2].rearrange("s h d e -> d (s h) e"),
                                      key="st0", writes=[r_st0[b2]])
                                p.op("act", lambda e, b2=b2: e.activation(out=st0b[b2], in_=st0[b2], func=AF.Copy),
                                     reads=[r_st0[b2]], writes=[r_st0b[b2]])
                                for h in range(4):
                                    p.op("dve", lambda e, h=h, b2=b2, g=g: e.tensor_tensor(
                                        out=qdm[b2][:, h, :, :],
                                        in0=qdT[:, h, 0:128].unsqueeze(1).to_broadcast([128, 2, 128]),
                                        in1=bmask[:, 2 * g:2 * g + 2, :], op=ALU.mult),
                                        reads=[r_qdT, r_c], writes=[r_qdm[b2]])
                                for h in range(4):
                                    hs_ = slice(h * 128, (h + 1) * 128)
                                    for sl in range(2):
                                        p.op("pe", lambda e, hs_=hs_, h=h, sl=sl, b2=b2: e.matmul(
                                            bank[ob][:, hs_], qdm[b2][:, h, sl, :], st0b[b2][:, sl * 4 + h, :],
                                            start=(sl == 0), stop=(sl == 1)),
                                            reads=[r_qdm[b2], r_st0b[b2]], writes=[bres[ob]])
                                p.op("dve", lambda e: e.tensor_tensor(out=oacc, in0=bank[ob], in1=oacc, op=ALU.add),
                                     reads=[bres[ob]], writes=[r_oacc])
                                for sl in range(2):
                                    s_ = 2 * g + sl
                                    k2 = s_ % 2
                                    p.op("dve", lambda e, k2=k2, s_=s_: e.tensor_scalar(
                                        out=kdm[k2], in0=kdn[:, 0, :, :], scalar1=rmask[:, s_:s_ + 1], scalar2=None,
                                        op0=ALU.mult),
                                        reads=[r_kdn[0], r_c], writes=[r_kdm[k2]])
                                    for h in range(4):
                                        hs_ = slice(h * 128, (h + 1) * 128)
                                        p.op("pe", lambda e, hs_=hs_, h=h, k2=k2: e.matmul(
                                            bank[7][:, hs_], kdm[k2][:, h, :], vsb[:, 0, hs_], start=True, stop=True),
                                            reads=[r_kdm[k2], r_v[0]], writes=[bres[7]])
                                    for h in range(4):
                                        hs_ = slice(h * 128, (h + 1) * 128)
                                        p.op("dve", lambda e, hs_=hs_, h=h, k2=k2, sl=sl, b2=b2: e.scalar_tensor_tensor(
                                            out=nst[k2][:, h, :], in0=st0[b2][:, sl * 4 + h, :],
                                            scalar=float(GAMMA[h] ** 8), in1=bank[7][:, hs_],
                                            op0=ALU.mult, op1=ALU.add),
                                            reads=[bres[7], r_st0[b2]], writes=[r_nst[k2]])
                                    (p.dma("sp", o_rets[s_].rearrange("h d e -> d h e"), nst[k2],
                                                         key="ons%d" % k2, reads=[r_nst[k2]]))
                        p.op("dve", lambda e: e.tensor_reduce(
                            out=stats[:, 0, :], in_=o_all, axis=AX.X, op=ALU.add),
                            reads=list(set(o_res)), writes=[r_stats])
                        for h in range(4):
                            p.op("act", lambda e, h=h: e.activation(
                                out=junk[:, 0:128], in_=o_h[h], func=AF.Square,
                                accum_out=stats[:, 1, h:h + 1]),
                                reads=[o_res[h]], writes=[r_junk, r_stats])
                        p.op("dve", lambda e: e.tensor_scalar(out=stats[:, 2, :], in0=stats[:, 0, :], scalar1=1.0 / 128,
                                                              scalar2=None, op0=ALU.mult), writes=[r_stats])
                        p.op("dve", lambda e: e.tensor_tensor(out=stats[:, 3, :], in0=stats[:, 2, :], in1=stats[:, 2, :],
                                                              op=ALU.mult), writes=[r_stats])
                        p.op("dve", lambda e: e.scalar_tensor_tensor(out=stats[:, 4, :], in0=stats[:, 1, :],
                                                                     scalar=1.0 / 128, in1=stats[:, 3, :],
                                                                     op0=ALU.mult, op1=ALU.subtract), writes=[r_stats])
                        p.op("dve", lambda e: e.tensor_scalar(out=stats[:, 4, :], in0=stats[:, 4, :], scalar1=0.0,
                                                              scalar2=EPS, op0=ALU.max, op1=ALU.add), writes=[r_stats])
                        p.op("pool", lambda e: e.tensor_tensor(out=stats[:, 5, :], in0=stats[:, 4, :], in1=cm05[:, 0:4],
                                                               op=ALU.pow), reads=[r_cm], writes=[r_stats])
                        for h in range(4):
                            hs_ = slice(h * 128, (h + 1) * 128)
                            p.op("dve", lambda e, hs_=hs_, h=h: e.tensor_scalar(
                                out=on[:, hs_], in0=o_h[h], scalar1=stats[:, 2, h:h + 1],
                                scalar2=stats[:, 5, h:h + 1], op0=ALU.subtract, op1=ALU.mult),
                                reads=[o_res[h], r_stats], writes=[r_on])
                        p.op("pool", lambda e: e.tensor_tensor(out=on, in0=on, in1=gng, op=ALU.mult),
                             reads=[r_c], writes=[r_on])
                        p.op("dve", lambda e, tt=tt: e.scalar_tensor_tensor(
                            out=ro, in0=on, scalar=0.5, in1=sg[:, tt, :], op0=ALU.mult, op1=ALU.mult),
                            reads=[r_on, r_sg[tt]], writes=[r_ro])
                        for h in range(4):
                            hs_ = slice(h * 128, (h + 1) * 128)
                            p.op("pe", lambda e, hs_=hs_: e.transpose(bank_bf[5][:, hs_], ro[:, hs_], ident),
                                 reads=[r_ro, r_ident], writes=[bres[5]])
                        p.op("act", lambda e, gi=gi: e.activation(
                            out=retT[:, :, gi * 128:(gi + 1) * 128],
                            in_=bank_bf[5][:, 0:512].rearrange("p (h d) -> p h d", h=4), func=AF.Copy),
                            reads=[bres[5]], writes=[r_retT[gi]])
                        yield
                    if is_s:
                        A.release(m1)
            Bf_p = dict(xnTb=xnTb, r_xnTb=r_xnTb, cosb=cosb, sinb=sinb, r_cs=r_cs, rt1s=rt1s, rt2s=rt2s,
                        r_rt1s=r_rt1s, r_rt2s=r_rt2s, qT=qT, kT=kT, qdT=qdT, r_qT=r_qT, r_kT=r_kT, r_qdT=r_qdT,
                        kdn=kdn, r_kdn=r_kdn, vsb=vsb, r_v=r_v, sg=sg, r_sg=r_sg)
            Bf_s = dict(xnTb=A.alloc([8, 128], BF16), r_xnTb=Res(), cosb=A.alloc([128], F32),
                        sinb=A.alloc([128], F32), r_cs=Res(),
                        rt1s=[A.alloc([128], F32)] * 2, rt2s=[A.alloc([128], F32)] * 2,
                        r_rt1s=[Res()] * 2, r_rt2s=[Res()] * 2,
                        qT=A.alloc([4, 128], BF16), kT=A.alloc([4, 128], BF16), qdT=A.alloc([4, 128], BF16),
                        r_qT=Res(), r_kT=Res(), r_qdT=Res(),
                        kdn=A.alloc([1, 4, 128], BF16), r_kdn=[Res()], vsb=A.alloc([1, 512], BF16), r_v=[Res()],
                        sg=A.alloc([1, 512], F32), r_sg=[Res()])
            interleave(ret_stream([(j * 256, 256) for j in range(8)], Bf_p), ret_stream([(2048, 128)], Bf_s))
            A.release(m0)
            p.barrier()

        def mix_lru(retT, r_retT):
            winv = w_in.rearrange("(k p) n -> p k n", p=128)
            woutv = w_out.rearrange("(k p) n -> p k n", p=128)
            m0 = A.mark()
            wug = A.alloc([8, 1024], BF16)
            wo_sb = A.alloc([8, 1024], BF16)
            r_wug, r_wo = Res(), Res()
            p.dma("pool", wug, winv[:, :, 2048:3072], key="wA", writes=[r_wug])
            p.dma("pool", wo_sb, woutv, key="wB", writes=[r_wo])
            wa_sb = A.alloc([4, 128], BF16)
            wx_sb = A.alloc([4, 128], BF16)
            r_wax = Res()
            p.dma("pool", wa_sb, wabd, key="wC", writes=[r_wax])
            p.dma("pool", wx_sb, wxbd, key="wC", writes=[r_wax])
            lv = A.alloc([4, 9], F32)
            ones = A.alloc([128], F32)
            h0T = A.alloc([4, 16], F32)
            r_c = Res()
            p.dma("sp", lv, lruvec, key="c0", writes=[r_c])
            p.dma("sp", ones, c_ones, key="c0", writes=[r_c])
            p.dma("sp", h0T, st_h, key="c0", writes=[r_c])
            load_gain("mix")
            cst = A.alloc([4, 8], F32)
            r_cst = Res()
            p.op("act", lambda e: e.activation(out=cst[:, :, 0], in_=lv[:, :, 7], func=AF.Exp, scale=-1.0),
                 reads=[r_c], writes=[r_cst])
            p.op("act", lambda e: e.activation(out=cst[:, :, 1], in_=cst[:, :, 0], func=AF.Ln, bias=1.0),
                 writes=[r_cst])
            p.op("dve", lambda e: e.tensor_scalar(out=cst[:, :, 2], in0=cst[:, :, 1], scalar1=-8.0, scalar2=None,
                                                  op0=ALU.mult), writes=[r_cst])
            p.op("dve", lambda e: e.tensor_scalar(out=cst[:, :, 3], in0=cst[:, :, 1], scalar1=-4.0, scalar2=None,
                                                  op0=ALU.mult), writes=[r_cst])
            p.op("dve", lambda e: e.tensor_scalar(out=cst[:, :, 4], in0=lv[:, :, 5], scalar1=0.5, scalar2=None,
                                                  op0=ALU.mult), reads=[r_c], writes=[r_cst])
            p.op("dve", lambda e: e.tensor_scalar(out=cst[:, :, 5], in0=lv[:, :, 6], scalar1=0.5, scalar2=None,
                                                  op0=ALU.mult), reads=[r_c], writes=[r_cst])
            xnTb = A.alloc([8, 512], BF16)
            r_xnTb = Res()
            ubuf = A.alloc([4, 3 + 512], F32)
            r_ub = [Res() for _ in range(4)]
            hprev = A.alloc([4], F32)
            r_hp = [Res() for _ in range(4)]
            p.op("dve", lambda e: e.memset(hprev, 0.0), writes=r_hp)
            p.op("dve", lambda e: e.memset(ubuf[:, :, 0:3], 0.0), writes=r_ub)
            hs = A.alloc([4, 512], F32)
            r_hs = [Res() for _ in range(4)]
            gg = A.alloc([4, 512], BF16)
            r_gg = [Res() for _ in range(4)]
            TS = []
            for _ in range(2):
                d = {}
                for nm in ("ta", "tb", "xc", "tr", "ti", "av", "a2", "sv", "gx"):
                    d[nm] = A.alloc([512], F32)
                    d["r_" + nm] = Res()
                d["xcb"] = A.alloc([512], BF16)
                d["r_xcb"] = Res()
                TS.append(d)
            sqb = [A.alloc([512], F32) for _ in range(2)]
            rl = A.alloc([512], F32)
            mixL = A.alloc([4, 512], BF16)
            t16 = A.alloc([16], F32)
            hsl = A.alloc([4, 16], F32)
            r_hsl = Res()
            r_rl, r_lo, r_t16 = Res(), Res(), Res()
            r_sqb = [Res(), Res()]
            r_mixL = Res()
            GC = 0.7978845608028654

            for (t0, nb) in BLOCKS:
                is_s = (t0 == 2048)
                nt = nb // 128
                last_p = (t0 == 1536)
                norm_block(t0, nb, xnTb, r_xnTb)
                if is_s:
                    us = ubuf[:, :, 0:176].rearrange("p c (s j) -> p c s j", j=11)
                    for c in range(4):
                        p.dma("sp", us[:, c, :, 0:3], st_conv[:, c, :, :], key="c0", writes=[r_ub[c]])
                for c in range(4):
                    T_ = TS[c % 2]
                    ta, tb, xc, xcb, tr, ti, av, a2, sv, gx = (T_[k] for k in
                                                               ("ta", "tb", "xc", "xcb", "tr", "ti", "av", "a2", "sv", "gx"))
                    r_ta, r_tb, r_xc, r_xcb, r_tr, r_ti, r_av, r_a2, r_sv, r_gx = (
                        T_["r_" + k] for k in ("ta", "tb", "xc", "xcb", "tr", "ti", "av", "a2", "sv", "gx"))
                    for k in range(8):
                        p.op("pe", lambda e, k=k: e.matmul(
                            bank[0][:, 0:nb], wug[:, k, c * 128:(c + 1) * 128], xnTb[:, k, 0:nb],
                            start=(k == 0), stop=(k == 7)), reads=[r_wug, r_xnTb], writes=[bres[0]])
                    if not is_s:
                        p.op("act", lambda e: e.activation(out=ubuf[:, c, 3:3 + nb], in_=bank[0][:, 0:nb],
                                                           func=AF.Copy), reads=[bres[0]], writes=[r_ub[c]])
                    else:
                        p.op("act", lambda e: e.activation(
                            out=us[:, c, :, 3:11], in_=bank[0][:, 0:128].rearrange("p (s t) -> p s t", t=8),
                            func=AF.Copy), reads=[bres[0]], writes=[r_ub[c]])
                    for k in range(8):
                        p.op("pe", lambda e, k=k: e.matmul(
                            bank[1][:, 0:nb], wug[:, k, 512 + c * 128:512 + (c + 1) * 128], xnTb[:, k, 0:nb],
                            start=(k == 0), stop=(k == 7)), reads=[r_wug, r_xnTb], writes=[bres[1]])
                    p.op("act", lambda e: e.activation(out=ta[:, 0:nb], in_=bank[1][:, 0:nb], func=AF.Square),
                         reads=[bres[1]], writes=[r_ta])
                    p.op("pool", lambda e: e.tensor_scalar(out=ta[:, 0:nb], in0=ta[:, 0:nb], scalar1=0.044715,
                                                           scalar2=1.0, op0=ALU.mult, op1=ALU.add), writes=[r_ta])
                    p.op("dve", lambda e: e.tensor_tensor(out=ta[:, 0:nb], in0=ta[:, 0:nb], in1=bank[1][:, 0:nb],
                                                          op=ALU.mult), reads=[bres[1]], writes=[r_ta])
                    p.op("act", lambda e: e.activation(out=tb[:, 0:nb], in_=ta[:, 0:nb], func=AF.Tanh, scale=GC),
                         reads=[r_ta], writes=[r_tb])
                    p.op("dve", lambda e: e.scalar_tensor_tensor(
                        out=gg[:, c, 0:nb], in0=tb[:, 0:nb], scalar=1.0, in1=bank[1][:, 0:nb],
                        op0=ALU.add, op1=ALU.mult), reads=[r_tb, bres[1]], writes=[r_gg[c]])
                    if not is_s:
                        uin = lambda j: ubuf[:, c, j:j + nb]
                        xcv = xc[:, 0:nb]
                    else:
                        uin = lambda j: us[:, c, :, j:j + 8]
                        xcv = xc[:, 0:128].rearrange("p (s t) -> p s t", t=8)
                    p.op("pool", lambda e: e.tensor_scalar(out=xcv, in0=uin(0), scalar1=lv[:, c, 0:1],
                                                           scalar2=lv[:, c, 4:5], op0=ALU.mult, op1=ALU.add),
                         reads=[r_ub[c], r_c], writes=[r_xc])
                    for j in range(1, 4):
                        p.op("dve", lambda e, j=j: e.scalar_tensor_tensor(
                            out=xcv, in0=uin(j), scalar=lv[:, c, j:j + 1], in1=xcv, op0=ALU.mult, op1=ALU.add),
                            reads=[r_ub[c], r_c], writes=[r_xc])
                    if last_p:
                        (p.dma("sp", o_convp[:, c, :], ubuf[:, c, nb:nb + 3], key="osm",
                                             reads=[r_ub[c]]))
                    if is_s:
                        (p.dma("sp", o_convs[:, c, :, :], us[:, c, :, 8:11], key="osm",
                                             reads=[r_ub[c]]))
                    elif not last_p:
                        p.op("dve", lambda e: e.tensor_copy(out=ubuf[:, c, 0:3], in_=ubuf[:, c, nb:nb + 3]),
                             writes=[r_ub[c]])
                    p.op("act", lambda e: e.activation(out=xcb[:, 0:nb], in_=xc[:, 0:nb], func=AF.Copy),
                         reads=[r_xc], writes=[r_xcb])
                    p.op("pe", lambda e: e.matmul(bank[2][:, 0:nb], wa_sb[:, c, :], xcb[:, 0:nb],
                                                  start=True, stop=True),
                         reads=[r_wax, r_xcb], writes=[bres[2]])
                    p.op("pe", lambda e: e.matmul(bank[3][:, 0:nb], wx_sb[:, c, :], xcb[:, 0:nb],
                                                  start=True, stop=True),
                         reads=[r_wax, r_xcb], writes=[bres[3]])
                    p.op("act", lambda e: e.activation(out=tr[:, 0:nb], in_=bank[2][:, 0:nb], func=AF.Tanh,
                                                       scale=0.5, bias=cst[:, c, 4:5]),
                         reads=[bres[2], r_cst], writes=[r_tr])
                    p.op("act", lambda e: e.activation(out=ti[:, 0:nb], in_=bank[3][:, 0:nb], func=AF.Tanh,
                                                       scale=0.5, bias=cst[:, c, 5:6]),
                         reads=[bres[3], r_cst], writes=[r_ti])
                    p.op("act", lambda e: e.activation(out=av[:, 0:nb], in_=tr[:, 0:nb], func=AF.Exp,
                                                       scale=cst[:, c, 3:4], bias=cst[:, c, 3:4]),
                         reads=[r_tr, r_cst], writes=[r_av])
                    p.op("act", lambda e: e.activation(out=a2[:, 0:nb], in_=tr[:, 0:nb], func=AF.Exp,
                                                       scale=cst[:, c, 2:3], bias=cst[:, c, 2:3]),
                         reads=[r_tr, r_cst], writes=[r_a2])
                    p.op("act", lambda e: e.activation(out=sv[:, 0:nb], in_=a2[:, 0:nb], func=AF.Sqrt,
                                                       scale=-1.0, bias=1.0),
                         reads=[r_a2], writes=[r_sv])
                    p.op("dve", lambda e: e.scalar_tensor_tensor(
                        out=gx[:, 0:nb], in0=ti[:, 0:nb], scalar=1.0, in1=xc[:, 0:nb], op0=ALU.add, op1=ALU.mult),
                        reads=[r_ti, r_xc], writes=[r_gx])
                    p.op("dve", lambda e: e.scalar_tensor_tensor(
                        out=gx[:, 0:nb], in0=gx[:, 0:nb], scalar=0.5, in1=sv[:, 0:nb], op0=ALU.mult, op1=ALU.mult),
                        reads=[r_sv], writes=[r_gx])
                    if not is_s:
                        p.op("dve", lambda e: e.tensor_tensor_scan(
                            out=hs[:, c, 0:nb], data0=av[:, 0:nb], data1=gx[:, 0:nb], initial=hprev[:, c:c + 1],
                            op0=ALU.mult, op1=ALU.add),
                            reads=[r_av, r_gx, r_hp[c]], writes=[r_hs[c]])
                        p.op("dve", lambda e: e.tensor_copy(out=hprev[:, c:c + 1], in_=hs[:, c, nb - 1:nb]),
                             reads=[r_hs[c]], writes=[r_hp[c]])
                        if last_p and c == 3:
                            (p.dma("sp", o_hp, hprev, key="osm", reads=r_hp))
                    else:
                        a3 = av[:, 0:128].rearrange("p (s t) -> p s t", t=8)
                        g3 = gx[:, 0:128].rearrange("p (s t) -> p s t", t=8)
                        p.op("dve", lambda e: e.tensor_tensor(out=t16, in0=a3[:, :, 0], in1=h0T[:, c, :],
                                                              op=ALU.mult), reads=[r_av, r_c], writes=[r_t16])
                        p.op("dve", lambda e: e.tensor_tensor(out=g3[:, :, 0], in0=g3[:, :, 0], in1=t16,
                                                              op=ALU.add), reads=[r_t16], writes=[r_gx])
                        p.op("dve", lambda e: e.memset(a3[:, :, 0], 0.0), writes=[r_av])
                        p.op("dve", lambda e: e.tensor_tensor_scan(
                            out=hs[:, c, 0:128], data0=av[:, 0:128], data1=gx[:, 0:128], initial=0.0,
                            op0=ALU.mult, op1=ALU.add), reads=[r_av, r_gx], writes=[r_hs[c]])
                        p.op("dve", lambda e: e.tensor_copy(
                            out=hsl[:, c, :], in_=hs[:, c, 0:128].rearrange("p (s t) -> p s t", t=8)[:, :, 7]),
                            reads=[r_hs[c]], writes=[r_hsl])
                        if c == 3:
                            (p.dma("sp", o_hs, hsl, key="osm", reads=[r_hsl]))
                    sb_ = sqb[c % 2]
                    p.op("act", lambda e: e.activation(out=sb_[:, 0:nb], in_=hs[:, c, 0:nb], func=AF.Square),
                         reads=[r_hs[c]], writes=[r_sqb[c % 2]])
                    p.op("pe", lambda e: e.matmul(bank[4][:, 0:nb], ones, sb_[:, 0:nb], start=(c == 0),
                                                  stop=(c == 3)),
                         reads=[r_sqb[c % 2], r_c], writes=[bres[4]])
                p.op("act", lambda e: e.activation(out=rl[:, 0:nb], in_=bank[4][:, 0:nb], func=AF.Sqrt,
                                                   scale=1.0 / 512, bias=EPS),
                     reads=[bres[4]], writes=[r_rl])
                p.op("dve", lambda e: e.reciprocal(out=rl[:, 0:nb], in_=rl[:, 0:nb]), writes=[r_rl])
                for c in range(4):
                    lo, r_lo_ = TS[c % 2]["ta"], TS[c % 2]["r_ta"]
                    p.op("dve", lambda e: e.scalar_tensor_tensor(
                        out=lo[:, 0:nb], in0=hs[:, c, 0:nb], scalar=lv[:, c, 8:9], in1=rl[:, 0:nb],
                        op0=ALU.mult, op1=ALU.mult), reads=[r_hs[c], r_rl, r_c], writes=[r_lo_])
                    p.op("dve", lambda e: e.scalar_tensor_tensor(
                        out=mixL[:, c, 0:nb], in0=lo[:, 0:nb], scalar=0.5, in1=gg[:, c, 0:nb],
                        op0=ALU.mult, op1=ALU.mult), reads=[r_lo_, r_gg[c]], writes=[r_mixL])
                for tt in range(nt):
                    gi = t0 // 128 + tt
                    for kk in range(8):
                        for hc in range(2):
                            if kk < 4:
                                lhs = retT[:, kk, gi * 128:(gi + 1) * 128]
                                rr = r_retT[gi]
                            else:
                                lhs = mixL[:, kk - 4, tt * 128:(tt + 1) * 128]
                                rr = r_mixL
                            p.op("pe", lambda e: e.matmul(bank[6 + hc], lhs, wo_sb[:, kk, hc * 512:(hc + 1) * 512],
                                                          start=(kk == 0), stop=(kk == 7)),
                                 reads=[rr, r_wo], writes=[bres[6 + hc]])
                    p.op("dve", lambda e: e.tensor_tensor(out=xres[:, gi, :], in0=bank2(6), in1=xres[:, gi, :],
                                                          op=ALU.add),
                         reads=[bres[6], bres[7]], writes=[xr[gi]])
            A.release(m0)
            p.barrier()

        def transpose_2x1024(src, r_src, dst, r_dst, tb):
            for rnd in range(2):
                for cc in range(4):
                    c = rnd * 4 + cc
                    for t in range(2):
                        p.op("pe", lambda e: e.transpose(
                            bank_bf[tb][:, cc * 256 + t * 128:cc * 256 + (t + 1) * 128],
                            src[:, t, c * 128:(c + 1) * 128], ident),
                            reads=[r_src, r_ident], writes=[bres[tb]])
                p.op("act", lambda e: e.activation(
                    out=dst[:, rnd * 4:(rnd + 1) * 4, :],
                    in_=bank_bf[tb].rearrange("p (c m) -> p c m", c=4), func=AF.Copy),
                    reads=[bres[tb]], writes=[r_dst])

        def xattn_kv(KTp, r_KTp, Vp, r_Vp):
            m0 = A.mark()
            wk_sb = A.alloc([8, 1024], BF16)
            wv_sb = A.alloc([8, 1024], BF16)
            r_wk, r_wv = Res(), Res()
            p.dma("pool", wk_sb, wk.rearrange("(k p) n -> p k n", p=128), key="wA", writes=[r_wk])
            p.dma("pool", wv_sb, wv.rearrange("(k p) n -> p k n", p=128), key="wB", writes=[r_wv])
            memb = A.alloc([2, 1024], BF16)
            r_memb = Res()
            p.dma("pool", memb, memp.rearrange("(t p) d -> p t d", p=128), key="wC", writes=[r_memb])
            memT = A.alloc([8, 256], BF16)
            r_memT = Res()
            kbp = A.alloc([2, 1024], BF16)
            r_kbp = Res()
            kf = [A.alloc([1024], F32) for _ in range(2)]
            r_kf = [Res(), Res()]
            transpose_2x1024(memb, r_memb, memT, r_memT, 5)
            n = 0
            for (wsb, r_w, dstb, r_dstb, oap, tag) in ((wk_sb, r_wk, kbp, r_kbp, o_mk, "k"),
                                                       (wv_sb, r_wv, Vp, r_Vp, o_mv, "v")):
                ov = oap.rearrange("(t p) d -> p t d", p=128)
                for t in range(2):
                    f = kf[n % 2]
                    r_f = r_kf[n % 2]
                    n += 1
                    for hc in range(2):
                        for k in range(8):
                            p.op("pe", lambda e: e.matmul(
                                bank[hc], memT[:, k, t * 128:(t + 1) * 128], wsb[:, k, hc * 512:(hc + 1) * 512],
                                start=(k == 0), stop=(k == 7)), reads=[r_memT, r_w], writes=[bres[hc]])
                    p.op("act", lambda e: e.activation(out=f, in_=bank2(0), func=AF.Copy),
                         reads=[bres[0], bres[1]], writes=[r_f])
                    p.op("dve", lambda e: e.tensor_copy(out=dstb[:, t, :], in_=f),
                         reads=[r_f], writes=[r_dstb])
                    (p.dma("sp", ov[:, t, :], f, key="okv%d" % (n % 2), reads=[r_f]))
            transpose_2x1024(kbp, r_kbp, KTp, r_KTp, 5)
            A.release(m0)
            p.barrier()

        def xattn_main(KTp, r_KTp, Vp, r_Vp):
            m0 = A.mark()
            wq_sb = A.alloc([8, 1024], BF16)
            wo_sb = A.alloc([8, 1024], BF16)
            r_wq, r_wo = Res(), Res()
            p.dma("pool", wq_sb, wq.rearrange("(k p) n -> p k n", p=128), key="wA", writes=[r_wq])
            p.dma("pool", wo_sb, wo.rearrange("(k p) n -> p k n", p=128), key="wB", writes=[r_wo])
            load_gain("xattn")
            bmask = A.alloc([16, 128], BF16)
            r_c = Res()
            p.dma("sp", bmask, c_bmask, key="c0", writes=[r_c])
            SC = 1.0 / 16.0

            def mkset():
                d = dict(Pm=A.alloc([4, 256], BF16), PT=A.alloc([8, 128], BF16), attn=A.alloc([1024], BF16),
                         attnT=A.alloc([8, 128], BF16), sst=A.alloc([4, 4], F32))
                for k in list(d):
                    d["r_" + k] = Res()
                return d

            def softmax(B, sc_all, sc_h, sc_res, tb=5):
                sst, Pm, PT = B["sst"], B["Pm"], B["PT"]
                p.op("dve", lambda e: e.tensor_reduce(out=sst[:, 0, :], in_=sc_all, axis=AX.X, op=ALU.max),
                     reads=list(set(sc_res)), writes=[B["r_sst"]])
                p.op("dve", lambda e: e.tensor_scalar(out=sst[:, 1, :], in0=sst[:, 0, :], scalar1=-SC, scalar2=None,
                                                      op0=ALU.mult), writes=[B["r_sst"]])
                for h in range(4):
                    p.op("act", lambda e: e.activation(out=Pm[:, h, :], in_=sc_h[h], func=AF.Exp, scale=SC,
                                                       bias=sst[:, 1, h:h + 1], accum_out=sst[:, 2, h:h + 1]),
                         reads=[sc_res[h], B["r_sst"]], writes=[B["r_Pm"], B["r_sst"]])
                p.op("dve", lambda e: e.reciprocal(out=sst[:, 3, :], in_=sst[:, 2, :]), writes=[B["r_sst"]])
                for h in range(4):
                    for mc in range(2):
                        j = h * 2 + mc
                        p.op("pe", lambda e: e.transpose(bank_bf[tb][:, j * 128:(j + 1) * 128],
                                                         Pm[:, h, mc * 128:(mc + 1) * 128], ident),
                             reads=[B["r_Pm"], r_ident], writes=[bres[tb]])
                p.op("act", lambda e: e.activation(out=PT, in_=bank_bf[tb].rearrange("p (j n) -> p j n", j=8),
                                                   func=AF.Copy), reads=[bres[tb]], writes=[B["r_PT"]])

            def out_proj(B, gi, pv_all, pv_res, yb, tb=5):
                sst, attn, attnT = B["sst"], B["attn"], B["attnT"]
                p.op("dve", lambda e: e.tensor_tensor(
                    out=attn.rearrange("p (h d) -> p h d", h=4), in0=pv_all,
                    in1=sst[:, 3, :].unsqueeze(2).to_broadcast([128, 4, 256]), op=ALU.mult),
                    reads=list(set(pv_res)) + [B["r_sst"]], writes=[B["r_attn"]])
                for k in range(8):
                    p.op("pe", lambda e: e.transpose(bank_bf[tb][:, k * 128:(k + 1) * 128],
                                                     attn[:, k * 128:(k + 1) * 128], ident),
                         reads=[B["r_attn"], r_ident], writes=[bres[tb]])
                p.op("act", lambda e: e.activation(out=attnT, in_=bank_bf[tb].rearrange("p (j n) -> p j n", j=8),
                                                   func=AF.Copy), reads=[bres[tb]], writes=[B["r_attnT"]])
                for k in range(8):
                    for hc in range(2):
                        p.op("pe", lambda e: e.matmul(bank[yb + hc], attnT[:, k, :],
                                                      wo_sb[:, k, hc * 512:(hc + 1) * 512],
                                                      start=(k == 0), stop=(k == 7)),
                             reads=[B["r_attnT"], r_wo], writes=[bres[yb + hc]])
                p.op("dve", lambda e: e.tensor_tensor(out=xres[:, gi, :], in0=bank2(yb), in1=xres[:, gi, :],
                                                      op=ALU.add),
                     reads=[bres[yb], bres[yb + 1]], writes=[xr[gi]])

            def q_proj(nb, xnT_, r_xnT_, qT_, r_qT_, qb_=5):
                for c in range(8):
                    for k in range(8):
                        p.op("pe", lambda e: e.matmul(bank[qb_][:, 0:nb], wq_sb[:, k, c * 128:(c + 1) * 128],
                                                      xnT_[:, k, 0:nb], start=(k == 0), stop=(k == 7)),
                             reads=[r_wq, r_xnT_], writes=[bres[qb_]])
                    p.op("act", lambda e: e.activation(out=qT_[:, c, 0:nb], in_=bank[qb_][:, 0:nb], func=AF.Copy),
                         reads=[bres[qb_]], writes=[r_qT_])

            def pair(b0):
                v = psum[:, b0 * 512:(b0 + 2) * 512]
                return dict(all=v.rearrange("p (h m) -> p h m", h=4),
                            h=[v[:, h * 256:(h + 1) * 256] for h in range(4)],
                            res=[bres[b0], bres[b0], bres[b0 + 1], bres[b0 + 1]], flat=v)
            pairs = [pair(0), pair(2)]

            xnTb = [A.alloc([8, 512], BF16)] * 2
            r_xnTb = [Res()] * 2
            qTb = [A.alloc([8, 512], BF16) for _ in range(2)]
            r_qTb = [Res(), Res()]
            sets = [mkset(), mkset()]

            def prompt_stream():
                for bi, (t0, nb) in enumerate(BLOCKS[:4]):
                    xb, rxb, qb, rqb = xnTb[bi % 2], r_xnTb[bi % 2], qTb[bi % 2], r_qTb[bi % 2]
                    norm_block(t0, nb, xb, rxb)
                    q_proj(nb, xb, rxb, qb, rqb)
                    yield
                    for tt in range(nb // 128):
                        gi = t0 // 128 + tt
                        B = sets[gi % 2]
                        PP = pairs[gi % 2]
                        tl = slice(tt * 128, (tt + 1) * 128)
                        for h in range(4):
                            for kk in range(2):
                                p.op("pe", lambda e: e.matmul(PP["h"][h], qb[:, 2 * h + kk, tl], KTp[:, 2 * h + kk, :],
                                                              start=(kk == 0), stop=(kk == 1)),
                                     reads=[rqb, r_KTp], writes=[PP["res"][h]])
                        softmax(B, PP["all"], PP["h"], PP["res"])
                        for h in range(4):
                            for mc in range(2):
                                p.op("pe", lambda e: e.matmul(PP["h"][h], B["PT"][:, h * 2 + mc, :],
                                                              Vp[:, mc, h * 256:(h + 1) * 256],
                                                              start=(mc == 0), stop=(mc == 1)),
                                     reads=[B["r_PT"], r_Vp], writes=[PP["res"][h]])
                        out_proj(B, gi, PP["all"], PP["res"], 6)
                        yield

            def sample_stream():
                xs = A.alloc([8, 128], BF16)
                qs = A.alloc([8, 128], BF16)
                r_xs, r_qs = Res(), Res()
                B = mkset()
                msk = [A.alloc([8, 2, 128], BF16) for _ in range(2)]
                r_msk = [Res(), Res()]
                kb = [A.alloc([2, 1024], BF16) for _ in range(2)]
                r_kb = [Res(), Res()]
                KTs = [A.alloc([8, 256], BF16) for _ in range(2)]
                r_KTs = [Res(), Res()]
                acc = A.alloc([1024], F32)
                r_acc = Res()
                norm_tile(16, xs, r_xs, 0, 4)
                q_proj(128, xs, r_xs, qs, r_qs, 4)
                p.op("dve", lambda e: e.memset(acc, 0.0), writes=[r_acc])
                yield
                for s_ in range(16):
                    b2 = s_ % 2
                    g = s_ // 2
                    g2 = g % 2
                    if s_ % 2 == 0:
                        for c in range(8):
                            p.op("pool", lambda e: e.tensor_tensor(
                                out=msk[g2][:, c, :, :],
                                in0=qs[:, c, :].unsqueeze(1).to_broadcast([128, 2, 128]),
                                in1=bmask[:, 2 * g:2 * g + 2, :], op=ALU.mult),
                                reads=[r_qs, r_c], writes=[r_msk[g2]])
                    p.dma("pool", kb[b2], ck[s_].rearrange("(t p) d -> p t d", p=128), key="kv%d" % b2,
                          writes=[r_kb[b2]])
                    transpose_2x1024(kb[b2], r_kb[b2], KTs[b2], r_KTs[b2], 4)
                    for hp in range(2):
                        for hh in range(2):
                            h = hp * 2 + hh
                            for kk in range(2):
                                p.op("pe", lambda e: e.matmul(
                                    bank[4][:, hh * 256:(hh + 1) * 256], msk[g2][:, 2 * h + kk, s_ % 2, :],
                                    KTs[b2][:, 2 * h + kk, :], start=(kk == 0), stop=(kk == 1)),
                                    reads=[r_msk[g2], r_KTs[b2]], writes=[bres[4]])
                        p.op("dve", lambda e: e.tensor_tensor(out=acc[:, hp * 512:(hp + 1) * 512], in0=bank[4],
                                                              in1=acc[:, hp * 512:(hp + 1) * 512], op=ALU.add),
                             reads=[bres[4]], writes=[r_acc])
                    yield
                acc_h = [acc[:, h * 256:(h + 1) * 256] for h in range(4)]
                softmax(B, acc.rearrange("p (h m) -> p h m", h=4), acc_h, [r_acc] * 4, 4)
                acc2 = acc
                r_acc2 = r_acc
                p.op("dve", lambda e: e.memset(acc2, 0.0), writes=[r_acc2])
                yield
                for s_ in range(16):
                    b2 = s_ % 2
                    g = s_ // 2
                    g2 = g % 2
                    if s_ % 2 == 0:
                        for j in range(8):
                            p.op("pool", lambda e: e.tensor_tensor(
                                out=msk[g2][:, j, :, :],
                                in0=B["PT"][:, j, :].unsqueeze(1).to_broadcast([128, 2, 128]),
                                in1=bmask[:, 2 * g:2 * g + 2, :], op=ALU.mult),
                                reads=[B["r_PT"], r_c], writes=[r_msk[g2]])
                    p.dma("pool", kb[b2], cv[s_].rearrange("(t p) d -> p t d", p=128), key="kv%d" % b2,
                          writes=[r_kb[b2]])
                    for hp in range(2):
                        for hh in range(2):
                            h = hp * 2 + hh
                            for mc in range(2):
                                p.op("pe", lambda e: e.matmul(
                                    bank[4][:, hh * 256:(hh + 1) * 256], msk[g2][:, h * 2 + mc, s_ % 2, :],
                                    kb[b2][:, mc, h * 256:(h + 1) * 256], start=(mc == 0), stop=(mc == 1)),
                                    reads=[r_msk[g2], r_kb[b2]], writes=[bres[4]])
                        p.op("dve", lambda e: e.tensor_tensor(out=acc2[:, hp * 512:(hp + 1) * 512], in0=bank[4],
                                                              in1=acc2[:, hp * 512:(hp + 1) * 512], op=ALU.add),
                             reads=[bres[4]], writes=[r_acc2])
                    yield
                out_proj(B, 16, acc2.rearrange("p (h m) -> p h m", h=4), [r_acc2] * 4, 6, 4)
                yield

            interleave(prompt_stream(), sample_stream())
            A.release(m0)
            p.barrier()

        def dump_ret(retT, r_retT):
            pass

        def final_out():
            load_gain("final")
            m0 = A.mark()
            ybuf = [A.alloc([D], F32) for _ in range(4)]
            r_yb = [Res() for _ in range(4)]
            for i in range(NT):
                nb = i % 4
                p.op("act", lambda e, i=i: e.activation(out=junk, in_=xres[:, i, :], func=AF.Square,
                                                        accum_out=ssq[:, i:i + 1]),
                     reads=[xr[i]], writes=[r_junk, r_ssq[i]])
                p.op("dve", lambda e, i=i: e.tensor_scalar(out=tmp17[:, i:i + 1], in0=ssq[:, i:i + 1],
                                                           scalar1=1.0 / D, scalar2=EPS, op0=ALU.mult, op1=ALU.add),
                     writes=[r_ssq[i]])
                p.op("pool", lambda e, i=i: e.tensor_tensor(out=rstd[:, i:i + 1], in0=tmp17[:, i:i + 1],
                                                            in1=cm05[:, 0:1], op=ALU.pow),
                     reads=[r_cm], writes=[r_ssq[i]])
                p.op("dve", lambda e, i=i, nb=nb: e.scalar_tensor_tensor(
                    out=ybuf[nb], in0=xres[:, i, :], scalar=rstd[:, i:i + 1], in1=gbc, op0=ALU.mult, op1=ALU.mult),
                    reads=[xr[i], r_gbc, r_ssq[i]], writes=[r_yb[nb]])
                (p.dma("sp", yv_out[:, i, :], ybuf[nb], key="yo%d" % (i % 4), reads=[r_yb[nb]]))
            A.release(m0)
            p.barrier()

        def dump_x():
            for i in range(NT):
                (p.dma("sp", yv_out[:, i, :], xres[:, i, :], key="yo%d" % (i % 2), reads=[xr[i]]))

        ffn("ffn1")
        if stop_after == "ffn1":
            dump_x()
        else:
            retT = A.alloc([4, TOK], BF16)
            r_retT = [Res() for _ in range(NT)]
            mix_ret(retT, r_retT)
            if stop_after == "ret_dbg":
                dump_x()
            elif stop_after == "ret":
                dbg = dout("dbg_retT", [128, 4, TOK], BF16)
                (p.dma("sp", dbg, retT, key="dbg", reads=r_retT))
                dump_x()
            else:
                mix_lru(retT, r_retT)
                A.release(base_mark)
                if stop_after == "mix":
                    dump_x()
                else:
                    KTp = A.alloc([8, 256], BF16)
                    Vp = A.alloc([2, 1024], BF16)
                    r_KTp, r_Vp = Res(), Res()
                    xattn_kv(KTp, r_KTp, Vp, r_Vp)
                    if stop_after != "xattn_kv":
                        xattn_main(KTp, r_KTp, Vp, r_Vp)
                    A.release(base_mark)
                    if stop_after in ("xattn", "xattn_kv", "xattn_p", "xattn_s0", "xattn_s1", "xattn_s2"):
                        dump_x()
                    else:
                        ffn("ffn2")
                        if stop_after == "ffn2":
                            dump_x()
                        else:
                            final_out()
        p.emit()
    return nc


def _consts():
    c = {}
    c["c_ident"] = np.eye(128, dtype=np.float32).astype(ml_dtypes.bfloat16)
    c["c_ones"] = np.ones((128, 128), np.float32)
    half = 64
    inv = (10000.0 ** (-np.arange(half, dtype=np.float32) / half)).astype(np.float32)
    pos = np.concatenate([np.arange(2048), 16384 + (np.arange(128) % 8)]).astype(np.float32)
    ang = (pos[None, :] * inv[:, None]).astype(np.float32)
    cos = np.cos(ang.astype(np.float64)).astype(np.float32)
    sin = np.sin(ang.astype(np.float64)).astype(np.float32)
    c["c_cos"] = np.concatenate([cos, cos], 0)
    c["c_sin"] = np.concatenate([-sin, sin], 0)
    lg = np.log(1.0 - 2.0 ** (-5.0 - np.arange(4, dtype=np.float64)))
    scale = 128.0 ** -0.5
    idx = np.arange(128)
    dm = np.zeros((2, 128, 4, 128), np.float64)
    qd = np.zeros((2, 128, 4, 128), np.float64)
    kd = np.zeros((2, 128, 4), np.float64)
    diff = idx[None, :] - idx[:, None]
    s_id = idx // 8
    t_id = idx % 8
    same = (s_id[:, None] == s_id[None, :])
    dts = t_id[None, :] - t_id[:, None]
    for h in range(4):
        dm[0, :, h, :] = np.where(diff >= 0, np.exp(lg[h] * np.maximum(diff, 0)), 0.0) * scale
        dm[1, :, h, :] = np.where(same & (dts >= 0), np.exp(lg[h] * np.maximum(dts, 0)), 0.0) * scale
        qd[0, :, h, :] = np.exp(lg[h] * (idx + 1.0))[None, :]
        qd[1, :, h, :] = np.exp(lg[h] * (t_id + 1.0))[None, :]
        kd[0, :, h] = np.exp(lg[h] * (127.0 - idx)) * scale
        kd[1, :, h] = np.exp(lg[h] * (7.0 - t_id)) * scale
    c["c_dmask"] = dm.astype(np.float32)
    c["c_qdec"] = qd.astype(np.float32)
    c["c_kdec"] = kd.astype(np.float32)
    bm = (s_id[None, :] == np.arange(16)[:, None]).astype(np.float32)
    c["c_bmask"] = np.broadcast_to(bm[None], (128, 16, 128)).astype(ml_dtypes.bfloat16)
    c["c_rmask"] = np.ascontiguousarray(bm.T)
    return c


_NC_CACHE = {}


def _prep_inputs(inp, stop_after=None):
    f = lambda a: np.ascontiguousarray(np.asarray(a, dtype=np.float32))
    shared = {}
    for n in ("ffn1", "ffn2"):
        shared[n + "_wg"] = f(inp[n + "_wg"][0])
        shared[n + "_wu"] = f(inp[n + "_wu"][0])
        shared[n + "_wd"] = f(inp[n + "_wd"][0])
    shared["w_in"] = f(inp["w_in"][0])
    shared["w_out"] = f(inp["w_out"][0])
    shared["wq"] = f(inp["xattn_wq"][0])
    shared["wk"] = f(inp["xattn_wk"][0])
    shared["wv"] = f(inp["xattn_wv"][0])
    shared["wo"] = f(inp["xattn_wo"][0])
    bc = lambda v, n: np.ascontiguousarray(np.broadcast_to(f(v).reshape(1, n), (128, n)))
    shared["g_ffn1"] = bc(inp["ffn1_norm"][0], D)
    shared["g_mix"] = bc(inp["mix_norm"][0], D)
    shared["g_xattn"] = bc(inp["xattn_norm"][0], D)
    shared["g_ffn2"] = bc(inp["ffn2_norm"][0], D)
    shared["g_final"] = bc(inp["final_norm"], D)
    shared["gn_gain"] = bc(inp["ret_gn_gain"][0], 512)
    fm = lambda v: f(v).reshape(4, 128).T
    lv = np.zeros((128, 4, 9), np.float32)
    cw = f(inp["conv_w"][0])
    for j in range(4):
        lv[:, :, j] = fm(cw[j])
    lv[:, :, 4] = fm(inp["conv_b"][0])
    lv[:, :, 5] = fm(inp["lru_ba"][0])
    lv[:, :, 6] = fm(inp["lru_bx"][0])
    lv[:, :, 7] = fm(inp["lru_lambda"][0])
    lv[:, :, 8] = fm(inp["lru_norm"][0])
    shared["lruvec"] = lv

    def bd(w):
        w = f(w)
        o = np.zeros((128, 4, 128), np.float32)
        for c in range(4):
            for b in range(2):
                o[b * 64:(b + 1) * 64, c, b * 64:(b + 1) * 64] = w[2 * c + b]
        return o
    shared["wabd"] = bd(inp["lru_wa"][0])
    shared["wxbd"] = bd(inp["lru_wx"][0])
    shared.update(_consts())
    maps = []
    for c in range(NCORES):
        m = dict(shared)
        xs = f(inp["x_sample"][16 * c:16 * c + 16]).reshape(128, D)
        m["x"] = np.ascontiguousarray(np.concatenate([f(inp["x_prompt"][c]), xs], 0))
        m["st_ret"] = f(inp["state_ret"][0, 16 * c:16 * c + 16])
        h0 = f(inp["state_lru_h"][0, 16 * c:16 * c + 16])
        m["st_h"] = np.ascontiguousarray(h0.reshape(16, 4, 128).transpose(2, 1, 0))
        cv0 = f(inp["state_lru_conv"][0, 16 * c:16 * c + 16])
        m["st_conv"] = np.ascontiguousarray(cv0.reshape(16, 3, 4, 128).transpose(3, 2, 0, 1))
        m["ck"] = f(inp["cache_mem_k"][0, 16 * c:16 * c + 16]).reshape(16, 256, D)
        m["cv"] = f(inp["cache_mem_v"][0, 16 * c:16 * c + 16]).reshape(16, 256, D)
        m["memp"] = f(inp["mem_prompt"][c])
        maps.append(m)
    return maps


def _run(inp, stop_after=None, cores=None):
    key = stop_after
    if key not in _NC_CACHE:
        _NC_CACHE[key] = build_program(stop_after)
    nc = _NC_CACHE[key]
    maps = _prep_inputs(inp)
    if cores is not None:
        maps = [maps[c] for c in cores]
    res = run_bass_kernel_spmd(nc, maps, core_ids=list(range(len(maps))))
    return res.results


def kernel(**inp):
    rs = _run(inp)
    g = lambda n: [np.asarray(r[n], dtype=np.float32) for r in rs]
    y = g("y")
    y_prompt = np.stack([a[0:2048] for a in y], 0)
    y_sample = np.concatenate([a[2048:].reshape(16, 8, D) for a in y], 0)
    new_ret_p = np.stack(g("o_retp"), 0)[None]
    new_h_p = np.stack([a.T.reshape(512) for a in g("o_hp")], 0)[None]
    new_conv_p = np.stack([a.transpose(2, 1, 0).reshape(3, 512) for a in g("o_convp")], 0)[None]
    mk = np.stack([a.reshape(256, 4, 256) for a in g("o_mk")], 0)[None]
    mv = np.stack([a.reshape(256, 4, 256) for a in g("o_mv")], 0)[None]
    new_ret_s = np.concatenate(g("o_rets"), 0)[None]
    new_h_s = np.concatenate([a.transpose(2, 1, 0).reshape(16, 512) for a in g("o_hs")], 0)[None]
    new_conv_s = np.concatenate([a.transpose(2, 3, 1, 0).reshape(16, 3, 512) for a in g("o_convs")], 0)[None]
    return (y_prompt, y_sample, new_ret_p, new_h_p, new_conv_p, mk, mv, new_ret_s, new_h_s, new_conv_s)
```

```python
import contextlib
import math
import numpy as np
import ml_dtypes
import concourse.bass as bass
import concourse.mybir as mybir
from concourse.bass_utils import run_bass_kernel_spmd

F32 = mybir.dt.float32
BF16 = mybir.dt.bfloat16
ALU = mybir.AluOpType
AF = mybir.ActivationFunctionType
AX = mybir.AxisListType

NCORES = 8
D = 1024
NT = 17
TOK = NT * 128
FF = 2816
NFC = FF // 128
EPS = 1e-6
BLOCKS = [(0, 512), (512, 512), (1024, 512), (1536, 512), (2048, 128)]
FF_PARTS = [(0, 8), (8, 16), (16, 22)]
ENGS = ("pe", "act", "dve", "pool", "sp")
AUTO_TRACK = True
PE_FILL = True
STRICT_WAR = False


class Op:
    __slots__ = ("idx", "kind", "eng", "call", "deps", "key", "val", "needed", "seq", "dur", "seg", "pos",
                 "nbytes", "waits", "cp", "tab", "fill", "waits_pre")

    def __init__(self, kind, eng):
        self.kind = kind
        self.eng = eng
        self.fill = 0
        self.waits_pre = []
        self.deps = []
        self.needed = False
        self.seq = None
        self.key = None
        self.val = 0
        self.pos = -1


class Res:
    __slots__ = ("name", "w", "r", "excl")

    def __init__(self, name="", excl=False):
        self.name = name
        self.w = None
        self.r = []
        self.excl = excl


class _Rec:
    def __init__(self):
        self.call = None

    def __getattr__(self, name):
        def f(*a, **kw):
            self.call = (name, a, kw)
            return self
        return f


def _free_elems(ap):
    n = 1
    for d in ap.shape[1:]:
        n *= int(d)
    return n


def _est_dur(eng, call):
    name, a, kw = call
    if name == "matmul":
        rhs = a[2] if len(a) > 2 else kw["rhs"]
        return 0.005 + _free_elems(rhs) / 2420.0
    if name == "transpose":
        return 0.06
    out = kw.get("out", a[0] if a else None)
    n = _free_elems(out) if out is not None else 64
    if eng == "act":
        return 0.12 + n * 0.00100
    if eng == "dve":
        return 0.12 + n * 0.00105
    if name == "tensor_scalar":
        return 0.35 + n * 0.0008
    if name == "tensor_tensor" and kw.get("op") == ALU.pow:
        return 0.5 + n * 0.17
    return 0.2 + n * 0.0021


class Prog:
    def __init__(self, nc):
        self.nc = nc
        self.all = []
        self.dma_cnt = {}
        self.dma_last = {}
        self.seg = 0
        self.pages = {}
        self.filler_ap = None

    def barrier(self):
        if not AUTO_TRACK:
            self.seg += 1

    PAGE = 1024

    def _ap_range(self, ap):
        try:
            if ap.tensor.name != "arena":
                return None
        except Exception:
            return None
        sz = 4 if ap.dtype == F32 else 2
        dims = list(ap.ap)
        pstride = dims[0][0]
        lo = ap.offset % pstride if pstride else ap.offset
        ext = 1
        for st_, cnt in dims[1:]:
            ext += abs(st_) * (cnt - 1)
        return (lo * sz, (lo + ext) * sz)

    def _track(self, op, call):
        if not AUTO_TRACK:
            return
        if call[0] == "dma":
            accs = [(call[1], True), (call[2], False)]
        else:
            name, a, kw = call
            accs = []
            items = list(kw.items()) + [("arg%d" % i, v) for i, v in enumerate(a)]
            for k, v in items:
                if not hasattr(v, "ap") or not hasattr(v, "tensor"):
                    continue
                is_out = k in ("out", "accum_out", "arg0")
                accs.append((v, is_out))
        seen = set(id(d) for d, _ in op.deps)
        pages = self.pages
        for ap, is_w in accs:
            r = self._ap_range(ap)
            if r is None:
                continue
            lo, hi = r
            for pg in range(lo // self.PAGE, (hi - 1) // self.PAGE + 1):
                lst = pages.setdefault(pg, [])
                keep = []
                for rec in lst:
                    rlo, rhi, rop, rw = rec
                    if rlo < hi and lo < rhi and rop is not op:
                        if is_w or rw:
                            if id(rop) not in seen:
                                seen.add(id(rop))
                                op.deps.append((rop, not rw))
                        if is_w and lo <= max(rlo, pg * self.PAGE) and min(rhi, (pg + 1) * self.PAGE) <= hi:
                            continue
                    keep.append(rec)
                keep.append((lo, hi, op, is_w))
                pages[pg] = keep

    def _edges(self, op, reads, writes, deps):
        for d in deps:
            if d is not None:
                op.deps.append((d, False))
        for R in reads:
            if R.w is not None:
                op.deps.append((R.w, False))
            if R.excl:
                for d in R.r:
                    if d.eng != op.eng:
                        op.deps.append((d, True))
        for R in writes:
            if R.w is not None:
                op.deps.append((R.w, False))
            for d in R.r:
                op.deps.append((d, True))
        for R in reads:
            R.r.append(op)
        for R in writes:
            R.w = op
            R.r = []

    def op(self, eng, fn, reads=(), writes=(), deps=()):
        o = Op("eng", eng)
        rec = _Rec()
        fn(rec)
        o.call = rec.call
        o.dur = _est_dur(eng, o.call)
        o.tab = None
        if eng == "act" and o.call[0] == "activation":
            fnc = o.call[2].get("func")
            if fnc in (AF.Exp, AF.Tanh):
                o.tab = "exp"
            elif fnc == AF.Sqrt:
                o.tab = "sqrt"
            elif fnc == AF.Ln:
                o.tab = "ln"
            elif fnc == AF.Silu:
                o.tab = "silu"
        o.seg = self.seg
        o.idx = len(self.all)
        self._edges(o, reads, writes, deps)
        self._track(o, o.call)
        self.all.append(o)
        return o

    def dma(self, eng, out, in_, key, reads=(), writes=(), deps=(), **kw):
        o = Op("dma", eng)
        o.tab = None
        o.call = ("dma", out, in_, kw)
        o.key = key
        self.dma_cnt[key] = self.dma_cnt.get(key, 0) + 16
        o.val = self.dma_cnt[key]
        deps = list(deps)
        if key in self.dma_last:
            deps.append(self.dma_last[key])
        self.dma_last[key] = o
        sz = 4 if out.dtype == F32 else 2
        o.nbytes = _free_elems(out) * int(out.shape[0]) * sz
        o.dur = 1.0 if eng == "pool" else 0.15
        o.seg = self.seg
        o.idx = len(self.all)
        self._edges(o, reads, writes, deps)
        self._track(o, o.call)
        self.all.append(o)
        return o

    def schedule(self):
        LAT = 0.3
        DMA_BW = 180e3
        WINDOW = 512
        order = {e: [] for e in ENGS}
        fin = {}
        nseg = self.seg + 1
        segs = [[] for _ in range(nseg)]
        for o in self.all:
            segs[o.seg].append(o)
        succ_cp = [0.0] * len(self.all)
        for o in reversed(self.all):
            mine = succ_cp[o.idx] + (o.dur if o.kind == "eng" else 2.5)
            o.cp = mine
            for d, _w in o.deps:
                if mine > succ_cp[d.idx]:
                    succ_cp[d.idx] = mine
        tnow = 0.0
        dma_clock = 0.0
        cur_tab = [None]
        for sg in segs:
            rem = {e: [o for o in sg if o.eng == e] for e in ENGS}
            clk = {e: tnow for e in ENGS}
            dma_clock = max(dma_clock, tnow)
            nleft = len(sg)
            cand = {e: None for e in ENGS}
            dirty = set(ENGS)
            while nleft:
                for e in list(dirty):
                    best = None
                    bestkey = None
                    lst = rem[e]
                    for j in range(min(WINDOW, len(lst))):
                        o = lst[j]
                        r = clk[e]
                        ok = True
                        for d, _w in o.deps:
                            f = fin.get(d.idx)
                            if f is None:
                                ok = False
                                break
                            if d.eng != e or d.kind == "dma":
                                f += LAT
                            if f > r:
                                r = f
                        if not ok:
                            continue
                        if r <= clk[e] + 1e-9:
                            key = (0 if (e == "act" and o.tab is not None and o.tab != cur_tab[0]) else 1, o.cp)
                            if best is None or best[0] > clk[e] + 1e-9 or key > bestkey:
                                best = (r, j, o)
                                bestkey = key
                        elif best is None or r < best[0] - 1e-9:
                            best = (r, j, o)
                    cand[e] = best
                dirty.clear()
                be = None
                for e in ENGS:
                    c = cand[e]
                    if c is not None and (be is None or c[0] < cand[be][0]):
                        be = e
                assert be is not None, "scheduler deadlock"
                r, j, o = cand[be]
                rem[be].pop(j)
                if o.kind == "dma":
                    clk[be] = r + o.dur
                    st = max(r, dma_clock)
                    dma_clock = st + o.nbytes / DMA_BW
                    fin[o.idx] = dma_clock + 2.0
                else:
                    if (be == "pe" and PE_FILL and o.call[0] == "matmul" and o.call[2].get("start")
                            and len(order["pe"]) > 64):
                        r_war = clk[be]
                        for d, w_ in o.deps:
                            if w_ and d.eng != "pe":
                                r_war = max(r_war, fin[d.idx] + LAT)
                        gap = r - r_war - 0.4
                        if gap > 0.6 and _free_elems(o.call[1][0]) >= 128:
                            o.fill = min(int(0.7 * gap / 0.07), 64)
                    extra = 0.0
                    if be == "act" and o.tab is not None and o.tab != cur_tab[0]:
                        extra = 1.3
                        cur_tab[0] = o.tab
                    clk[be] = r + o.dur + extra
                    fin[o.idx] = clk[be]
                o.pos = len(order[be])
                order[be].append(o)
                nleft -= 1
                dirty = set(ENGS)
            tnow = max([tnow] + [fin[o.idx] for o in sg])
        self.est_total = tnow
        return order

    def emit(self):
        nc = self.nc
        order = self.schedule()
        last_by_seg = {}
        waits = {}
        for e in ENGS:
            prev_seg = -1
            for o in order[e]:
                extra = []
                if o.seg != prev_seg:
                    for e2 in ENGS:
                        if e2 == e:
                            continue
                        cands = [x for x in order[e2] if x.seg < o.seg and x.kind == "eng"]
                        if cands:
                            extra.append(cands[-1])
                    keys = {}
                    for x in self.all:
                        if x.kind == "dma" and x.seg < o.seg:
                            keys[x.key] = x
                    extra.extend(keys.values())
                    prev_seg = o.seg
                per_eng = {}
                per_eng_war = {}
                dmas = {}
                for d, war in [(x, False) for x in extra] + o.deps:
                    if d.kind == "dma":
                        if d.key not in dmas or dmas[d.key].val < d.val:
                            dmas[d.key] = d
                        continue
                    if d.eng == e:
                        if e == "pe":
                            continue
                        if war and not STRICT_WAR:
                            continue
                    tgt = per_eng_war if (war and o.fill) else per_eng
                    if d.eng not in tgt or tgt[d.eng].pos < d.pos:
                        tgt[d.eng] = d
                o.waits_pre = list(per_eng_war.values())
                o.waits = list(per_eng.values()) + list(dmas.values())
                for d in list(per_eng.values()) + list(per_eng_war.values()):
                    d.needed = True
        for e in ENGS:
            c = 0
            for o in order[e]:
                if o.kind == "eng" and o.needed:
                    c += 1
                    o.seq = c
        with contextlib.ExitStack() as st:
            esem = {e: st.enter_context(nc.semaphore("s_" + e)) for e in ENGS}
            dsem = {k: st.enter_context(nc.semaphore("d_%s" % k)) for k in self.dma_cnt}
            block = st.enter_context(nc.Block())

            def run(e, engobj):
                waited = {}
                for o in order[e]:
                    if o.fill:
                        for d in o.waits_pre:
                            sem, val, k = esem[d.eng], d.seq, ("e", d.eng)
                            if waited.get(k, 0) >= val:
                                continue
                            waited[k] = val
                            engobj.wait_ge(sem, val)
                        fout = o.call[1][0][:, 0:128]
                        for _ in range(o.fill):
                            engobj.matmul(fout, self.filler_ap, self.filler_ap, start=True, stop=True)
                    for d in o.waits:
                        if d.kind == "eng":
                            sem, val, k = esem[d.eng], d.seq, ("e", d.eng)
                        else:
                            sem, val, k = dsem[d.key], d.val, ("d", d.key)
                        if waited.get(k, 0) >= val:
                            continue
                        waited[k] = val
                        engobj.wait_ge(sem, val)
                    fn = o.call
                    if fn[0] == "dma":
                        _, out, in_, kw = fn
                        engobj.dma_start(out=out, in_=in_, **kw).then_inc(dsem[o.key], 16)
                    else:
                        ins = getattr(engobj, fn[0])(*fn[1], **fn[2])
                        if o.needed:
                            ins.then_inc(esem[e], 1)

            @block.tensor
            def _(eng):
                run("pe", eng)

            @block.scalar
            def _(eng):
                run("act", eng)

            @block.vector
            def _(eng):
                run("dve", eng)

            @block.gpsimd
            def _(eng):
                run("pool", eng)

            @block.sync
            def _(eng):
                run("sp", eng)
                for k, v in self.dma_cnt.items():
                    eng.wait_ge(dsem[k], v)


class Arena:
    def __init__(self, t, nwords):
        self.t = t
        self.n = nwords
        self.off = 0

    def alloc(self, free, dtype=F32):
        n = int(np.prod(free))
        sz = 4 if dtype == F32 else 2
        words = (n * sz + 3) // 4
        words = (words + 1) // 2 * 2
        assert self.off + words <= self.n, ("arena overflow", self.off, words, self.n)
        ap = self.t[:, self.off:self.off + words]
        self.off += words
        if dtype != F32:
            ap = ap.bitcast(dtype)
        ap = ap[:, 0:n]
        if len(free) == 2:
            ap = ap.rearrange("p (a b) -> p a b", a=free[0])
        elif len(free) == 3:
            ap = ap.rearrange("p (a b c) -> p a b c", a=free[0], b=free[1])
        elif len(free) == 4:
            ap = ap.rearrange("p (a b c d) -> p a b c d", a=free[0], b=free[1], c=free[2])
        return ap

    def mark(self):
        return self.off

    def release(self, m):
        self.off = m


def build_program(stop_after=None):
    nc = bass.Bass("TRN2", target_bir_lowering=False)

    def din(name, shape, dt=F32):
        return nc.dram_tensor(name, list(shape), dt, kind="ExternalInput").ap()

    def dout(name, shape, dt=F32):
        return nc.dram_tensor(name, list(shape), dt, kind="ExternalOutput").ap()

    x_in = din("x", [TOK, D])
    st_ret = din("st_ret", [16, 4, 128, 128])
    st_h = din("st_h", [128, 4, 16])
    st_conv = din("st_conv", [128, 4, 16, 3])
    ck = din("ck", [16, 256, D])
    cv = din("cv", [16, 256, D])
    memp = din("memp", [256, D])
    gains = {n: din("g_" + n, [128, D]) for n in ("ffn1", "mix", "xattn", "ffn2", "final")}
    wts = {}
    for n in ("ffn1", "ffn2"):
        wts[n + "_wg"] = din(n + "_wg", [D, FF])
        wts[n + "_wu"] = din(n + "_wu", [D, FF])
        wts[n + "_wd"] = din(n + "_wd", [FF, D])
    w_in = din("w_in", [D, 3072])
    w_out = din("w_out", [D, D])
    wq = din("wq", [D, D])
    wk = din("wk", [D, D])
    wv = din("wv", [D, D])
    wo = din("wo", [D, D])
    gn_gain = din("gn_gain", [128, 512])
    lruvec = din("lruvec", [128, 4, 9])
    wabd = din("wabd", [128, 4, 128])
    wxbd = din("wxbd", [128, 4, 128])
    c_ident = din("c_ident", [128, 128], BF16)
    c_ones = din("c_ones", [128, 128])
    c_cos = din("c_cos", [128, TOK])
    c_sin = din("c_sin", [128, TOK])
    c_dmask = din("c_dmask", [2, 128, 4, 128])
    c_qdec = din("c_qdec", [2, 128, 4, 128])
    c_kdec = din("c_kdec", [2, 128, 4])
    c_bmask = din("c_bmask", [128, 16, 128], BF16)
    c_rmask = din("c_rmask", [128, 16])

    y_out = dout("y", [TOK, D])
    o_retp = dout("o_retp", [4, 128, 128])
    o_hp = dout("o_hp", [128, 4])
    o_convp = dout("o_convp", [128, 4, 3])
    o_mk = dout("o_mk", [256, D])
    o_mv = dout("o_mv", [256, D])
    o_rets = dout("o_rets", [16, 4, 128, 128])
    o_hs = dout("o_hs", [128, 4, 16])
    o_convs = dout("o_convs", [128, 4, 16, 3])

    GAMMA = [1.0 - 2.0 ** (-5.0 - h) for h in range(4)]

    NW = 212800 // 4
    with contextlib.ExitStack() as st:
        arena_t = st.enter_context(nc.sbuf_tensor("arena", [128, NW], F32))
        psum = st.enter_context(nc.psum_tensor("psum", [128, 4096], F32))
        p = Prog(nc)
        A = Arena(arena_t, NW)
        dbgn = [0]

        def dbg(name, ap, reads):
            shp = list(ap.shape)
            o = dout("dbg_" + name, shp, ap.dtype)
            dbgn[0] += 1
            (p.dma("sp", o, ap, key="dbg%d" % dbgn[0], reads=reads))

        bank = [psum[:, b * 512:(b + 1) * 512] for b in range(8)]
        bank_bf = [bank[b].bitcast(BF16) for b in range(8)]
        bres = [Res("bank%d" % b, excl=True) for b in range(8)]

        def bank2(b):
            return psum[:, b * 512:(b + 2) * 512]

        xres = A.alloc([NT, D], F32)
        xr = [Res("x%d" % i) for i in range(NT)]
        ident = A.alloc([128], BF16)
        r_ident = Res()
        ssq = A.alloc([NT], F32)
        rstd = A.alloc([NT], F32)
        tmp17 = A.alloc([NT], F32)
        cm05 = A.alloc([4], F32)
        cp05 = A.alloc([4], F32)
        r_cm = Res()
        gbc = A.alloc([D], F32)
        r_gbc = Res()
        xnb = [A.alloc([D], BF16) for _ in range(2)]
        r_xnb = [Res(), Res()]
        junk = A.alloc([D], BF16)
        r_junk = Res()
        r_ssq = [Res() for _ in range(NT)]
        base_mark = A.mark()

        xv_in = x_in.rearrange("(i p) d -> p i d", p=128)
        yv_out = y_out.rearrange("(i p) d -> p i d", p=128)

        p.dma("sp", ident, c_ident, key="c0", writes=[r_ident])
        p.filler_ap = ident
        p.op("dve", lambda e: e.memset(cm05, -0.5), writes=[r_cm])
        p.op("dve", lambda e: e.memset(cp05, 0.5), writes=[r_cm])
        for i in range(NT):
            p.dma("sp", xres[:, i, :], xv_in[:, i, :], key="xin%d" % (i % 4), writes=[xr[i]])

        kcnt = [0]

        def norm_tile(i, dst, dst_res, dst_col0, tpb):
            nb = i % 2
            p.op("act", lambda e: e.activation(out=junk, in_=xres[:, i, :], func=AF.Square,
                                               accum_out=ssq[:, i:i + 1]),
                 reads=[xr[i]], writes=[r_junk, r_ssq[i]])
            p.op("dve", lambda e: e.tensor_scalar(out=tmp17[:, i:i + 1], in0=ssq[:, i:i + 1], scalar1=1.0 / D,
                                                  scalar2=EPS, op0=ALU.mult, op1=ALU.add),
                 reads=[], writes=[r_ssq[i]])
            p.op("pool", lambda e: e.tensor_tensor(out=rstd[:, i:i + 1], in0=tmp17[:, i:i + 1],
                                                   in1=cm05[:, 0:1], op=ALU.pow),
                 reads=[r_cm], writes=[r_ssq[i]])
            p.op("dve", lambda e: e.scalar_tensor_tensor(out=xnb[nb], in0=xres[:, i, :], scalar=rstd[:, i:i + 1],
                                                         in1=gbc, op0=ALU.mult, op1=ALU.mult),
                 reads=[xr[i], r_gbc, r_ssq[i]], writes=[r_xnb[nb]])
            for k in range(8):
                p.op("pe", lambda e, k=k: e.transpose(bank_bf[tpb][:, k * 128:(k + 1) * 128],
                                                      xnb[nb][:, k * 128:(k + 1) * 128], ident),
                     reads=[r_xnb[nb], r_ident], writes=[bres[tpb]])
            p.op("act", lambda e: e.activation(out=dst[:, :, dst_col0:dst_col0 + 128],
                                               in_=bank_bf[tpb].rearrange("p (k n) -> p k n", k=8),
                                               func=AF.Copy),
                 reads=[bres[tpb]], writes=[dst_res])

        def load_gain(name):
            p.dma("sp", gbc, gains[name], key="gbc", writes=[r_gbc])

        def ffn(name):
            wgv = wts[name + "_wg"].rearrange("(k p) n -> p k n", p=128)
            wuv = wts[name + "_wu"].rearrange("(k p) n -> p k n", p=128)
            wdv = wts[name + "_wd"].rearrange("(f p) d -> p f d", p=128)
            m0 = A.mark()
            xnT = A.alloc([8, TOK], BF16)
            r_xnT = [Res() for _ in range(NT)]
            hT = A.alloc([8, TOK], BF16)
            r_hT = {}
            wd_sb = A.alloc([8, D], BF16)
            r_wd = Res()
            ring = [A.alloc([2, 8, 256], BF16) for _ in range(2)]
            r_rg = [Res(), Res()]
            r_ru = [Res(), Res()]
            sgb = [A.alloc([512], F32) for _ in range(2)]
            r_sgb = [Res(), Res()]
            load_gain(name)
            for i in range(NT):
                norm_tile(i, xnT, r_xnT[i], i * 128, 4 + 2 * (i % 2))
            gi = 0
            for (f0, f1) in FF_PARTS:
                nf = f1 - f0
                p.dma("pool", wd_sb[:, 0:nf, :], wdv[:, f0:f1, :], key="wd", writes=[r_wd])
                for g0 in range(f0, f1, 2):
                    sl = gi % 2
                    gi += 1
                    p.dma("pool", ring[sl][:, 0, :, :], wgv[:, :, g0 * 128:(g0 + 2) * 128], key="rg%d" % sl,
                          writes=[r_rg[sl]])
                    p.dma("pool", ring[sl][:, 1, :, :], wuv[:, :, g0 * 128:(g0 + 2) * 128], key="ru%d" % sl,
                          writes=[r_ru[sl]])
                    for fl in range(2):
                        f = g0 + fl
                        for bi, (t0, nb) in enumerate(BLOCKS):
                            par = kcnt[0] % 2
                            kcnt[0] += 1
                            gb, ub = par, 2 + par
                            tiles = range(t0 // 128, (t0 + nb) // 128)
                            for k in range(8):
                                p.op("pe", lambda e, k=k, fl=fl, sl=sl, gb=gb, t0=t0, nb=nb: e.matmul(
                                    bank[gb][:, 0:nb], ring[sl][:, 0, k, fl * 128:(fl + 1) * 128],
                                    xnT[:, k, t0:t0 + nb], start=(k == 0), stop=(k == 7)),
                                    reads=[r_rg[sl]] + [r_xnT[i] for i in tiles], writes=[bres[gb]])
                            for k in range(8):
                                p.op("pe", lambda e, k=k, fl=fl, sl=sl, ub=ub, t0=t0, nb=nb: e.matmul(
                                    bank[ub][:, 0:nb], ring[sl][:, 1, k, fl * 128:(fl + 1) * 128],
                                    xnT[:, k, t0:t0 + nb], start=(k == 0), stop=(k == 7)),
                                    reads=[r_ru[sl]] + [r_xnT[i] for i in tiles], writes=[bres[ub]])
                            p.op("act", lambda e, gb=gb, par=par, nb=nb: e.activation(
                                out=sgb[par][:, 0:nb], in_=bank[gb][:, 0:nb], func=AF.Silu),
                                reads=[bres[gb]], writes=[r_sgb[par]])
                            rh = r_hT.setdefault((f - f0, bi), Res())
                            p.op("dve", lambda e, ub=ub, par=par, nb=nb, t0=t0, f=f, f0=f0: e.tensor_tensor(
                                out=hT[:, f - f0, t0:t0 + nb], in0=sgb[par][:, 0:nb], in1=bank[ub][:, 0:nb],
                                op=ALU.mult),
                                reads=[r_sgb[par], bres[ub]], writes=[rh])
                for i in range(NT):
                    yb = 4 + 2 * (i % 2)
                    bi = min(i // 4, 4)
                    for f in range(nf):
                        for hc in range(2):
                            p.op("pe", lambda e, f=f, hc=hc, yb=yb, i=i: e.matmul(
                                bank[yb + hc], hT[:, f, i * 128:(i + 1) * 128],
                                wd_sb[:, f, hc * 512:(hc + 1) * 512], start=(f == 0), stop=(f == nf - 1)),
                                reads=[r_hT[(f, bi)], r_wd], writes=[bres[yb + hc]])
                    p.op("dve", lambda e, yb=yb, i=i: e.scalar_tensor_tensor(
                        out=xres[:, i, :], in0=bank2(yb), scalar=0.5, in1=xres[:, i, :],
                        op0=ALU.mult, op1=ALU.add),
                        reads=[bres[yb], bres[yb + 1]], writes=[xr[i]])
            A.release(m0)
            p.barrier()
            return None

        def interleave(*gens):
            gens = list(gens)
            while gens:
                for g in list(gens):
                    try:
                        next(g)
                    except StopIteration:
                        gens.remove(g)

        def norm_block(t0, nb, xnTb, r_xnTb):
            for tt in range(nb // 128):
                norm_tile(t0 // 128 + tt, xnTb, r_xnTb, tt * 128, 5)

        def mix_ret(retT, r_retT):
            winv = w_in.rearrange("(k p) n -> p k n", p=128)
            m0 = A.mark()
            wqk = A.alloc([8, 1024], BF16)
            wvg = A.alloc([8, 1024], BF16)
            r_wqk, r_wvg = Res(), Res()
            p.dma("pool", wqk, winv[:, :, 0:1024], key="wA", writes=[r_wqk])
            p.dma("pool", wvg, winv[:, :, 1024:2048], key="wB", writes=[r_wvg])
            dmask = A.alloc([2, 512], F32)
            qdec = A.alloc([2, 512], F32)
            kdec = A.alloc([2, 4], F32)
            gng = A.alloc([512], F32)
            bmask = A.alloc([16, 128], BF16)
            rmask = A.alloc([16], F32)
            r_c = Res()
            for v in range(2):
                p.dma("sp", dmask[:, v, :], c_dmask[v].rearrange("p h i -> p (h i)"), key="c0", writes=[r_c])
                p.dma("sp", qdec[:, v, :], c_qdec[v].rearrange("p h i -> p (h i)"), key="c0", writes=[r_c])
                p.dma("sp", kdec[:, v, :], c_kdec[v], key="c0", writes=[r_c])
            p.dma("sp", gng, gn_gain, key="c0", writes=[r_c])
            p.dma("sp", bmask, c_bmask, key="c0", writes=[r_c])
            p.dma("sp", rmask, c_rmask, key="c0", writes=[r_c])
            load_gain("mix")
            xnTb = A.alloc([8, 256], BF16)
            r_xnTb = Res()
            cosb = A.alloc([256], F32)
            sinb = A.alloc([256], F32)
            r_cs = Res()
            rt1s = [A.alloc([256], F32) for _ in range(2)]
            rt2s = [A.alloc([256], F32) for _ in range(2)]
            r_rt1s, r_rt2s = [Res(), Res()], [Res(), Res()]
            rt1, rt2, r_rt1, r_rt2 = rt1s[0], rt2s[0], r_rt1s[0], r_rt2s[0]
            qT = A.alloc([4, 256], BF16)
            kT = A.alloc([4, 256], BF16)
            qdT = A.alloc([4, 256], BF16)
            kdn = A.alloc([2, 4, 128], BF16)
            vsb = A.alloc([2, 512], BF16)
            sg = A.alloc([2, 512], F32)
            r_qT, r_kT, r_qdT = Res(), Res(), Res()
            r_kdn = [Res() for _ in range(4)]
            r_v = [Res() for _ in range(4)]
            r_sg = [Res() for _ in range(4)]
            tht = A.alloc([512], F32)
            r_tht = Res()
            state = A.alloc([4, 128], F32)
            state_bf = A.alloc([4, 128], BF16)
            r_state, r_statebf = Res(), Res()
            sms = [A.alloc([512], BF16) for _ in range(2)]
            r_sms = [Res(), Res()]
            ons = [A.alloc([512], F32) for _ in range(2)]
            r_ons = [Res(), Res()]
            ros = [A.alloc([512], BF16) for _ in range(2)]
            r_ros = [Res(), Res()]
            statss = [A.alloc([8, 4], F32) for _ in range(2)]
            r_statss = [Res(), Res()]
            sm, r_sm, on, r_on, ro, r_ro, stats, r_stats = sms[0], r_sms[0], ons[0], r_ons[0], ros[0], r_ros[0], statss[0], r_statss[0]
            p.op("dve", lambda e: e.memset(state, 0.0), writes=[r_state])
            p.op("dve", lambda e: e.memset(state_bf, 0.0), writes=[r_statebf])

            def ret_stream(blocks, Bf):
                (xnTb, r_xnTb, cosb, sinb, r_cs, rt1s, rt2s, r_rt1s, r_rt2s, qT, kT, qdT, r_qT, r_kT, r_qdT,
                 kdn, r_kdn, vsb, r_v, sg, r_sg) = (Bf[k] for k in (
                    "xnTb", "r_xnTb", "cosb", "sinb", "r_cs", "rt1s", "rt2s", "r_rt1s", "r_rt2s", "qT", "kT", "qdT",
                    "r_qT", "r_kT", "r_qdT", "kdn", "r_kdn", "vsb", "r_v", "sg", "r_sg"))
                rt1, rt2, r_rt1, r_rt2 = rt1s[0], rt2s[0], r_rt1s[0], r_rt2s[0]
                for (t0, nb) in blocks:
                    is_s = (t0 == 2048)
                    v = 1 if is_s else 0
                    nt = nb // 128
                    norm_block(t0, nb, xnTb, r_xnTb)
                    p.dma("sp", cosb[:, 0:nb], c_cos[:, t0:t0 + nb], key=("cs%d" % (1 if t0 == 2048 else 0)), writes=[r_cs])
                    p.dma("sp", sinb[:, 0:nb], c_sin[:, t0:t0 + nb], key=("cs%d" % (1 if t0 == 2048 else 0)), writes=[r_cs])
                    for c in range(8):
                        pb = c % 2
                        rt1, rt2, r_rt1, r_rt2 = rt1s[pb], rt2s[pb], r_rt1s[pb], r_rt2s[pb]
                        for k in range(8):
                            p.op("pe", lambda e, k=k, c=c, pb=pb: e.matmul(
                                bank[pb][:, 0:nb], wqk[:, k, c * 128:(c + 1) * 128], xnTb[:, k, 0:nb],
                                start=(k == 0), stop=(k == 7)),
                                reads=[r_wqk, r_xnTb], writes=[bres[pb]])
                        dst = qT if c < 4 else kT
                        r_dst = r_qT if c < 4 else r_kT
                        h = c % 4
                        p.op("dve", lambda e, pb=pb: e.tensor_tensor(out=rt1[:, 0:nb], in0=bank[pb][:, 0:nb],
                                                                      in1=cosb[:, 0:nb], op=ALU.mult),
                             reads=[bres[pb], r_cs], writes=[r_rt1])
                        p.op("dve", lambda e, pb=pb: e.tensor_tensor(out=rt2[0:64, 0:nb], in0=bank[pb][64:128, 0:nb],
                                                                      in1=sinb[0:64, 0:nb], op=ALU.mult),
                             reads=[bres[pb], r_cs], writes=[r_rt2])
                        p.op("dve", lambda e, pb=pb: e.tensor_tensor(out=rt2[64:128, 0:nb], in0=bank[pb][0:64, 0:nb],
                                                                      in1=sinb[64:128, 0:nb], op=ALU.mult),
                             reads=[bres[pb], r_cs], writes=[r_rt2])
                        p.op("pool", lambda e, dst=dst, h=h: e.tensor_tensor(out=dst[:, h, 0:nb], in0=rt1[:, 0:nb],
                                                                             in1=rt2[:, 0:nb], op=ALU.add),
                             reads=[r_rt1, r_rt2], writes=[r_dst])
                    for h in range(4):
                        p.op("pool", lambda e, h=h: e.tensor_tensor(
                            out=qdT[:, h, 0:nb].rearrange("p (t i) -> p t i", i=128),
                            in0=qT[:, h, 0:nb].rearrange("p (t i) -> p t i", i=128),
                            in1=qdec[:, v, h * 128:(h + 1) * 128].unsqueeze(1).to_broadcast([128, nt, 128]),
                            op=ALU.mult),
                            reads=[r_qT, r_c], writes=[r_qdT])
                    for tt in range(nt):
                        for h in range(4):
                            p.op("pe", lambda e, h=h, tt=tt: e.transpose(
                                bank_bf[5][:, h * 128:(h + 1) * 128], kT[:, h, tt * 128:(tt + 1) * 128], ident),
                                reads=[r_kT, r_ident], writes=[bres[5]])
                        p.op("dve", lambda e, tt=tt: e.tensor_tensor(
                            out=kdn[:, tt, :, :],
                            in0=bank_bf[5][:, 0:512].rearrange("p (h d) -> p h d", h=4),
                            in1=kdec[:, v, :].unsqueeze(2).to_broadcast([128, 4, 128]), op=ALU.mult),
                            reads=[bres[5], r_c], writes=[r_kdn[tt]])
                        for k in range(8):
                            p.op("pe", lambda e, k=k, tt=tt: e.matmul(
                                bank[2], xnTb[:, k, tt * 128:(tt + 1) * 128], wvg[:, k, 0:512],
                                start=(k == 0), stop=(k == 7)),
                                reads=[r_wvg, r_xnTb], writes=[bres[2]])
                        p.op("act", lambda e, tt=tt: e.activation(out=vsb[:, tt, :], in_=bank[2], func=AF.Copy),
                             reads=[bres[2]], writes=[r_v[tt]])
                        for k in range(8):
                            p.op("pe", lambda e, k=k, tt=tt: e.matmul(
                                bank[3], xnTb[:, k, tt * 128:(tt + 1) * 128], wvg[:, k, 512:1024],
                                start=(k == 0), stop=(k == 7)),
                                reads=[r_wvg, r_xnTb], writes=[bres[3]])
                        p.op("act", lambda e: e.activation(out=tht, in_=bank[3], func=AF.Tanh, scale=0.5),
                             reads=[bres[3]], writes=[r_tht])
                        p.op("dve", lambda e, tt=tt: e.scalar_tensor_tensor(
                            out=sg[:, tt, :], in0=tht, scalar=1.0, in1=bank[3], op0=ALU.add, op1=ALU.mult),
                            reads=[r_tht, bres[3]], writes=[r_sg[tt]])
                    if is_s:
                        m1 = A.mark()
                        st0 = [A.alloc([8, 128], F32)] * 2
                        st0b = [A.alloc([8, 128], BF16)] * 2
                        r_st0 = [Res()] * 2
                        r_st0b = [Res()] * 2
                        qdm = [A.alloc([4, 2, 128], BF16)] * 2
                        r_qdm = [Res()] * 2
                        kdm = [A.alloc([4, 128], BF16) for _ in range(2)]
                        r_kdm = [Res(), Res()]
                        nst = [A.alloc([4, 128], F32) for _ in range(2)]
                        r_nst = [Res(), Res()]
                        oacc = A.alloc([512], F32)
                        r_oacc = Res()
                    yield
                    for tt in range(nt):
                        tl = slice(tt * 128, (tt + 1) * 128)
                        gi = t0 // 128 + tt
                        pq = gi % 2
                        sm, r_sm, on, r_on, ro, r_ro, stats, r_stats = (sms[pq], r_sms[pq], ons[pq], r_ons[pq], ros[pq],
                                                                        r_ros[pq], statss[pq], r_statss[pq])
                        ob = 6 + pq
                        if is_s:
                            o_h = [oacc[:, h * 128:(h + 1) * 128] for h in range(4)]
                            o_all = oacc.rearrange("p (h d) -> p h d", h=4)
                            o_res = [r_oacc] * 4
                        else:
                            o_h = [bank[ob][:, h * 128:(h + 1) * 128] for h in range(4)]
                            o_all = bank[ob].rearrange("p (h d) -> p h d", h=4)
                            o_res = [bres[ob]] * 4
                        for h in range(4):
                            p.op("pe", lambda e, h=h, tl=tl: e.matmul(
                                bank[4][:, h * 128:(h + 1) * 128], kT[:, h, tl], qT[:, h, tl], start=True, stop=True),
                                reads=[r_kT, r_qT], writes=[bres[4]])
                        p.op("dve", lambda e: e.tensor_tensor(out=sm, in0=bank[4], in1=dmask[:, v, :], op=ALU.mult),
                             reads=[bres[4], r_c], writes=[r_sm])
                        if not is_s:
                            for h in range(4):
                                hs_ = slice(h * 128, (h + 1) * 128)
                                p.op("pe", lambda e, hs_=hs_, tt=tt: e.matmul(
                                    bank[ob][:, hs_], sm[:, hs_], vsb[:, tt, hs_], start=True, stop=False),
                                    reads=[r_sm, r_v[tt]], writes=[bres[ob]])
                                p.op("pe", lambda e, hs_=hs_, h=h, tl=tl: e.matmul(
                                    bank[ob][:, hs_], qdT[:, h, tl], state_bf[:, h, :], start=False, stop=True),
                                    reads=[r_qdT, r_statebf], writes=[bres[ob]])
                            for h in range(4):
                                hs_ = slice(h * 128, (h + 1) * 128)
                                p.op("pe", lambda e, hs_=hs_, h=h, tt=tt: e.matmul(
                                    bank[4][:, hs_], kdn[:, tt, h, :], vsb[:, tt, hs_], start=True, stop=True),
                                    reads=[r_kdn[tt], r_v[tt]], writes=[bres[4]])
                            for h in range(4):
                                hs_ = slice(h * 128, (h + 1) * 128)
                                p.op("dve", lambda e, hs_=hs_, h=h: e.scalar_tensor_tensor(
                                    out=state[:, h, :], in0=state[:, h, :], scalar=float(GAMMA[h] ** 128),
                                    in1=bank[4][:, hs_], op0=ALU.mult, op1=ALU.add),
                                    reads=[bres[4]], writes=[r_state])
                            p.op("act", lambda e: e.activation(out=state_bf, in_=state, func=AF.Copy),
                                 reads=[r_state], writes=[r_statebf])
                            if gi == 15:
                                (p.dma("sp", o_retp.rearrange("h d e -> d h e"), state, key="osm",
                                                     reads=[r_state]))
                        else:
                            for h in range(4):
                                hs_ = slice(h * 128, (h + 1) * 128)
                                p.op("pe", lambda e, hs_=hs_, h=h: e.matmul(
                                    bank[ob][:, hs_], sm[:, hs_], vsb[:, 0, hs_], start=True, stop=True),
                                    reads=[r_sm, r_v[0]], writes=[bres[ob]])
                            p.op("dve", lambda e: e.tensor_copy(out=oacc, in_=bank[ob]),
                                 reads=[bres[ob]], writes=[r_oacc])
                            for g in range(8):
                                yield
                                b2 = g % 2
                                p.dma("sp", st0[b2], st_ret[2 * g:2 * g + 2].rearrange("s h d e -> d (s h) e"),
                                      key="st0", writes=[r_st0[b2]])
                                p.op("act", lambda e, b2=b2: e.activation(out=st0b[b2], in_=st0[b2], func=AF.Copy),
                                     reads=[r_st0[b2]], writes=[r_st0b[b2]])
                                for h in range(4):
                                    p.op("dve", lambda e, h=h, b2=b2, g=g: e.tensor_tensor(
                                        out=qdm[b2][:, h, :, :],
                                        in0=qdT[:, h, 0:128].unsqueeze(1).to_broadcast([128, 2, 128]),
                                        in1=bmask[:, 2 * g:2 * g + 2, :], op=ALU.mult),
                                        reads=[r_qdT, r_c], writes=[r_qdm[b2]])
                                for h in range(4):
                                    hs_ = slice(h * 128, (h + 1) * 128)
                                    for sl in range(2):
                                        p.op("pe", lambda e, hs_=hs_, h=h, sl=sl, b2=b2: e.matmul(
                                            bank[ob][:, hs_], qdm[b2][:, h, sl, :], st0b[b2][:, sl * 4 + h, :],
                                            start=(sl == 0), stop=(sl == 1)),
                                            reads=[r_qdm[b2], r_st0b[b2]], writes=[bres[ob]])
                                p.op("dve", lambda e: e.tensor_tensor(out=oacc, in0=bank[ob], in1=oacc, op=ALU.add),
                                     reads=[bres[ob]], writes=[r_oacc])
                                for sl in range(2):
                                    s_ = 2 * g + sl
                                    k2 = s_ % 2
                                    p.op("dve", lambda e, k2=k2, s_=s_: e.tensor_scalar(
                                        out=kdm[k2], in0=kdn[:, 0, :, :], scalar1=rmask[:, s_:s_ + 1], scalar2=None,
                                        op0=ALU.mult),
                                        reads=[r_kdn[0], r_c], writes=[r_kdm[k2]])
                                    for h in range(4):
                                        hs_ = slice(h * 128, (h + 1) * 128)
                                        p.op("pe", lambda e, hs_=hs_, h=h, k2=k2: e.matmul(
                                            bank[7][:, hs_], kdm[k2][:, h, :], vsb[:, 0, hs_], start=True, stop=True),
                                            reads=[r_kdm[k2], r_v[0]], writes=[bres[7]])
                                    for h in range(4):
                                        hs_ = slice(h * 128, (h + 1) * 128)
                                        p.op("dve", lambda e, hs_=hs_, h=h, k2=k2, sl=sl, b2=b2: e.scalar_tensor_tensor(
                                            out=nst[k2][:, h, :], in0=st0[b2][:, sl * 4 + h, :],
                                            scalar=float(GAMMA[h] ** 8), in1=bank[7][:, hs_],
                                            op0=ALU.mult, op1=ALU.add),
                                            reads=[bres[7], r_st0[b2]], writes=[r_nst[k2]])
                                    (p.dma("sp", o_rets[s_].rearrange("h d e -> d h e"), nst[k2],
                                                         key="ons%d" % k2, reads=[r_nst[k2]]))
                        p.op("dve", lambda e: e.tensor_reduce(
                            out=stats[:, 0, :], in_=o_all, axis=AX.X, op=ALU.add),
                            reads=list(set(o_res)), writes=[r_stats])
                        for h in range(4):
                            p.op("act", lambda e, h=h: e.activation(
                                out=junk[:, 0:128], in_=o_h[h], func=AF.Square,
                                accum_out=stats[:, 1, h:h + 1]),
                                reads=[o_res[h]], writes=[r_junk, r_stats])
                        p.op("dve", lambda e: e.tensor_scalar(out=stats[:, 2, :], in0=stats[:, 0, :], scalar1=1.0 / 128,
                                                              scalar2=None, op0=ALU.mult), writes=[r_stats])
                        p.op("dve", lambda e: e.tensor_tensor(out=stats[:, 3, :], in0=stats[:, 2, :], in1=stats[:, 2, :],
                                                              op=ALU.mult), writes=[r_stats])
                        p.op("dve", lambda e: e.scalar_tensor_tensor(out=stats[:, 4, :], in0=stats[:, 1, :],
                                                                     scalar=1.0 / 128, in1=stats[:, 3, :],
                                                                     op0=ALU.mult, op1=ALU.subtract), writes=[r_stats])
                        p.op("dve", lambda e: e.tensor_scalar(out=stats[:, 4, :], in0=stats[:, 4, :], scalar1=0.0,
                                                              scalar2=EPS, op0=ALU.max, op1=ALU.add), writes=[r_stats])
                        p.op("pool", lambda e: e.tensor_tensor(out=stats[:, 5, :], in0=stats[:, 4, :], in1=cm05[:, 0:4],
                                                               op=ALU.pow), reads=[r_cm], writes=[r_stats])
                        for h in range(4):
                            hs_ = slice(h * 128, (h + 1) * 128)
                            p.op("dve", lambda e, hs_=hs_, h=h: e.tensor_scalar(
                                out=on[:, hs_], in0=o_h[h], scalar1=stats[:, 2, h:h + 1],
                                scalar2=stats[:, 5, h:h + 1], op0=ALU.subtract, op1=ALU.mult),
                                reads=[o_res[h], r_stats], writes=[r_on])
                        p.op("pool", lambda e: e.tensor_tensor(out=on, in0=on, in1=gng, op=ALU.mult),
                             reads=[r_c], writes=[r_on])
                        p.op("dve", lambda e, tt=tt: e.scalar_tensor_tensor(
                            out=ro, in0=on, scalar=0.5, in1=sg[:, tt, :], op0=ALU.mult, op1=ALU.mult),
                            reads=[r_on, r_sg[tt]], writes=[r_ro])
                        for h in range(4):
                            hs_ = slice(h * 128, (h + 1) * 128)
                            p.op("pe", lambda e, hs_=hs_: e.transpose(bank_bf[5][:, hs_], ro[:, hs_], ident),
                                 reads=[r_ro, r_ident], writes=[bres[5]])
                        p.op("act", lambda e, gi=gi: e.activation(
                            out=retT[:, :, gi * 128:(gi + 1) * 128],
                            in_=bank_bf[5][:, 0:512].rearrange("p (h d) -> p h d", h=4), func=AF.Copy),
                            reads=[bres[5]], writes=[r_retT[gi]])
                        yield
                    if is_s:
                        A.release(m1)
            Bf_p = dict(xnTb=xnTb, r_xnTb=r_xnTb, cosb=cosb, sinb=sinb, r_cs=r_cs, rt1s=rt1s, rt2s=rt2s,
                        r_rt1s=r_rt1s, r_rt2s=r_rt2s, qT=qT, kT=kT, qdT=qdT, r_qT=r_qT, r_kT=r_kT, r_qdT=r_qdT,
                        kdn=kdn, r_kdn=r_kdn, vsb=vsb, r_v=r_v, sg=sg, r_sg=r_sg)
            Bf_s = dict(xnTb=A.alloc([8, 128], BF16), r_xnTb=Res(), cosb=A.alloc([128], F32),
                        sinb=A.alloc([128], F32), r_cs=Res(),
                        rt1s=[A.alloc([128], F32)] * 2, rt2s=[A.alloc([128], F32)] * 2,
                        r_rt1s=[Res()] * 2, r_rt2s=[Res()] * 2,
                        qT=A.alloc([4, 128], BF16), kT=A.alloc([4, 128], BF16), qdT=A.alloc([4, 128], BF16),
                        r_qT=Res(), r_kT=Res(), r_qdT=Res(),
                        kdn=A.alloc([1, 4, 128], BF16), r_kdn=[Res()], vsb=A.alloc([1, 512], BF16), r_v=[Res()],
                        sg=A.alloc([1, 512], F32), r_sg=[Res()])
            interleave(ret_stream([(j * 256, 256) for j in range(8)], Bf_p), ret_stream([(2048, 128)], Bf_s))
            A.release(m0)
            p.barrier()

        def mix_lru(retT, r_retT):
            winv = w_in.rearrange("(k p) n -> p k n", p=128)
            woutv = w_out.rearrange("(k p) n -> p k n", p=128)
            m0 = A.mark()
            wug = A.alloc([8, 1024], BF16)
            wo_sb = A.alloc([8, 1024], BF16)
            r_wug, r_wo = Res(), Res()
            p.dma("pool", wug, winv[:, :, 2048:3072], key="wA", writes=[r_wug])
            p.dma("pool", wo_sb, woutv, key="wB", writes=[r_wo])
            wa_sb = A.alloc([4, 128], BF16)
            wx_sb = A.alloc([4, 128], BF16)
            r_wax = Res()
            p.dma("pool", wa_sb, wabd, key="wC", writes=[r_wax])
            p.dma("pool", wx_sb, wxbd, key="wC", writes=[r_wax])
            lv = A.alloc([4, 9], F32)
            ones = A.alloc([128], F32)
            h0T = A.alloc([4, 16], F32)
            r_c = Res()
            p.dma("sp", lv, lruvec, key="c0", writes=[r_c])
            p.dma("sp", ones, c_ones, key="c0", writes=[r_c])
            p.dma("sp", h0T, st_h, key="c0", writes=[r_c])
            load_gain("mix")
            cst = A.alloc([4, 8], F32)
            r_cst = Res()
            p.op("act", lambda e: e.activation(out=cst[:, :, 0], in_=lv[:, :, 7], func=AF.Exp, scale=-1.0),
                 reads=[r_c], writes=[r_cst])
            p.op("act", lambda e: e.activation(out=cst[:, :, 1], in_=cst[:, :, 0], func=AF.Ln, bias=1.0),
                 writes=[r_cst])
            p.op("dve", lambda e: e.tensor_scalar(out=cst[:, :, 2], in0=cst[:, :, 1], scalar1=-8.0, scalar2=None,
                                                  op0=ALU.mult), writes=[r_cst])
            p.op("dve", lambda e: e.tensor_scalar(out=cst[:, :, 3], in0=cst[:, :, 1], scalar1=-4.0, scalar2=None,
                                                  op0=ALU.mult), writes=[r_cst])
            p.op("dve", lambda e: e.tensor_scalar(out=cst[:, :, 4], in0=lv[:, :, 5], scalar1=0.5, scalar2=None,
                                                  op0=ALU.mult), reads=[r_c], writes=[r_cst])
            p.op("dve", lambda e: e.tensor_scalar(out=cst[:, :, 5], in0=lv[:, :, 6], scalar1=0.5, scalar2=None,
                                                  op0=ALU.mult), reads=[r_c], writes=[r_cst])
            xnTb = A.alloc([8, 512], BF16)
            r_xnTb = Res()
            ubuf = A.alloc([4, 3 + 512], F32)
            r_ub = [Res() for _ in range(4)]
            hprev = A.alloc([4], F32)
            r_hp = [Res() for _ in range(4)]
            p.op("dve", lambda e: e.memset(hprev, 0.0), writes=r_hp)
            p.op("dve", lambda e: e.memset(ubuf[:, :, 0:3], 0.0), writes=r_ub)
            hs = A.alloc([4, 512], F32)
            r_hs = [Res() for _ in range(4)]
            gg = A.alloc([4, 512], BF16)
            r_gg = [Res() for _ in range(4)]
            TS = []
            for _ in range(2):
                d = {}
                for nm in ("ta", "tb", "xc", "tr", "ti", "av", "a2", "sv", "gx"):
                    d[nm] = A.alloc([512], F32)
                    d["r_" + nm] = Res()
                d["xcb"] = A.alloc([512], BF16)
                d["r_xcb"] = Res()
                TS.append(d)
            sqb = [A.alloc([512], F32) for _ in range(2)]
            rl = A.alloc([512], F32)
            mixL = A.alloc([4, 512], BF16)
            t16 = A.alloc([16], F32)
            hsl = A.alloc([4, 16], F32)
            r_hsl = Res()
            r_rl, r_lo, r_t16 = Res(), Res(), Res()
            r_sqb = [Res(), Res()]
            r_mixL = Res()
            GC = 0.7978845608028654

            for (t0, nb) in BLOCKS:
                is_s = (t0 == 2048)
                nt = nb // 128
                last_p = (t0 == 1536)
                norm_block(t0, nb, xnTb, r_xnTb)
                if is_s:
                    us = ubuf[:, :, 0:176].rearrange("p c (s j) -> p c s j", j=11)
                    for c in range(4):
                        p.dma("sp", us[:, c, :, 0:3], st_conv[:, c, :, :], key="c0", writes=[r_ub[c]])
                for c in range(4):
                    T_ = TS[c % 2]
                    ta, tb, xc, xcb, tr, ti, av, a2, sv, gx = (T_[k] for k in
                                                               ("ta", "tb", "xc", "xcb", "tr", "ti", "av", "a2", "sv", "gx"))
                    r_ta, r_tb, r_xc, r_xcb, r_tr, r_ti, r_av, r_a2, r_sv, r_gx = (
                        T_["r_" + k] for k in ("ta", "tb", "xc", "xcb", "tr", "ti", "av", "a2", "sv", "gx"))
                    for k in range(8):
                        p.op("pe", lambda e, k=k: e.matmul(
                            bank[0][:, 0:nb], wug[:, k, c * 128:(c + 1) * 128], xnTb[:, k, 0:nb],
                            start=(k == 0), stop=(k == 7)), reads=[r_wug, r_xnTb], writes=[bres[0]])
                    if not is_s:
                        p.op("act", lambda e: e.activation(out=ubuf[:, c, 3:3 + nb], in_=bank[0][:, 0:nb],
                                                           func=AF.Copy), reads=[bres[0]], writes=[r_ub[c]])
                    else:
                        p.op("act", lambda e: e.activation(
                            out=us[:, c, :, 3:11], in_=bank[0][:, 0:128].rearrange("p (s t) -> p s t", t=8),
                            func=AF.Copy), reads=[bres[0]], writes=[r_ub[c]])
                    for k in range(8):
                        p.op("pe", lambda e, k=k: e.matmul(
                            bank[1][:, 0:nb], wug[:, k, 512 + c * 128:512 + (c + 1) * 128], xnTb[:, k, 0:nb],
                            start=(k == 0), stop=(k == 7)), reads=[r_wug, r_xnTb], writes=[bres[1]])
                    p.op("act", lambda e: e.activation(out=ta[:, 0:nb], in_=bank[1][:, 0:nb], func=AF.Square),
                         reads=[bres[1]], writes=[r_ta])
                    p.op("pool", lambda e: e.tensor_scalar(out=ta[:, 0:nb], in0=ta[:, 0:nb], scalar1=0.044715,
                                                           scalar2=1.0, op0=ALU.mult, op1=ALU.add), writes=[r_ta])
                    p.op("dve", lambda e: e.tensor_tensor(out=ta[:, 0:nb], in0=ta[:, 0:nb], in1=bank[1][:, 0:nb],
                                                          op=ALU.mult), reads=[bres[1]], writes=[r_ta])
                    p.op("act", lambda e: e.activation(out=tb[:, 0:nb], in_=ta[:, 0:nb], func=AF.Tanh, scale=GC),
                         reads=[r_ta], writes=[r_tb])
                    p.op("dve", lambda e: e.scalar_tensor_tensor(
                        out=gg[:, c, 0:nb], in0=tb[:, 0:nb], scalar=1.0, in1=bank[1][:, 0:nb],
                        op0=ALU.add, op1=ALU.mult), reads=[r_tb, bres[1]], writes=[r_gg[c]])
                    if not is_s:
                        uin = lambda j: ubuf[:, c, j:j + nb]
                        xcv = xc[:, 0:nb]
                    else:
                        uin = lambda j: us[:, c, :, j:j + 8]
                        xcv = xc[:, 0:128].rearrange("p (s t) -> p s t", t=8)
                    p.op("pool", lambda e: e.tensor_scalar(out=xcv, in0=uin(0), scalar1=lv[:, c, 0:1],
                                                           scalar2=lv[:, c, 4:5], op0=ALU.mult, op1=ALU.add),
                         reads=[r_ub[c], r_c], writes=[r_xc])
                    for j in range(1, 4):
                        p.op("dve", lambda e, j=j: e.scalar_tensor_tensor(
                            out=xcv, in0=uin(j), scalar=lv[:, c, j:j + 1], in1=xcv, op0=ALU.mult, op1=ALU.add),
                            reads=[r_ub[c], r_c], writes=[r_xc])
                    if last_p:
                        (p.dma("sp", o_convp[:, c, :], ubuf[:, c, nb:nb + 3], key="osm",
                                             reads=[r_ub[c]]))
                    if is_s:
                        (p.dma("sp", o_convs[:, c, :, :], us[:, c, :, 8:11], key="osm",
                                             reads=[r_ub[c]]))
                    elif not last_p:
                        p.op("dve", lambda e: e.tensor_copy(out=ubuf[:, c, 0:3], in_=ubuf[:, c, nb:nb + 3]),
                             writes=[r_ub[c]])
                    p.op("act", lambda e: e.activation(out=xcb[:, 0:nb], in_=xc[:, 0:nb], func=AF.Copy),
                         reads=[r_xc], writes=[r_xcb])
                    p.op("pe", lambda e: e.matmul(bank[2][:, 0:nb], wa_sb[:, c, :], xcb[:, 0:nb],
                                                  start=True, stop=True),
                         reads=[r_wax, r_xcb], writes=[bres[2]])
                    p.op("pe", lambda e: e.matmul(bank[3][:, 0:nb], wx_sb[:, c, :], xcb[:, 0:nb],
                                                  start=True, stop=True),
                         reads=[r_wax, r_xcb], writes=[bres[3]])
                    p.op("act", lambda e: e.activation(out=tr[:, 0:nb], in_=bank[2][:, 0:nb], func=AF.Tanh,
                                                       scale=0.5, bias=cst[:, c, 4:5]),
                         reads=[bres[2], r_cst], writes=[r_tr])
                    p.op("act", lambda e: e.activation(out=ti[:, 0:nb], in_=bank[3][:, 0:nb], func=AF.Tanh,
                                                       scale=0.5, bias=cst[:, c, 5:6]),
                         reads=[bres[3], r_cst], writes=[r_ti])
                    p.op("act", lambda e: e.activation(out=av[:, 0:nb], in_=tr[:, 0:nb], func=AF.Exp,
                                                       scale=cst[:, c, 3:4], bias=cst[:, c, 3:4]),
                         reads=[r_tr, r_cst], writes=[r_av])
                    p.op("act", lambda e: e.activation(out=a2[:, 0:nb], in_=tr[:, 0:nb], func=AF.Exp,
                                                       scale=cst[:, c, 2:3], bias=cst[:, c, 2:3]),
                         reads=[r_tr, r_cst], writes=[r_a2])
                    p.op("act", lambda e: e.activation(out=sv[:, 0:nb], in_=a2[:, 0:nb], func=AF.Sqrt,
                                                       scale=-1.0, bias=1.0),
                         reads=[r_a2], writes=[r_sv])
                    p.op("dve", lambda e: e.scalar_tensor_tensor(
                        out=gx[:, 0:nb], in0=ti[:, 0:nb], scalar=1.0, in1=xc[:, 0:nb], op0=ALU.add, op1=ALU.mult),
                        reads=[r_ti, r_xc], writes=[r_gx])
                    p.op("dve", lambda e: e.scalar_tensor_tensor(
                        out=gx[:, 0:nb], in0=gx[:, 0:nb], scalar=0.5, in1=sv[:, 0:nb], op0=ALU.mult, op1=ALU.mult),
                        reads=[r_sv], writes=[r_gx])
                    if not is_s:
                        p.op("dve", lambda e: e.tensor_tensor_scan(
                            out=hs[:, c, 0:nb], data0=av[:, 0:nb], data1=gx[:, 0:nb], initial=hprev[:, c:c + 1],
                            op0=ALU.mult, op1=ALU.add),
                            reads=[r_av, r_gx, r_hp[c]], writes=[r_hs[c]])
                        p.op("dve", lambda e: e.tensor_copy(out=hprev[:, c:c + 1], in_=hs[:, c, nb - 1:nb]),
                             reads=[r_hs[c]], writes=[r_hp[c]])
                        if last_p and c == 3:
                            (p.dma("sp", o_hp, hprev, key="osm", reads=r_hp))
                    else:
                        a3 = av[:, 0:128].rearrange("p (s t) -> p s t", t=8)
                        g3 = gx[:, 0:128].rearrange("p (s t) -> p s t", t=8)
                        p.op("dve", lambda e: e.tensor_tensor(out=t16, in0=a3[:, :, 0], in1=h0T[:, c, :],
                                                              op=ALU.mult), reads=[r_av, r_c], writes=[r_t16])
                        p.op("dve", lambda e: e.tensor_tensor(out=g3[:, :, 0], in0=g3[:, :, 0], in1=t16,
                                                              op=ALU.add), reads=[r_t16], writes=[r_gx])
                        p.op("dve", lambda e: e.memset(a3[:, :, 0], 0.0), writes=[r_av])
                        p.op("dve", lambda e: e.tensor_tensor_scan(
                            out=hs[:, c, 0:128], data0=av[:, 0:128], data1=gx[:, 0:128], initial=0.0,
                            op0=ALU.mult, op1=ALU.add), reads=[r_av, r_gx], writes=[r_hs[c]])
                        p.op("dve", lambda e: e.tensor_copy(
                            out=hsl[:, c, :], in_=hs[:, c, 0:128].rearrange("p (s t) -> p s t", t=8)[:, :, 7]),
                            reads=[r_hs[c]], writes=[r_hsl])
                        if c == 3:
                            (p.dma("sp", o_hs, hsl, key="osm", reads=[r_hsl]))
                    sb_ = sqb[c % 2]
                    p.op("act", lambda e: e.activation(out=sb_[:, 0:nb], in_=hs[:, c, 0:nb], func=AF.Square),
                         reads=[r_hs[c]], writes=[r_sqb[c % 2]])
                    p.op("pe", lambda e: e.matmul(bank[4][:, 0:nb], ones, sb_[:, 0:nb], start=(c == 0),
                                                  stop=(c == 3)),
                         reads=[r_sqb[c % 2], r_c], writes=[bres[4]])
                p.op("act", lambda e: e.activation(out=rl[:, 0:nb], in_=bank[4][:, 0:nb], func=AF.Sqrt,
                                                   scale=1.0 / 512, bias=EPS),
                     reads=[bres[4]], writes=[r_rl])
                p.op("dve", lambda e: e.reciprocal(out=rl[:, 0:nb], in_=rl[:, 0:nb]), writes=[r_rl])
                for c in range(4):
                    lo, r_lo_ = TS[c % 2]["ta"], TS[c % 2]["r_ta"]
                    p.op("dve", lambda e: e.scalar_tensor_tensor(
                        out=lo[:, 0:nb], in0=hs[:, c, 0:nb], scalar=lv[:, c, 8:9], in1=rl[:, 0:nb],
                        op0=ALU.mult, op1=ALU.mult), reads=[r_hs[c], r_rl, r_c], writes=[r_lo_])
                    p.op("dve", lambda e: e.scalar_tensor_tensor(
                        out=mixL[:, c, 0:nb], in0=lo[:, 0:nb], scalar=0.5, in1=gg[:, c, 0:nb],
                        op0=ALU.mult, op1=ALU.mult), reads=[r_lo_, r_gg[c]], writes=[r_mixL])
                for tt in range(nt):
                    gi = t0 // 128 + tt
                    for kk in range(8):
                        for hc in range(2):
                            if kk < 4:
                                lhs = retT[:, kk, gi * 128:(gi + 1) * 128]
                                rr = r_retT[gi]
                            else:
                                lhs = mixL[:, kk - 4, tt * 128:(tt + 1) * 128]
                                rr = r_mixL
                            p.op("pe", lambda e: e.matmul(bank[6 + hc], lhs, wo_sb[:, kk, hc * 512:(hc + 1) * 512],
                                                          start=(kk == 0), stop=(kk == 7)),
                                 reads=[rr, r_wo], writes=[bres[6 + hc]])
                    p.op("dve", lambda e: e.tensor_tensor(out=xres[:, gi, :], in0=bank2(6), in1=xres[:, gi, :],
                                                          op=ALU.add),
                         reads=[bres[6], bres[7]], writes=[xr[gi]])
            A.release(m0)
            p.barrier()

        def transpose_2x1024(src, r_src, dst, r_dst, tb):
            for rnd in range(2):
                for cc in range(4):
                    c = rnd * 4 + cc
                    for t in range(2):
                        p.op("pe", lambda e: e.transpose(
                            bank_bf[tb][:, cc * 256 + t * 128:cc * 256 + (t + 1) * 128],
                            src[:, t, c * 128:(c + 1) * 128], ident),
                            reads=[r_src, r_ident], writes=[bres[tb]])
                p.op("act", lambda e: e.activation(
                    out=dst[:, rnd * 4:(rnd + 1) * 4, :],
                    in_=bank_bf[tb].rearrange("p (c m) -> p c m", c=4), func=AF.Copy),
                    reads=[bres[tb]], writes=[r_dst])

        def xattn_kv(KTp, r_KTp, Vp, r_Vp):
            m0 = A.mark()
            wk_sb = A.alloc([8, 1024], BF16)
            wv_sb = A.alloc([8, 1024], BF16)
            r_wk, r_wv = Res(), Res()
            p.dma("pool", wk_sb, wk.rearrange("(k p) n -> p k n", p=128), key="wA", writes=[r_wk])
            p.dma("pool", wv_sb, wv.rearrange("(k p) n -> p k n", p=128), key="wB", writes=[r_wv])
            memb = A.alloc([2, 1024], BF16)
            r_memb = Res()
            p.dma("pool", memb, memp.rearrange("(t p) d -> p t d", p=128), key="wC", writes=[r_memb])
            memT = A.alloc([8, 256], BF16)
            r_memT = Res()
            kbp = A.alloc([2, 1024], BF16)
            r_kbp = Res()
            kf = [A.alloc([1024], F32) for _ in range(2)]
            r_kf = [Res(), Res()]
            transpose_2x1024(memb, r_memb, memT, r_memT, 5)
            n = 0
            for (wsb, r_w, dstb, r_dstb, oap, tag) in ((wk_sb, r_wk, kbp, r_kbp, o_mk, "k"),
                                                       (wv_sb, r_wv, Vp, r_Vp, o_mv, "v")):
                ov = oap.rearrange("(t p) d -> p t d", p=128)
                for t in range(2):
                    f = kf[n % 2]
                    r_f = r_kf[n % 2]
                    n += 1
                    for hc in range(2):
                        for k in range(8):
                            p.op("pe", lambda e: e.matmul(
                                bank[hc], memT[:, k, t * 128:(t + 1) * 128], wsb[:, k, hc * 512:(hc + 1) * 512],
                                start=(k == 0), stop=(k == 7)), reads=[r_memT, r_w], writes=[bres[hc]])
                    p.op("act", lambda e: e.activation(out=f, in_=bank2(0), func=AF.Copy),
                         reads=[bres[0], bres[1]], writes=[r_f])
                    p.op("dve", lambda e: e.tensor_copy(out=dstb[:, t, :], in_=f),
                         reads=[r_f], writes=[r_dstb])
                    (p.dma("sp", ov[:, t, :], f, key="okv%d" % (n % 2), reads=[r_f]))
            transpose_2x1024(kbp, r_kbp, KTp, r_KTp, 5)
            A.release(m0)
            p.barrier()

        def xattn_main(KTp, r_KTp, Vp, r_Vp):
            m0 = A.mark()
            wq_sb = A.alloc([8, 1024], BF16)
            wo_sb = A.alloc([8, 1024], BF16)
            r_wq, r_wo = Res(), Res()
            p.dma("pool", wq_sb, wq.rearrange("(k p) n -> p k n", p=128), key="wA", writes=[r_wq])
            p.dma("pool", wo_sb, wo.rearrange("(k p) n -> p k n", p=128), key="wB", writes=[r_wo])
            load_gain("xattn")
            bmask = A.alloc([16, 128], BF16)
            r_c = Res()
            p.dma("sp", bmask, c_bmask, key="c0", writes=[r_c])
            SC = 1.0 / 16.0

            def mkset():
                d = dict(Pm=A.alloc([4, 256], BF16), PT=A.alloc([8, 128], BF16), attn=A.alloc([1024], BF16),
                         attnT=A.alloc([8, 128], BF16), sst=A.alloc([4, 4], F32))
                for k in list(d):
                    d["r_" + k] = Res()
                return d

            def softmax(B, sc_all, sc_h, sc_res, tb=5):
                sst, Pm, PT = B["sst"], B["Pm"], B["PT"]
                p.op("dve", lambda e: e.tensor_reduce(out=sst[:, 0, :], in_=sc_all, axis=AX.X, op=ALU.max),
                     reads=list(set(sc_res)), writes=[B["r_sst"]])
                p.op("dve", lambda e: e.tensor_scalar(out=sst[:, 1, :], in0=sst[:, 0, :], scalar1=-SC, scalar2=None,
                                                      op0=ALU.mult), writes=[B["r_sst"]])
                for h in range(4):
                    p.op("act", lambda e: e.activation(out=Pm[:, h, :], in_=sc_h[h], func=AF.Exp, scale=SC,
                                                       bias=sst[:, 1, h:h + 1], accum_out=sst[:, 2, h:h + 1]),
                         reads=[sc_res[h], B["r_sst"]], writes=[B["r_Pm"], B["r_sst"]])
                p.op("dve", lambda e: e.reciprocal(out=sst[:, 3, :], in_=sst[:, 2, :]), writes=[B["r_sst"]])
                for h in range(4):
                    for mc in range(2):
                        j = h * 2 + mc
                        p.op("pe", lambda e: e.transpose(bank_bf[tb][:, j * 128:(j + 1) * 128],
                                                         Pm[:, h, mc * 128:(mc + 1) * 128], ident),
                             reads=[B["r_Pm"], r_ident], writes=[bres[tb]])
                p.op("act", lambda e: e.activation(out=PT, in_=bank_bf[tb].rearrange("p (j n) -> p j n", j=8),
                                                   func=AF.Copy), reads=[bres[tb]], writes=[B["r_PT"]])

            def out_proj(B, gi, pv_all, pv_res, yb, tb=5):
                sst, attn, attnT = B["sst"], B["attn"], B["attnT"]
                p.op("dve", lambda e: e.tensor_tensor(
                    out=attn.rearrange("p (h d) -> p h d", h=4), in0=pv_all,
                    in1=sst[:, 3, :].unsqueeze(2).to_broadcast([128, 4, 256]), op=ALU.mult),
                    reads=list(set(pv_res)) + [B["r_sst"]], writes=[B["r_attn"]])
                for k in range(8):
                    p.op("pe", lambda e: e.transpose(bank_bf[tb][:, k * 128:(k + 1) * 128],
                                                     attn[:, k * 128:(k + 1) * 128], ident),
                         reads=[B["r_attn"], r_ident], writes=[bres[tb]])
                p.op("act", lambda e: e.activation(out=attnT, in_=bank_bf[tb].rearrange("p (j n) -> p j n", j=8),
                                                   func=AF.Copy), reads=[bres[tb]], writes=[B["r_attnT"]])
                for k in range(8):
                    for hc in range(2):
                        p.op("pe", lambda e: e.matmul(bank[yb + hc], attnT[:, k, :],
                                                      wo_sb[:, k, hc * 512:(hc + 1) * 512],
                                                      start=(k == 0), stop=(k == 7)),
                             reads=[B["r_attnT"], r_wo], writes=[bres[yb + hc]])
                p.op("dve", lambda e: e.tensor_tensor(out=xres[:, gi, :], in0=bank2(yb), in1=xres[:, gi, :],
                                                      op=ALU.add),
                     reads=[bres[yb], bres[yb + 1]], writes=[xr[gi]])

            def q_proj(nb, xnT_, r_xnT_, qT_, r_qT_, qb_=5):
                for c in range(8):
                    for k in range(8):
                        p.op("pe", lambda e: e.matmul(bank[qb_][:, 0:nb], wq_sb[:, k, c * 128:(c + 1) * 128],
                                                      xnT_[:, k, 0:nb], start=(k == 0), stop=(k == 7)),
                             reads=[r_wq, r_xnT_], writes=[bres[qb_]])
                    p.op("act", lambda e: e.activation(out=qT_[:, c, 0:nb], in_=bank[qb_][:, 0:nb], func=AF.Copy),
                         reads=[bres[qb_]], writes=[r_qT_])

            def pair(b0):
                v = psum[:, b0 * 512:(b0 + 2) * 512]
                return dict(all=v.rearrange("p (h m) -> p h m", h=4),
                            h=[v[:, h * 256:(h + 1) * 256] for h in range(4)],
                            res=[bres[b0], bres[b0], bres[b0 + 1], bres[b0 + 1]], flat=v)
            pairs = [pair(0), pair(2)]

            xnTb = [A.alloc([8, 512], BF16)] * 2
            r_xnTb = [Res()] * 2
            qTb = [A.alloc([8, 512], BF16) for _ in range(2)]
            r_qTb = [Res(), Res()]
            sets = [mkset(), mkset()]

            def prompt_stream():
                for bi, (t0, nb) in enumerate(BLOCKS[:4]):
                    xb, rxb, qb, rqb = xnTb[bi % 2], r_xnTb[bi % 2], qTb[bi % 2], r_qTb[bi % 2]
                    norm_block(t0, nb, xb, rxb)
                    q_proj(nb, xb, rxb, qb, rqb)
                    yield
                    for tt in range(nb // 128):
                        gi = t0 // 128 + tt
                        B = sets[gi % 2]
                        PP = pairs[gi % 2]
                        tl = slice(tt * 128, (tt + 1) * 128)
                        for h in range(4):
                            for kk in range(2):
                                p.op("pe", lambda e: e.matmul(PP["h"][h], qb[:, 2 * h + kk, tl], KTp[:, 2 * h + kk, :],
                                                              start=(kk == 0), stop=(kk == 1)),
                                     reads=[rqb, r_KTp], writes=[PP["res"][h]])
                        softmax(B, PP["all"], PP["h"], PP["res"])
                        for h in range(4):
                            for mc in range(2):
                                p.op("pe", lambda e: e.matmul(PP["h"][h], B["PT"][:, h * 2 + mc, :],
                                                              Vp[:, mc, h * 256:(h + 1) * 256],
                                                              start=(mc == 0), stop=(mc == 1)),
                                     reads=[B["r_PT"], r_Vp], writes=[PP["res"][h]])
                        out_proj(B, gi, PP["all"], PP["res"], 6)
                        yield

            def sample_stream():
                xs = A.alloc([8, 128], BF16)
                qs = A.alloc([8, 128], BF16)
                r_xs, r_qs = Res(), Res()
                B = mkset()
                msk = [A.alloc([8, 2, 128], BF16) for _ in range(2)]
                r_msk = [Res(), Res()]
                kb = [A.alloc([2, 1024], BF16) for _ in range(2)]
                r_kb = [Res(), Res()]
                KTs = [A.alloc([8, 256], BF16) for _ in range(2)]
                r_KTs = [Res(), Res()]
                acc = A.alloc([1024], F32)
                r_acc = Res()
                norm_tile(16, xs, r_xs, 0, 4)
                q_proj(128, xs, r_xs, qs, r_qs, 4)
                p.op("dve", lambda e: e.memset(acc, 0.0), writes=[r_acc])
                yield
                for s_ in range(16):
                    b2 = s_ % 2
                    g = s_ // 2
                    g2 = g % 2
                    if s_ % 2 == 0:
                        for c in range(8):
                            p.op("pool", lambda e: e.tensor_tensor(
                                out=msk[g2][:, c, :, :],
                                in0=qs[:, c, :].unsqueeze(1).to_broadcast([128, 2, 128]),
                                in1=bmask[:, 2 * g:2 * g + 2, :], op=ALU.mult),
                                reads=[r_qs, r_c], writes=[r_msk[g2]])
                    p.dma("pool", kb[b2], ck[s_].rearrange("(t p) d -> p t d", p=128), key="kv%d" % b2,
                          writes=[r_kb[b2]])
                    transpose_2x1024(kb[b2], r_kb[b2], KTs[b2], r_KTs[b2], 4)
                    for hp in range(2):
                        for hh in range(2):
                            h = hp * 2 + hh
                            for kk in range(2):
                                p.op("pe", lambda e: e.matmul(
                                    bank[4][:, hh * 256:(hh + 1) * 256], msk[g2][:, 2 * h + kk, s_ % 2, :],
                                    KTs[b2][:, 2 * h + kk, :], start=(kk == 0), stop=(kk == 1)),
                                    reads=[r_msk[g2], r_KTs[b2]], writes=[bres[4]])
                        p.op("dve", lambda e: e.tensor_tensor(out=acc[:, hp * 512:(hp + 1) * 512], in0=bank[4],
                                                              in1=acc[:, hp * 512:(hp + 1) * 512], op=ALU.add),
                             reads=[bres[4]], writes=[r_acc])
                    yield
                acc_h = [acc[:, h * 256:(h + 1) * 256] for h in range(4)]
                softmax(B, acc.rearrange("p (h m) -> p h m", h=4), acc_h, [r_acc] * 4, 4)
                acc2 = acc
                r_acc2 = r_acc
                p.op("dve", lambda e: e.memset(acc2, 0.0), writes=[r_acc2])
                yield
                for s_ in range(16):
                    b2 = s_ % 2
                    g = s_ // 2
                    g2 = g % 2
                    if s_ % 2 == 0:
                        for j in range(8):
                            p.op("pool", lambda e: e.tensor_tensor(
                                out=msk[g2][:, j, :, :],
                                in0=B["PT"][:, j, :].unsqueeze(1).to_broadcast([128, 2, 128]),
                                in1=bmask[:, 2 * g:2 * g + 2, :], op=ALU.mult),
                                reads=[B["r_PT"], r_c], writes=[r_msk[g2]])
                    p.dma("pool", kb[b2], cv[s_].rearrange("(t p) d -> p t d", p=128), key="kv%d" % b2,
                          writes=[r_kb[b2]])
                    for hp in range(2):
                        for hh in range(2):
                            h = hp * 2 + hh
                            for mc in range(2):
                                p.op("pe", lambda e: e.matmul(
                                    bank[4][:, hh * 256:(hh + 1) * 256], msk[g2][:, h * 2 + mc, s_ % 2, :],
                                    kb[b2][:, mc, h * 256:(h + 1) * 256], start=(mc == 0), stop=(mc == 1)),
                                    reads=[r_msk[g2], r_kb[b2]], writes=[bres[4]])
                        p.op("dve", lambda e: e.tensor_tensor(out=acc2[:, hp * 512:(hp + 1) * 512], in0=bank[4],
                                                              in1=acc2[:, hp * 512:(hp + 1) * 512], op=ALU.add),
                             reads=[bres[4]], writes=[r_acc2])
                    yield
                out_proj(B, 16, acc2.rearrange("p (h m) -> p h m", h=4), [r_acc2] * 4, 6, 4)
                yield

            interleave(prompt_stream(), sample_stream())
            A.release(m0)
            p.barrier()

        def dump_ret(retT, r_retT):
            pass

        def final_out():
            load_gain("final")
            m0 = A.mark()
            ybuf = [A.alloc([D], F32) for _ in range(4)]
            r_yb = [Res() for _ in range(4)]
            for i in range(NT):
                nb = i % 4
                p.op("act", lambda e, i=i: e.activation(out=junk, in_=xres[:, i, :], func=AF.Square,
                                                        accum_out=ssq[:, i:i + 1]),
                     reads=[xr[i]], writes=[r_junk, r_ssq[i]])
                p.op("dve", lambda e, i=i: e.tensor_scalar(out=tmp17[:, i:i + 1], in0=ssq[:, i:i + 1],
                                                           scalar1=1.0 / D, scalar2=EPS, op0=ALU.mult, op1=ALU.add),
                     writes=[r_ssq[i]])
                p.op("pool", lambda e, i=i: e.tensor_tensor(out=rstd[:, i:i + 1], in0=tmp17[:, i:i + 1],
                                                            in1=cm05[:, 0:1], op=ALU.pow),
                     reads=[r_cm], writes=[r_ssq[i]])
                p.op("dve", lambda e, i=i, nb=nb: e.scalar_tensor_tensor(
                    out=ybuf[nb], in0=xres[:, i, :], scalar=rstd[:, i:i + 1], in1=gbc, op0=ALU.mult, op1=ALU.mult),
                    reads=[xr[i], r_gbc, r_ssq[i]], writes=[r_yb[nb]])
                (p.dma("sp", yv_out[:, i, :], ybuf[nb], key="yo%d" % (i % 4), reads=[r_yb[nb]]))
            A.release(m0)
            p.barrier()

        def dump_x():
            for i in range(NT):
                (p.dma("sp", yv_out[:, i, :], xres[:, i, :], key="yo%d" % (i % 2), reads=[xr[i]]))

        ffn("ffn1")
        if stop_after == "ffn1":
            dump_x()
        else:
            retT = A.alloc([4, TOK], BF16)
            r_retT = [Res() for _ in range(NT)]
            mix_ret(retT, r_retT)
            if stop_after == "ret_dbg":
                dump_x()
            elif stop_after == "ret":
                dbg = dout("dbg_retT", [128, 4, TOK], BF16)
                (p.dma("sp", dbg, retT, key="dbg", reads=r_retT))
                dump_x()
            else:
                mix_lru(retT, r_retT)
                A.release(base_mark)
                if stop_after == "mix":
                    dump_x()
                else:
                    KTp = A.alloc([8, 256], BF16)
                    Vp = A.alloc([2, 1024], BF16)
                    r_KTp, r_Vp = Res(), Res()
                    xattn_kv(KTp, r_KTp, Vp, r_Vp)
                    if stop_after != "xattn_kv":
                        xattn_main(KTp, r_KTp, Vp, r_Vp)
                    A.release(base_mark)
                    if stop_after in ("xattn", "xattn_kv", "xattn_p", "xattn_s0", "xattn_s1", "xattn_s2"):
                        dump_x()
                    else:
                        ffn("ffn2")
                        if stop_after == "ffn2":
                            dump_x()
                        else:
                            final_out()
        p.emit()
    return nc


def _consts():
    c = {}
    c["c_ident"] = np.eye(128, dtype=np.float32).astype(ml_dtypes.bfloat16)
    c["c_ones"] = np.ones((128, 128), np.float32)
    half = 64
    inv = (10000.0 ** (-np.arange(half, dtype=np.float32) / half)).astype(np.float32)
    pos = np.concatenate([np.arange(2048), 16384 + (np.arange(128) % 8)]).astype(np.float32)
    ang = (pos[None, :] * inv[:, None]).astype(np.float32)
    cos = np.cos(ang.astype(np.float64)).astype(np.float32)
    sin = np.sin(ang.astype(np.float64)).astype(np.float32)
    c["c_cos"] = np.concatenate([cos, cos], 0)
    c["c_sin"] = np.concatenate([-sin, sin], 0)
    lg = np.log(1.0 - 2.0 ** (-5.0 - np.arange(4, dtype=np.float64)))
    scale = 128.0 ** -0.5
    idx = np.arange(128)
    dm = np.zeros((2, 128, 4, 128), np.float64)
    qd = np.zeros((2, 128, 4, 128), np.float64)
    kd = np.zeros((2, 128, 4), np.float64)
    diff = idx[None, :] - idx[:, None]
    s_id = idx // 8
    t_id = idx % 8
    same = (s_id[:, None] == s_id[None, :])
    dts = t_id[None, :] - t_id[:, None]
    for h in range(4):
        dm[0, :, h, :] = np.where(diff >= 0, np.exp(lg[h] * np.maximum(diff, 0)), 0.0) * scale
        dm[1, :, h, :] = np.where(same & (dts >= 0), np.exp(lg[h] * np.maximum(dts, 0)), 0.0) * scale
        qd[0, :, h, :] = np.exp(lg[h] * (idx + 1.0))[None, :]
        qd[1, :, h, :] = np.exp(lg[h] * (t_id + 1.0))[None, :]
        kd[0, :, h] = np.exp(lg[h] * (127.0 - idx)) * scale
        kd[1, :, h] = np.exp(lg[h] * (7.0 - t_id)) * scale
    c["c_dmask"] = dm.astype(np.float32)
    c["c_qdec"] = qd.astype(np.float32)
    c["c_kdec"] = kd.astype(np.float32)
    bm = (s_id[None, :] == np.arange(16)[:, None]).astype(np.float32)
    c["c_bmask"] = np.broadcast_to(bm[None], (128, 16, 128)).astype(ml_dtypes.bfloat16)
    c["c_rmask"] = np.ascontiguousarray(bm.T)
    return c


_NC_CACHE = {}


def _prep_inputs(inp, stop_after=None):
    f = lambda a: np.ascontiguousarray(np.asarray(a, dtype=np.float32))
    shared = {}
    for n in ("ffn1", "ffn2"):
        shared[n + "_wg"] = f(inp[n + "_wg"][0])
        shared[n + "_wu"] = f(inp[n + "_wu"][0])
        shared[n + "_wd"] = f(inp[n + "_wd"][0])
    shared["w_in"] = f(inp["w_in"][0])
    shared["w_out"] = f(inp["w_out"][0])
    shared["wq"] = f(inp["xattn_wq"][0])
    shared["wk"] = f(inp["xattn_wk"][0])
    shared["wv"] = f(inp["xattn_wv"][0])
    shared["wo"] = f(inp["xattn_wo"][0])
    bc = lambda v, n: np.ascontiguousarray(np.broadcast_to(f(v).reshape(1, n), (128, n)))
    shared["g_ffn1"] = bc(inp["ffn1_norm"][0], D)
    shared["g_mix"] = bc(inp["mix_norm"][0], D)
    shared["g_xattn"] = bc(inp["xattn_norm"][0], D)
    shared["g_ffn2"] = bc(inp["ffn2_norm"][0], D)
    shared["g_final"] = bc(inp["final_norm"], D)
    shared["gn_gain"] = bc(inp["ret_gn_gain"][0], 512)
    fm = lambda v: f(v).reshape(4, 128).T
    lv = np.zeros((128, 4, 9), np.float32)
    cw = f(inp["conv_w"][0])
    for j in range(4):
        lv[:, :, j] = fm(cw[j])
    lv[:, :, 4] = fm(inp["conv_b"][0])
    lv[:, :, 5] = fm(inp["lru_ba"][0])
    lv[:, :, 6] = fm(inp["lru_bx"][0])
    lv[:, :, 7] = fm(inp["lru_lambda"][0])
    lv[:, :, 8] = fm(inp["lru_norm"][0])
    shared["lruvec"] = lv

    def bd(w):
        w = f(w)
        o = np.zeros((128, 4, 128), np.float32)
        for c in range(4):
            for b in range(2):
                o[b * 64:(b + 1) * 64, c, b * 64:(b + 1) * 64] = w[2 * c + b]
        return o
    shared["wabd"] = bd(inp["lru_wa"][0])
    shared["wxbd"] = bd(inp["lru_wx"][0])
    shared.update(_consts())
    maps = []
    for c in range(NCORES):
        m = dict(shared)
        xs = f(inp["x_sample"][16 * c:16 * c + 16]).reshape(128, D)
        m["x"] = np.ascontiguousarray(np.concatenate([f(inp["x_prompt"][c]), xs], 0))
        m["st_ret"] = f(inp["state_ret"][0, 16 * c:16 * c + 16])
        h0 = f(inp["state_lru_h"][0, 16 * c:16 * c + 16])
        m["st_h"] = np.ascontiguousarray(h0.reshape(16, 4, 128).transpose(2, 1, 0))
        cv0 = f(inp["state_lru_conv"][0, 16 * c:16 * c + 16])
        m["st_conv"] = np.ascontiguousarray(cv0.reshape(16, 3, 4, 128).transpose(3, 2, 0, 1))
        m["ck"] = f(inp["cache_mem_k"][0, 16 * c:16 * c + 16]).reshape(16, 256, D)
        m["cv"] = f(inp["cache_mem_v"][0, 16 * c:16 * c + 16]).reshape(16, 256, D)
        m["memp"] = f(inp["mem_prompt"][c])
        maps.append(m)
    return maps


def _run(inp, stop_after=None, cores=None):
    key = stop_after
    if key not in _NC_CACHE:
        _NC_CACHE[key] = build_program(stop_after)
    nc = _NC_CACHE[key]
    maps = _prep_inputs(inp)
    if cores is not None:
        maps = [maps[c] for c in cores]
    res = run_bass_kernel_spmd(nc, maps, core_ids=list(range(len(maps))))
    return res.results


def kernel(**inp):
    rs = _run(inp)
    g = lambda n: [np.asarray(r[n], dtype=np.float32) for r in rs]
    y = g("y")
    y_prompt = np.stack([a[0:2048] for a in y], 0)
    y_sample = np.concatenate([a[2048:].reshape(16, 8, D) for a in y], 0)
    new_ret_p = np.stack(g("o_retp"), 0)[None]
    new_h_p = np.stack([a.T.reshape(512) for a in g("o_hp")], 0)[None]
    new_conv_p = np.stack([a.transpose(2, 1, 0).reshape(3, 512) for a in g("o_convp")], 0)[None]
    mk = np.stack([a.reshape(256, 4, 256) for a in g("o_mk")], 0)[None]
    mv = np.stack([a.reshape(256, 4, 256) for a in g("o_mv")], 0)[None]
    new_ret_s = np.concatenate(g("o_rets"), 0)[None]
    new_h_s = np.concatenate([a.transpose(2, 1, 0).reshape(16, 512) for a in g("o_hs")], 0)[None]
    new_conv_s = np.concatenate([a.transpose(2, 3, 1, 0).reshape(16, 3, 512) for a in g("o_convs")], 0)[None]
    return (y_prompt, y_sample, new_ret_p, new_h_p, new_conv_p, mk, mv, new_ret_s, new_h_s, new_conv_s)
```

```python
import contextlib
import math
import numpy as np
import ml_dtypes
import concourse.bass as bass
import concourse.mybir as mybir
from concourse.bass_utils import run_bass_kernel_spmd

F32 = mybir.dt.float32
BF16 = mybir.dt.bfloat16
ALU = mybir.AluOpType
AF = mybir.ActivationFunctionType
AX = mybir.AxisListType

NCORES = 8
D = 1024
NT = 17
TOK = NT * 128
FF = 2816
NFC = FF // 128
EPS = 1e-6
BLOCKS = [(0, 512), (512, 512), (1024, 512), (1536, 512), (2048, 128)]
FF_PARTS = [(0, 8), (8, 16), (16, 22)]
ENGS = ("pe", "act", "dve", "pool", "sp")
AUTO_TRACK = True
PE_FILL = True
STRICT_WAR = False


class Op:
    __slots__ = ("idx", "kind", "eng", "call", "deps", "key", "val", "needed", "seq", "dur", "seg", "pos",
                 "nbytes", "waits", "cp", "tab", "fill", "waits_pre")

    def __init__(self, kind, eng):
        self.kind = kind
        self.eng = eng
        self.fill = 0
        self.waits_pre = []
        self.deps = []
        self.needed = False
        self.seq = None
        self.key = None
        self.val = 0
        self.pos = -1


class Res:
    __slots__ = ("name", "w", "r", "excl")

    def __init__(self, name="", excl=False):
        self.name = name
        self.w = None
        self.r = []
        self.excl = excl


class _Rec:
    def __init__(self):
        self.call = None

    def __getattr__(self, name):
        def f(*a, **kw):
            self.call = (name, a, kw)
            return self
        return f


def _free_elems(ap):
    n = 1
    for d in ap.shape[1:]:
        n *= int(d)
    return n


def _est_dur(eng, call):
    name, a, kw = call
    if name == "matmul":
        rhs = a[2] if len(a) > 2 else kw["rhs"]
        return 0.005 + _free_elems(rhs) / 2420.0
    if name == "transpose":
        return 0.06
    out = kw.get("out", a[0] if a else None)
    n = _free_elems(out) if out is not None else 64
    if eng == "act":
        return 0.12 + n * 0.00100
    if eng == "dve":
        if name == "tensor_tensor_scan":
            return 0.15 + n * 0.0022
        if name == "reciprocal":
            return 0.15 + n * 0.0063
        return 0.15 + n * 0.00105
    if name == "tensor_scalar":
        return 0.35 + n * 0.0008
    if name == "tensor_tensor" and kw.get("op") == ALU.pow:
        return 0.5 + n * 0.17
    return 0.2 + n * 0.0021


class Prog:
    def __init__(self, nc):
        self.nc = nc
        self.all = []
        self.dma_cnt = {}
        self.dma_last = {}
        self.seg = 0
        self.pages = {}
        self.filler_ap = None

    def barrier(self):
        if not AUTO_TRACK:
            self.seg += 1

    PAGE = 1024

    def _ap_range(self, ap):
        try:
            if ap.tensor.name != "arena":
                return None
        except Exception:
            return None
        sz = 4 if ap.dtype == F32 else 2
        dims = list(ap.ap)
        pstride = dims[0][0]
        lo = ap.offset % pstride if pstride else ap.offset
        ext = 1
        for st_, cnt in dims[1:]:
            ext += abs(st_) * (cnt - 1)
        return (lo * sz, (lo + ext) * sz)

    def _track(self, op, call):
        if not AUTO_TRACK:
            return
        if call[0] == "dma":
            accs = [(call[1], True), (call[2], False)]
        else:
            name, a, kw = call
            accs = []
            items = list(kw.items()) + [("arg%d" % i, v) for i, v in enumerate(a)]
            for k, v in items:
                if not hasattr(v, "ap") or not hasattr(v, "tensor"):
                    continue
                is_out = k in ("out", "accum_out", "arg0")
                accs.append((v, is_out))
        seen = set(id(d) for d, _ in op.deps)
        pages = self.pages
        for ap, is_w in accs:
            r = self._ap_range(ap)
            if r is None:
                continue
            lo, hi = r
            for pg in range(lo // self.PAGE, (hi - 1) // self.PAGE + 1):
                lst = pages.setdefault(pg, [])
                keep = []
                for rec in lst:
                    rlo, rhi, rop, rw = rec
                    if rlo < hi and lo < rhi and rop is not op:
                        if is_w or rw:
                            if id(rop) not in seen:
                                seen.add(id(rop))
                                op.deps.append((rop, not rw))
                        if is_w and lo <= max(rlo, pg * self.PAGE) and min(rhi, (pg + 1) * self.PAGE) <= hi:
                            continue
                    keep.append(rec)
                keep.append((lo, hi, op, is_w))
                pages[pg] = keep

    def _edges(self, op, reads, writes, deps):
        for d in deps:
            if d is not None:
                op.deps.append((d, False))
        for R in reads:
            if R.w is not None:
                op.deps.append((R.w, False))
            if R.excl:
                for d in R.r:
                    if d.eng != op.eng:
                        op.deps.append((d, True))
        for R in writes:
            if R.w is not None:
                op.deps.append((R.w, False))
            for d in R.r:
                op.deps.append((d, True))
        for R in reads:
            R.r.append(op)
        for R in writes:
            R.w = op
            R.r = []

    def op(self, eng, fn, reads=(), writes=(), deps=()):
        o = Op("eng", eng)
        rec = _Rec()
        fn(rec)
        o.call = rec.call
        o.dur = _est_dur(eng, o.call)
        o.tab = None
        if eng == "act" and o.call[0] == "activation":
            fnc = o.call[2].get("func")
            if fnc in (AF.Exp, AF.Tanh):
                o.tab = "exp"
            elif fnc == AF.Sqrt:
                o.tab = "sqrt"
            elif fnc == AF.Ln:
                o.tab = "ln"
            elif fnc == AF.Silu:
                o.tab = "silu"
        o.seg = self.seg
        o.idx = len(self.all)
        self._edges(o, reads, writes, deps)
        self._track(o, o.call)
        self.all.append(o)
        return o

    def dma(self, eng, out, in_, key, reads=(), writes=(), deps=(), **kw):
        o = Op("dma", eng)
        o.tab = None
        o.call = ("dma", out, in_, kw)
        o.key = key
        self.dma_cnt[key] = self.dma_cnt.get(key, 0) + 16
        o.val = self.dma_cnt[key]
        deps = list(deps)
        if key in self.dma_last:
            deps.append(self.dma_last[key])
        self.dma_last[key] = o
        sz = 4 if out.dtype == F32 else 2
        o.nbytes = _free_elems(out) * int(out.shape[0]) * sz
        o.dur = 1.0 if eng == "pool" else 0.15
        o.seg = self.seg
        o.idx = len(self.all)
        self._edges(o, reads, writes, deps)
        self._track(o, o.call)
        self.all.append(o)
        return o

    def schedule(self):
        LAT = 0.3
        DMA_BW = 180e3
        WINDOW = 512
        order = {e: [] for e in ENGS}
        fin = {}
        nseg = self.seg + 1
        segs = [[] for _ in range(nseg)]
        for o in self.all:
            segs[o.seg].append(o)
        succ_cp = [0.0] * len(self.all)
        for o in reversed(self.all):
            mine = succ_cp[o.idx] + (o.dur if o.kind == "eng" else 2.5)
            o.cp = mine
            for d, _w in o.deps:
                if mine > succ_cp[d.idx]:
                    succ_cp[d.idx] = mine
        tnow = 0.0
        dma_clock = 0.0
        cur_tab = [None]
        for sg in segs:
            rem = {e: [o for o in sg if o.eng == e] for e in ENGS}
            clk = {e: tnow for e in ENGS}
            dma_clock = max(dma_clock, tnow)
            nleft = len(sg)
            cand = {e: None for e in ENGS}
            dirty = set(ENGS)
            while nleft:
                for e in list(dirty):
                    best = None
                    bestkey = None
                    lst = rem[e]
                    for j in range(min(WINDOW, len(lst))):
                        o = lst[j]
                        r = clk[e]
                        ok = True
                        for d, _w in o.deps:
                            f = fin.get(d.idx)
                            if f is None:
                                ok = False
                                break
                            if d.eng != e or d.kind == "dma":
                                f += LAT
                            if f > r:
                                r = f
                        if not ok:
                            continue
                        if r <= clk[e] + 1e-9:
                            key = (0 if (e == "act" and o.tab is not None and o.tab != cur_tab[0]) else 1, o.cp)
                            if best is None or best[0] > clk[e] + 1e-9 or key > bestkey:
                                best = (r, j, o)
                                bestkey = key
                        elif best is None or r < best[0] - 1e-9:
                            best = (r, j, o)
                    cand[e] = best
                dirty.clear()
                be = None
                for e in ENGS:
                    c = cand[e]
                    if c is not None and (be is None or c[0] < cand[be][0]):
                        be = e
                assert be is not None, "scheduler deadlock"
                r, j, o = cand[be]
                rem[be].pop(j)
                if o.kind == "dma":
                    clk[be] = r + o.dur
                    st = max(r, dma_clock)
                    dma_clock = st + o.nbytes / DMA_BW
                    fin[o.idx] = dma_clock + 2.0
                else:
                    if (be == "pe" and PE_FILL and o.call[0] == "matmul" and o.call[2].get("start")
                            and len(order["pe"]) > 64):
                        r_war = clk[be]
                        for d, w_ in o.deps:
                            if w_ and d.eng != "pe":
                                r_war = max(r_war, fin[d.idx] + LAT)
                        gap = r - r_war - 0.4
                        if gap > 0.6 and _free_elems(o.call[1][0]) >= 128:
                            o.fill = min(int(0.7 * gap / 0.07), 64)
                    extra = 0.0
                    if be == "act" and o.tab is not None and o.tab != cur_tab[0]:
                        extra = 1.3
                        cur_tab[0] = o.tab
                    clk[be] = r + o.dur + extra
                    fin[o.idx] = clk[be]
                o.pos = len(order[be])
                order[be].append(o)
                nleft -= 1
                dirty = set(ENGS)
            tnow = max([tnow] + [fin[o.idx] for o in sg])
        self.est_total = tnow
        return order

    def emit(self):
        nc = self.nc
        order = self.schedule()
        last_by_seg = {}
        waits = {}
        for e in ENGS:
            prev_seg = -1
            for o in order[e]:
                extra = []
                if o.seg != prev_seg:
                    for e2 in ENGS:
                        if e2 == e:
                            continue
                        cands = [x for x in order[e2] if x.seg < o.seg and x.kind == "eng"]
                        if cands:
                            extra.append(cands[-1])
                    keys = {}
                    for x in self.all:
                        if x.kind == "dma" and x.seg < o.seg:
                            keys[x.key] = x
                    extra.extend(keys.values())
                    prev_seg = o.seg
                per_eng = {}
                per_eng_war = {}
                dmas = {}
                for d, war in [(x, False) for x in extra] + o.deps:
                    if d.kind == "dma":
                        if d.key not in dmas or dmas[d.key].val < d.val:
                            dmas[d.key] = d
                        continue
                    if d.eng == e:
                        if e == "pe":
                            continue
                        if war and not STRICT_WAR:
                            continue
                    tgt = per_eng_war if (war and o.fill) else per_eng
                    if d.eng not in tgt or tgt[d.eng].pos < d.pos:
                        tgt[d.eng] = d
                o.waits_pre = list(per_eng_war.values())
                o.waits = list(per_eng.values()) + list(dmas.values())
                for d in list(per_eng.values()) + list(per_eng_war.values()):
                    d.needed = True
        for e in ENGS:
            c = 0
            for o in order[e]:
                if o.kind == "eng" and o.needed:
                    c += 1
                    o.seq = c
        with contextlib.ExitStack() as st:
            esem = {e: st.enter_context(nc.semaphore("s_" + e)) for e in ENGS}
            dsem = {k: st.enter_context(nc.semaphore("d_%s" % k)) for k in self.dma_cnt}
            block = st.enter_context(nc.Block())

            def run(e, engobj):
                waited = {}
                for o in order[e]:
                    if o.fill:
                        for d in o.waits_pre:
                            sem, val, k = esem[d.eng], d.seq, ("e", d.eng)
                            if waited.get(k, 0) >= val:
                                continue
                            waited[k] = val
                            engobj.wait_ge(sem, val)
                        fout = o.call[1][0][:, 0:128]
                        for _ in range(o.fill):
                            engobj.matmul(fout, self.filler_ap, self.filler_ap, start=True, stop=True)
                    for d in o.waits:
                        if d.kind == "eng":
                            sem, val, k = esem[d.eng], d.seq, ("e", d.eng)
                        else:
                            sem, val, k = dsem[d.key], d.val, ("d", d.key)
                        if waited.get(k, 0) >= val:
                            continue
                        waited[k] = val
                        engobj.wait_ge(sem, val)
                    fn = o.call
                    if fn[0] == "dma":
                        _, out, in_, kw = fn
                        engobj.dma_start(out=out, in_=in_, **kw).then_inc(dsem[o.key], 16)
                    else:
                        ins = getattr(engobj, fn[0])(*fn[1], **fn[2])
                        if o.needed:
                            ins.then_inc(esem[e], 1)

            @block.tensor
            def _(eng):
                run("pe", eng)

            @block.scalar
            def _(eng):
                run("act", eng)

            @block.vector
            def _(eng):
                run("dve", eng)

            @block.gpsimd
            def _(eng):
                run("pool", eng)

            @block.sync
            def _(eng):
                run("sp", eng)
                for k, v in self.dma_cnt.items():
                    eng.wait_ge(dsem[k], v)


class Arena:
    def __init__(self, t, nwords):
        self.t = t
        self.n = nwords
        self.off = 0

    def alloc(self, free, dtype=F32):
        n = int(np.prod(free))
        sz = 4 if dtype == F32 else 2
        words = (n * sz + 3) // 4
        words = (words + 1) // 2 * 2
        assert self.off + words <= self.n, ("arena overflow", self.off, words, self.n)
        ap = self.t[:, self.off:self.off + words]
        self.off += words
        if dtype != F32:
            ap = ap.bitcast(dtype)
        ap = ap[:, 0:n]
        if len(free) == 2:
            ap = ap.rearrange("p (a b) -> p a b", a=free[0])
        elif len(free) == 3:
            ap = ap.rearrange("p (a b c) -> p a b c", a=free[0], b=free[1])
        elif len(free) == 4:
            ap = ap.rearrange("p (a b c d) -> p a b c d", a=free[0], b=free[1], c=free[2])
        return ap

    def mark(self):
        return self.off

    def release(self, m):
        self.off = m


def build_program(stop_after=None):
    nc = bass.Bass("TRN2", target_bir_lowering=False)

    def din(name, shape, dt=F32):
        return nc.dram_tensor(name, list(shape), dt, kind="ExternalInput").ap()

    def dout(name, shape, dt=F32):
        return nc.dram_tensor(name, list(shape), dt, kind="ExternalOutput").ap()

    x_in = din("x", [TOK, D])
    st_ret = din("st_ret", [16, 4, 128, 128])
    st_h = din("st_h", [128, 4, 16])
    st_conv = din("st_conv", [128, 4, 16, 3])
    ck = din("ck", [16, 256, D])
    cv = din("cv", [16, 256, D])
    memp = din("memp", [256, D])
    gains = {n: din("g_" + n, [128, D]) for n in ("ffn1", "mix", "xattn", "ffn2", "final")}
    wts = {}
    for n in ("ffn1", "ffn2"):
        wts[n + "_wg"] = din(n + "_wg", [D, FF])
        wts[n + "_wu"] = din(n + "_wu", [D, FF])
        wts[n + "_wd"] = din(n + "_wd", [FF, D])
    w_in = din("w_in", [D, 3072])
    w_out = din("w_out", [D, D])
    wq = din("wq", [D, D])
    wk = din("wk", [D, D])
    wv = din("wv", [D, D])
    wo = din("wo", [D, D])
    gn_gain = din("gn_gain", [128, 512])
    lruvec = din("lruvec", [128, 4, 9])
    wabd = din("wabd", [128, 4, 128])
    wxbd = din("wxbd", [128, 4, 128])
    c_ident = din("c_ident", [128, 128], BF16)
    c_ones = din("c_ones", [128, 128])
    c_cos = din("c_cos", [128, TOK])
    c_sin = din("c_sin", [128, TOK])
    c_dmask = din("c_dmask", [2, 128, 4, 128])
    c_qdec = din("c_qdec", [2, 128, 4, 128])
    c_kdec = din("c_kdec", [2, 128, 4])
    c_bmask = din("c_bmask", [128, 16, 128], BF16)
    c_rmask = din("c_rmask", [128, 16])

    y_out = dout("y", [TOK, D])
    o_retp = dout("o_retp", [4, 128, 128])
    o_hp = dout("o_hp", [128, 4])
    o_convp = dout("o_convp", [128, 4, 3])
    o_mk = dout("o_mk", [256, D])
    o_mv = dout("o_mv", [256, D])
    o_rets = dout("o_rets", [16, 4, 128, 128])
    o_hs = dout("o_hs", [128, 4, 16])
    o_convs = dout("o_convs", [128, 4, 16, 3])

    GAMMA = [1.0 - 2.0 ** (-5.0 - h) for h in range(4)]

    NW = 212800 // 4
    with contextlib.ExitStack() as st:
        arena_t = st.enter_context(nc.sbuf_tensor("arena", [128, NW], F32))
        psum = st.enter_context(nc.psum_tensor("psum", [128, 4096], F32))
        p = Prog(nc)
        A = Arena(arena_t, NW)
        dbgn = [0]

        def dbg(name, ap, reads):
            shp = list(ap.shape)
            o = dout("dbg_" + name, shp, ap.dtype)
            dbgn[0] += 1
            (p.dma("sp", o, ap, key="dbg%d" % dbgn[0], reads=reads))

        bank = [psum[:, b * 512:(b + 1) * 512] for b in range(8)]
        bank_bf = [bank[b].bitcast(BF16) for b in range(8)]
        bres = [Res("bank%d" % b, excl=True) for b in range(8)]

        def bank2(b):
            return psum[:, b * 512:(b + 2) * 512]

        xres = A.alloc([NT, D], F32)
        xr = [Res("x%d" % i) for i in range(NT)]
        ident = A.alloc([128], BF16)
        r_ident = Res()
        ssq = A.alloc([NT], F32)
        rstd = A.alloc([NT], F32)
        tmp17 = A.alloc([NT], F32)
        cm05 = A.alloc([4], F32)
        cp05 = A.alloc([4], F32)
        r_cm = Res()
        gbc = A.alloc([D], F32)
        r_gbc = Res()
        xnb = [A.alloc([D], BF16) for _ in range(2)]
        r_xnb = [Res(), Res()]
        junk = A.alloc([D], BF16)
        r_junk = Res()
        r_ssq = [Res() for _ in range(NT)]
        base_mark = A.mark()

        xv_in = x_in.rearrange("(i p) d -> p i d", p=128)
        yv_out = y_out.rearrange("(i p) d -> p i d", p=128)

        p.dma("sp", ident, c_ident, key="c0", writes=[r_ident])
        p.filler_ap = ident
        p.op("dve", lambda e: e.memset(cm05, -0.5), writes=[r_cm])
        p.op("dve", lambda e: e.memset(cp05, 0.5), writes=[r_cm])
        for i in range(NT):
            p.dma("sp", xres[:, i, :], xv_in[:, i, :], key="xin%d" % (i % 4), writes=[xr[i]])

        kcnt = [0]

        def norm_tile(i, dst, dst_res, dst_col0, tpb):
            nb = i % 2
            p.op("act", lambda e: e.activation(out=junk, in_=xres[:, i, :], func=AF.Square,
                                               accum_out=ssq[:, i:i + 1]),
                 reads=[xr[i]], writes=[r_junk, r_ssq[i]])
            p.op("dve", lambda e: e.tensor_scalar(out=tmp17[:, i:i + 1], in0=ssq[:, i:i + 1], scalar1=1.0 / D,
                                                  scalar2=EPS, op0=ALU.mult, op1=ALU.add),
                 reads=[], writes=[r_ssq[i]])
            p.op("pool", lambda e: e.tensor_tensor(out=rstd[:, i:i + 1], in0=tmp17[:, i:i + 1],
                                                   in1=cm05[:, 0:1], op=ALU.pow),
                 reads=[r_cm], writes=[r_ssq[i]])
            p.op("dve", lambda e: e.scalar_tensor_tensor(out=xnb[nb], in0=xres[:, i, :], scalar=rstd[:, i:i + 1],
                                                         in1=gbc, op0=ALU.mult, op1=ALU.mult),
                 reads=[xr[i], r_gbc, r_ssq[i]], writes=[r_xnb[nb]])
            for k in range(8):
                p.op("pe", lambda e, k=k: e.transpose(bank_bf[tpb][:, k * 128:(k + 1) * 128],
                                                      xnb[nb][:, k * 128:(k + 1) * 128], ident),
                     reads=[r_xnb[nb], r_ident], writes=[bres[tpb]])
            p.op("act", lambda e: e.activation(out=dst[:, :, dst_col0:dst_col0 + 128],
                                               in_=bank_bf[tpb].rearrange("p (k n) -> p k n", k=8),
                                               func=AF.Copy),
                 reads=[bres[tpb]], writes=[dst_res])

        def load_gain(name):
            p.dma("sp", gbc, gains[name], key="gbc", writes=[r_gbc])

        def ffn(name):
            wgv = wts[name + "_wg"].rearrange("(k p) n -> p k n", p=128)
            wuv = wts[name + "_wu"].rearrange("(k p) n -> p k n", p=128)
            wdv = wts[name + "_wd"].rearrange("(f p) d -> p f d", p=128)
            m0 = A.mark()
            xnT = A.alloc([8, TOK], BF16)
            r_xnT = [Res() for _ in range(NT)]
            hT = A.alloc([8, TOK], BF16)
            r_hT = {}
            wd_sb = A.alloc([8, D], BF16)
            r_wd = Res()
            ring = [A.alloc([2, 8, 256], BF16) for _ in range(2)]
            r_rg = [Res(), Res()]
            r_ru = [Res(), Res()]
            sgb = [A.alloc([512], F32) for _ in range(2)]
            r_sgb = [Res(), Res()]
            load_gain(name)
            for i in range(NT):
                norm_tile(i, xnT, r_xnT[i], i * 128, 4 + 2 * (i % 2))
            gi = 0
            for (f0, f1) in FF_PARTS:
                nf = f1 - f0
                p.dma("pool", wd_sb[:, 0:nf, :], wdv[:, f0:f1, :], key="wd", writes=[r_wd])
                for g0 in range(f0, f1, 2):
                    sl = gi % 2
                    gi += 1
                    p.dma("pool", ring[sl][:, 0, :, :], wgv[:, :, g0 * 128:(g0 + 2) * 128], key="rg%d" % sl,
                          writes=[r_rg[sl]])
                    p.dma("pool", ring[sl][:, 1, :, :], wuv[:, :, g0 * 128:(g0 + 2) * 128], key="ru%d" % sl,
                          writes=[r_ru[sl]])
                    for fl in range(2):
                        f = g0 + fl
                        for bi, (t0, nb) in enumerate(BLOCKS):
                            par = kcnt[0] % 2
                            kcnt[0] += 1
                            gb, ub = par, 2 + par
                            tiles = range(t0 // 128, (t0 + nb) // 128)
                            for k in range(8):
                                p.op("pe", lambda e, k=k, fl=fl, sl=sl, gb=gb, t0=t0, nb=nb: e.matmul(
                                    bank[gb][:, 0:nb], ring[sl][:, 0, k, fl * 128:(fl + 1) * 128],
                                    xnT[:, k, t0:t0 + nb], start=(k == 0), stop=(k == 7)),
                                    reads=[r_rg[sl]] + [r_xnT[i] for i in tiles], writes=[bres[gb]])
                            for k in range(8):
                                p.op("pe", lambda e, k=k, fl=fl, sl=sl, ub=ub, t0=t0, nb=nb: e.matmul(
                                    bank[ub][:, 0:nb], ring[sl][:, 1, k, fl * 128:(fl + 1) * 128],
                                    xnT[:, k, t0:t0 + nb], start=(k == 0), stop=(k == 7)),
                                    reads=[r_ru[sl]] + [r_xnT[i] for i in tiles], writes=[bres[ub]])
                            p.op("act", lambda e, gb=gb, par=par, nb=nb: e.activation(
                                out=sgb[par][:, 0:nb], in_=bank[gb][:, 0:nb], func=AF.Silu),
                                reads=[bres[gb]], writes=[r_sgb[par]])
                            rh = r_hT.setdefault((f - f0, bi), Res())
                            p.op("dve", lambda e, ub=ub, par=par, nb=nb, t0=t0, f=f, f0=f0: e.tensor_tensor(
                                out=hT[:, f - f0, t0:t0 + nb], in0=sgb[par][:, 0:nb], in1=bank[ub][:, 0:nb],
                                op=ALU.mult),
                                reads=[r_sgb[par], bres[ub]], writes=[rh])
                for i in range(NT):
                    yb = 4 + 2 * (i % 2)
                    bi = min(i // 4, 4)
                    for f in range(nf):
                        for hc in range(2):
                            p.op("pe", lambda e, f=f, hc=hc, yb=yb, i=i: e.matmul(
                                bank[yb + hc], hT[:, f, i * 128:(i + 1) * 128],
                                wd_sb[:, f, hc * 512:(hc + 1) * 512], start=(f == 0), stop=(f == nf - 1)),
                                reads=[r_hT[(f, bi)], r_wd], writes=[bres[yb + hc]])
                    p.op("dve", lambda e, yb=yb, i=i: e.scalar_tensor_tensor(
                        out=xres[:, i, :], in0=bank2(yb), scalar=0.5, in1=xres[:, i, :],
                        op0=ALU.mult, op1=ALU.add),
                        reads=[bres[yb], bres[yb + 1]], writes=[xr[i]])
            A.release(m0)
            p.barrier()
            return None

        def interleave(*gens):
            gens = list(gens)
            while gens:
                for g in list(gens):
                    try:
                        next(g)
                    except StopIteration:
                        gens.remove(g)

        def norm_block(t0, nb, xnTb, r_xnTb):
            for tt in range(nb // 128):
                norm_tile(t0 // 128 + tt, xnTb, r_xnTb, tt * 128, 5)

        def mix_ret(retT, r_retT):
            winv = w_in.rearrange("(k p) n -> p k n", p=128)
            m0 = A.mark()
            wqk = A.alloc([8, 1024], BF16)
            wvg = A.alloc([8, 1024], BF16)
            r_wqk, r_wvg = Res(), Res()
            p.dma("pool", wqk, winv[:, :, 0:1024], key="wA", writes=[r_wqk])
            p.dma("pool", wvg, winv[:, :, 1024:2048], key="wB", writes=[r_wvg])
            dmask = A.alloc([2, 512], F32)
            qdec = A.alloc([2, 512], F32)
            kdec = A.alloc([2, 4], F32)
            gng = A.alloc([512], F32)
            bmask = A.alloc([16, 128], BF16)
            rmask = A.alloc([16], F32)
            r_c = Res()
            for v in range(2):
                p.dma("sp", dmask[:, v, :], c_dmask[v].rearrange("p h i -> p (h i)"), key="c0", writes=[r_c])
                p.dma("sp", qdec[:, v, :], c_qdec[v].rearrange("p h i -> p (h i)"), key="c0", writes=[r_c])
                p.dma("sp", kdec[:, v, :], c_kdec[v], key="c0", writes=[r_c])
            p.dma("sp", gng, gn_gain, key="c0", writes=[r_c])
            p.dma("sp", bmask, c_bmask, key="c0", writes=[r_c])
            p.dma("sp", rmask, c_rmask, key="c0", writes=[r_c])
            load_gain("mix")
            xnTb = A.alloc([8, 256], BF16)
            r_xnTb = Res()
            cosb = A.alloc([256], F32)
            sinb = A.alloc([256], F32)
            r_cs = Res()
            rt1s = [A.alloc([256], F32) for _ in range(2)]
            rt2s = [A.alloc([256], F32) for _ in range(2)]
            r_rt1s, r_rt2s = [Res(), Res()], [Res(), Res()]
            rt1, rt2, r_rt1, r_rt2 = rt1s[0], rt2s[0], r_rt1s[0], r_rt2s[0]
            qT = A.alloc([4, 256], BF16)
            kT = A.alloc([4, 256], BF16)
            qdT = A.alloc([4, 256], BF16)
            kdn = A.alloc([2, 4, 128], BF16)
            vsb = A.alloc([2, 512], BF16)
            sg = A.alloc([2, 512], F32)
            r_qT, r_kT, r_qdT = Res(), Res(), Res()
            r_kdn = [Res() for _ in range(4)]
            r_v = [Res() for _ in range(4)]
            r_sg = [Res() for _ in range(4)]
            tht = A.alloc([512], F32)
            r_tht = Res()
            state = A.alloc([4, 128], F32)
            state_bf = A.alloc([4, 128], BF16)
            r_state, r_statebf = Res(), Res()
            sms = [A.alloc([512], BF16) for _ in range(2)]
            r_sms = [Res(), Res()]
            ons = [A.alloc([512], F32) for _ in range(2)]
            r_ons = [Res(), Res()]
            ros = [A.alloc([512], BF16) for _ in range(2)]
            r_ros = [Res(), Res()]
            statss = [A.alloc([8, 4], F32) for _ in range(2)]
            r_statss = [Res(), Res()]
            sm, r_sm, on, r_on, ro, r_ro, stats, r_stats = sms[0], r_sms[0], ons[0], r_ons[0], ros[0], r_ros[0], statss[0], r_statss[0]
            p.op("dve", lambda e: e.memset(state, 0.0), writes=[r_state])
            p.op("dve", lambda e: e.memset(state_bf, 0.0), writes=[r_statebf])

            def ret_stream(blocks, Bf):
                (xnTb, r_xnTb, cosb, sinb, r_cs, rt1s, rt2s, r_rt1s, r_rt2s, qT, kT, qdT, r_qT, r_kT, r_qdT,
                 kdn, r_kdn, vsb, r_v, sg, r_sg) = (Bf[k] for k in (
                    "xnTb", "r_xnTb", "cosb", "sinb", "r_cs", "rt1s", "rt2s", "r_rt1s", "r_rt2s", "qT", "kT", "qdT",
                    "r_qT", "r_kT", "r_qdT", "kdn", "r_kdn", "vsb", "r_v", "sg", "r_sg"))
                rt1, rt2, r_rt1, r_rt2 = rt1s[0], rt2s[0], r_rt1s[0], r_rt2s[0]
                for (t0, nb) in blocks:
                    is_s = (t0 == 2048)
                    v = 1 if is_s else 0
                    nt = nb // 128
                    norm_block(t0, nb, xnTb, r_xnTb)
                    p.dma("sp", cosb[:, 0:nb], c_cos[:, t0:t0 + nb], key=("cs%d" % (1 if t0 == 2048 else 0)), writes=[r_cs])
                    p.dma("sp", sinb[:, 0:nb], c_sin[:, t0:t0 + nb], key=("cs%d" % (1 if t0 == 2048 else 0)), writes=[r_cs])
                    for c in range(8):
                        pb = c % 2
                        rt1, rt2, r_rt1, r_rt2 = rt1s[pb], rt2s[pb], r_rt1s[pb], r_rt2s[pb]
                        for k in range(8):
                            p.op("pe", lambda e, k=k, c=c, pb=pb: e.matmul(
                                bank[pb][:, 0:nb], wqk[:, k, c * 128:(c + 1) * 128], xnTb[:, k, 0:nb],
                                start=(k == 0), stop=(k == 7)),
                                reads=[r_wqk, r_xnTb], writes=[bres[pb]])
                        dst = qT if c < 4 else kT
                        r_dst = r_qT if c < 4 else r_kT
                        h = c % 4
                        p.op("dve", lambda e, pb=pb: e.tensor_tensor(out=rt1[:, 0:nb], in0=bank[pb][:, 0:nb],
                                                                      in1=cosb[:, 0:nb], op=ALU.mult),
                             reads=[bres[pb], r_cs], writes=[r_rt1])
                        p.op("dve", lambda e, pb=pb: e.tensor_tensor(out=rt2[0:64, 0:nb], in0=bank[pb][64:128, 0:nb],
                                                                      in1=sinb[0:64, 0:nb], op=ALU.mult),
                             reads=[bres[pb], r_cs], writes=[r_rt2])
                        p.op("dve", lambda e, pb=pb: e.tensor_tensor(out=rt2[64:128, 0:nb], in0=bank[pb][0:64, 0:nb],
                                                                      in1=sinb[64:128, 0:nb], op=ALU.mult),
                             reads=[bres[pb], r_cs], writes=[r_rt2])
                        p.op("pool", lambda e, dst=dst, h=h: e.tensor_tensor(out=dst[:, h, 0:nb], in0=rt1[:, 0:nb],
                                                                             in1=rt2[:, 0:nb], op=ALU.add),
                             reads=[r_rt1, r_rt2], writes=[r_dst])
                    for h in range(4):
                        p.op("pool", lambda e, h=h: e.tensor_tensor(
                            out=qdT[:, h, 0:nb].rearrange("p (t i) -> p t i", i=128),
                            in0=qT[:, h, 0:nb].rearrange("p (t i) -> p t i", i=128),
                            in1=qdec[:, v, h * 128:(h + 1) * 128].unsqueeze(1).to_broadcast([128, nt, 128]),
                            op=ALU.mult),
                            reads=[r_qT, r_c], writes=[r_qdT])
                    for tt in range(nt):
                        for h in range(4):
                            p.op("pe", lambda e, h=h, tt=tt: e.transpose(
                                bank_bf[5][:, h * 128:(h + 1) * 128], kT[:, h, tt * 128:(tt + 1) * 128], ident),
                                reads=[r_kT, r_ident], writes=[bres[5]])
                        p.op("dve", lambda e, tt=tt: e.tensor_tensor(
                            out=kdn[:, tt, :, :],
                            in0=bank_bf[5][:, 0:512].rearrange("p (h d) -> p h d", h=4),
                            in1=kdec[:, v, :].unsqueeze(2).to_broadcast([128, 4, 128]), op=ALU.mult),
                            reads=[bres[5], r_c], writes=[r_kdn[tt]])
                        for k in range(8):
                            p.op("pe", lambda e, k=k, tt=tt: e.matmul(
                                bank[2], xnTb[:, k, tt * 128:(tt + 1) * 128], wvg[:, k, 0:512],
                                start=(k == 0), stop=(k == 7)),
                                reads=[r_wvg, r_xnTb], writes=[bres[2]])
                        p.op("act", lambda e, tt=tt: e.activation(out=vsb[:, tt, :], in_=bank[2], func=AF.Copy),
                             reads=[bres[2]], writes=[r_v[tt]])
                        for k in range(8):
                            p.op("pe", lambda e, k=k, tt=tt: e.matmul(
                                bank[3], xnTb[:, k, tt * 128:(tt + 1) * 128], wvg[:, k, 512:1024],
                                start=(k == 0), stop=(k == 7)),
                                reads=[r_wvg, r_xnTb], writes=[bres[3]])
                        p.op("act", lambda e: e.activation(out=tht, in_=bank[3], func=AF.Tanh, scale=0.5),
                             reads=[bres[3]], writes=[r_tht])
                        p.op("dve", lambda e, tt=tt: e.scalar_tensor_tensor(
                            out=sg[:, tt, :], in0=tht, scalar=1.0, in1=bank[3], op0=ALU.add, op1=ALU.mult),
                            reads=[r_tht, bres[3]], writes=[r_sg[tt]])
                    if is_s:
                        m1 = A.mark()
                        st0 = [A.alloc([8, 128], F32)] * 2
                        st0b = [A.alloc([8, 128], BF16)] * 2
                        r_st0 = [Res()] * 2
                        r_st0b = [Res()] * 2
                        qdm = [A.alloc([4, 2, 128], BF16)] * 2
                        r_qdm = [Res()] * 2
                        kdm = [A.alloc([4, 128], BF16) for _ in range(2)]
                        r_kdm = [Res(), Res()]
                        nst = [A.alloc([4, 128], F32) for _ in range(2)]
                        r_nst = [Res(), Res()]
                        oacc = A.alloc([512], F32)
                        r_oacc = Res()
                    yield
                    for tt in range(nt):
                        tl = slice(tt * 128, (tt + 1) * 128)
                        gi = t0 // 128 + tt
                        pq = gi % 2
                        sm, r_sm, on, r_on, ro, r_ro, stats, r_stats = (sms[pq], r_sms[pq], ons[pq], r_ons[pq], ros[pq],
                                                                        r_ros[pq], statss[pq], r_statss[pq])
                        ob = 6 + pq
                        if is_s:
                            o_h = [oacc[:, h * 128:(h + 1) * 128] for h in range(4)]
                            o_all = oacc.rearrange("p (h d) -> p h d", h=4)
                            o_res = [r_oacc] * 4
                        else:
                            o_h = [bank[ob][:, h * 128:(h + 1) * 128] for h in range(4)]
                            o_all = bank[ob].rearrange("p (h d) -> p h d", h=4)
                            o_res = [bres[ob]] * 4
                        for h in range(4):
                            p.op("pe", lambda e, h=h, tl=tl: e.matmul(
                                bank[4][:, h * 128:(h + 1) * 128], kT[:, h, tl], qT[:, h, tl], start=True, stop=True),
                                reads=[r_kT, r_qT], writes=[bres[4]])
                        p.op("dve", lambda e: e.tensor_tensor(out=sm, in0=bank[4], in1=dmask[:, v, :], op=ALU.mult),
                             reads=[bres[4], r_c], writes=[r_sm])
                        if not is_s:
                            for h in range(4):
                                hs_ = slice(h * 128, (h + 1) * 128)
                                p.op("pe", lambda e, hs_=hs_, tt=tt: e.matmul(
                                    bank[ob][:, hs_], sm[:, hs_], vsb[:, tt, hs_], start=True, stop=False),
                                    reads=[r_sm, r_v[tt]], writes=[bres[ob]])
                                p.op("pe", lambda e, hs_=hs_, h=h, tl=tl: e.matmul(
                                    bank[ob][:, hs_], qdT[:, h, tl], state_bf[:, h, :], start=False, stop=True),
                                    reads=[r_qdT, r_statebf], writes=[bres[ob]])
                            for h in range(4):
                                hs_ = slice(h * 128, (h + 1) * 128)
                                p.op("pe", lambda e, hs_=hs_, h=h, tt=tt: e.matmul(
                                    bank[4][:, hs_], kdn[:, tt, h, :], vsb[:, tt, hs_], start=True, stop=True),
                                    reads=[r_kdn[tt], r_v[tt]], writes=[bres[4]])
                            for h in range(4):
                                hs_ = slice(h * 128, (h + 1) * 128)
                                p.op("dve", lambda e, hs_=hs_, h=h: e.scalar_tensor_tensor(
                                    out=state[:, h, :], in0=state[:, h, :], scalar=float(GAMMA[h] ** 128),
                                    in1=bank[4][:, hs_], op0=ALU.mult, op1=ALU.add),
                                    reads=[bres[4]], writes=[r_state])
                            p.op("act", lambda e: e.activation(out=state_bf, in_=state, func=AF.Copy),
                                 reads=[r_state], writes=[r_statebf])
                            if gi == 15:
                                (p.dma("sp", o_retp.rearrange("h d e -> d h e"), state, key="osm",
                                                     reads=[r_state]))
                        else:
                            for h in range(4):
                                hs_ = slice(h * 128, (h + 1) * 128)
                                p.op("pe", lambda e, hs_=hs_, h=h: e.matmul(
                                    bank[ob][:, hs_], sm[:, hs_], vsb[:, 0, hs_], start=True, stop=True),
                                    reads=[r_sm, r_v[0]], writes=[bres[ob]])
                            p.op("dve", lambda e: e.tensor_copy(out=oacc, in_=bank[ob]),
                                 reads=[bres[ob]], writes=[r_oacc])
                            for g in range(8):
                                yield
                                b2 = g % 2
                                p.dma("sp", st0[b2], st_ret[2 * g:2 * g + 2].rearrange("s h d e -> d (s h) e"),
                                      key="st0", writes=[r_st0[b2]])
                                p.op("act", lambda e, b2=b2: e.activation(out=st0b[b2], in_=st0[b2], func=AF.Copy),
                                     reads=[r_st0[b2]], writes=[r_st0b[b2]])
                                for h in range(4):
                                    p.op("dve", lambda e, h=h, b2=b2, g=g: e.tensor_tensor(
                                        out=qdm[b2][:, h, :, :],
                                        in0=qdT[:, h, 0:128].unsqueeze(1).to_broadcast([128, 2, 128]),
                                        in1=bmask[:, 2 * g:2 * g + 2, :], op=ALU.mult),
                                        reads=[r_qdT, r_c], writes=[r_qdm[b2]])
                                for h in range(4):
                                    hs_ = slice(h * 128, (h + 1) * 128)
                                    for sl in range(2):
                                        p.op("pe", lambda e, hs_=hs_, h=h, sl=sl, b2=b2: e.matmul(
                                            bank[ob][:, hs_], qdm[b2][:, h, sl, :], st0b[b2][:, sl * 4 + h, :],
                                            start=(sl == 0), stop=(sl == 1)),
                                            reads=[r_qdm[b2], r_st0b[b2]], writes=[bres[ob]])
                                p.op("dve", lambda e: e.tensor_tensor(out=oacc, in0=bank[ob], in1=oacc, op=ALU.add),
                                     reads=[bres[ob]], writes=[r_oacc])
                                for sl in range(2):
                                    s_ = 2 * g + sl
                                    k2 = s_ % 2
                                    p.op("dve", lambda e, k2=k2, s_=s_: e.tensor_scalar(
                                        out=kdm[k2], in0=kdn[:, 0, :, :], scalar1=rmask[:, s_:s_ + 1], scalar2=None,
                                        op0=ALU.mult),
                                        reads=[r_kdn[0], r_c], writes=[r_kdm[k2]])
                                    for h in range(4):
                                        hs_ = slice(h * 128, (h + 1) * 128)
                                        p.op("pe", lambda e, hs_=hs_, h=h, k2=k2: e.matmul(
                                            bank[7][:, hs_], kdm[k2][:, h, :], vsb[:, 0, hs_], start=True, stop=True),
                                            reads=[r_kdm[k2], r_v[0]], writes=[bres[7]])
                                    for h in range(4):
                                        hs_ = slice(h * 128, (h + 1) * 128)
                                        p.op("dve", lambda e, hs_=hs_, h=h, k2=k2, sl=sl, b2=b2: e.scalar_tensor_tensor(
                                            out=nst[k2][:, h, :], in0=st0[b2][:, sl * 4 + h, :],
                                            scalar=float(GAMMA[h] ** 8), in1=bank[7][:, hs_],
                                            op0=ALU.mult, op1=ALU.add),
                                            reads=[bres[7], r_st0[b2]], writes=[r_nst[k2]])
                                    (p.dma("sp", o_rets[s_].rearrange("h d e -> d h e"), nst[k2],
                                                         key="ons%d" % k2, reads=[r_nst[k2]]))
                        p.op("dve", lambda e: e.tensor_reduce(
                            out=stats[:, 0, :], in_=o_all, axis=AX.X, op=ALU.add),
                            reads=list(set(o_res)), writes=[r_stats])
                        for h in range(4):
                            p.op("act", lambda e, h=h: e.activation(
                                out=junk[:, 0:128], in_=o_h[h], func=AF.Square,
                                accum_out=stats[:, 1, h:h + 1]),
                                reads=[o_res[h]], writes=[r_junk, r_stats])
                        p.op("dve", lambda e: e.tensor_scalar(out=stats[:, 2, :], in0=stats[:, 0, :], scalar1=1.0 / 128,
                                                              scalar2=None, op0=ALU.mult), writes=[r_stats])
                        p.op("dve", lambda e: e.tensor_tensor(out=stats[:, 3, :], in0=stats[:, 2, :], in1=stats[:, 2, :],
                                                              op=ALU.mult), writes=[r_stats])
                        p.op("dve", lambda e: e.scalar_tensor_tensor(out=stats[:, 4, :], in0=stats[:, 1, :],
                                                                     scalar=1.0 / 128, in1=stats[:, 3, :],
                                                                     op0=ALU.mult, op1=ALU.subtract), writes=[r_stats])
                        p.op("dve", lambda e: e.tensor_scalar(out=stats[:, 4, :], in0=stats[:, 4, :], scalar1=0.0,
                                                              scalar2=EPS, op0=ALU.max, op1=ALU.add), writes=[r_stats])
                        p.op("pool", lambda e: e.tensor_tensor(out=stats[:, 5, :], in0=stats[:, 4, :], in1=cm05[:, 0:4],
                                                               op=ALU.pow), reads=[r_cm], writes=[r_stats])
                        for h in range(4):
                            hs_ = slice(h * 128, (h + 1) * 128)
                            p.op("dve", lambda e, hs_=hs_, h=h: e.tensor_scalar(
                                out=on[:, hs_], in0=o_h[h], scalar1=stats[:, 2, h:h + 1],
                                scalar2=stats[:, 5, h:h + 1], op0=ALU.subtract, op1=ALU.mult),
                                reads=[o_res[h], r_stats], writes=[r_on])
                        p.op("pool", lambda e: e.tensor_tensor(out=on, in0=on, in1=gng, op=ALU.mult),
                             reads=[r_c], writes=[r_on])
                        p.op("dve", lambda e, tt=tt: e.scalar_tensor_tensor(
                            out=ro, in0=on, scalar=0.5, in1=sg[:, tt, :], op0=ALU.mult, op1=ALU.mult),
                            reads=[r_on, r_sg[tt]], writes=[r_ro])
                        for h in range(4):
                            hs_ = slice(h * 128, (h + 1) * 128)
                            p.op("pe", lambda e, hs_=hs_: e.transpose(bank_bf[5][:, hs_], ro[:, hs_], ident),
                                 reads=[r_ro, r_ident], writes=[bres[5]])
                        p.op("act", lambda e, gi=gi: e.activation(
                            out=retT[:, :, gi * 128:(gi + 1) * 128],
                            in_=bank_bf[5][:, 0:512].rearrange("p (h d) -> p h d", h=4), func=AF.Copy),
                            reads=[bres[5]], writes=[r_retT[gi]])
                        yield
                    if is_s:
                        A.release(m1)
            Bf_p = dict(xnTb=xnTb, r_xnTb=r_xnTb, cosb=cosb, sinb=sinb, r_cs=r_cs, rt1s=rt1s, rt2s=rt2s,
                        r_rt1s=r_rt1s, r_rt2s=r_rt2s, qT=qT, kT=kT, qdT=qdT, r_qT=r_qT, r_kT=r_kT, r_qdT=r_qdT,
                        kdn=kdn, r_kdn=r_kdn, vsb=vsb, r_v=r_v, sg=sg, r_sg=r_sg)
            Bf_s = dict(xnTb=A.alloc([8, 128], BF16), r_xnTb=Res(), cosb=A.alloc([128], F32),
                        sinb=A.alloc([128], F32), r_cs=Res(),
                        rt1s=[A.alloc([128], F32)] * 2, rt2s=[A.alloc([128], F32)] * 2,
                        r_rt1s=[Res()] * 2, r_rt2s=[Res()] * 2,
                        qT=A.alloc([4, 128], BF16), kT=A.alloc([4, 128], BF16), qdT=A.alloc([4, 128], BF16),
                        r_qT=Res(), r_kT=Res(), r_qdT=Res(),
                        kdn=A.alloc([1, 4, 128], BF16), r_kdn=[Res()], vsb=A.alloc([1, 512], BF16), r_v=[Res()],
                        sg=A.alloc([1, 512], F32), r_sg=[Res()])
            interleave(ret_stream([(j * 256, 256) for j in range(8)], Bf_p), ret_stream([(2048, 128)], Bf_s))
            A.release(m0)
            p.barrier()

        def mix_lru(retT, r_retT):
            winv = w_in.rearrange("(k p) n -> p k n", p=128)
            woutv = w_out.rearrange("(k p) n -> p k n", p=128)
            m0 = A.mark()
            wug = A.alloc([8, 1024], BF16)
            wo_sb = A.alloc([8, 1024], BF16)
            r_wug, r_wo = Res(), Res()
            p.dma("pool", wug, winv[:, :, 2048:3072], key="wA", writes=[r_wug])
            p.dma("pool", wo_sb, woutv, key="wB", writes=[r_wo])
            wa_sb = A.alloc([4, 128], BF16)
            wx_sb = A.alloc([4, 128], BF16)
            r_wax = Res()
            p.dma("pool", wa_sb, wabd, key="wC", writes=[r_wax])
            p.dma("pool", wx_sb, wxbd, key="wC", writes=[r_wax])
            lv = A.alloc([4, 9], F32)
            ones = A.alloc([128], F32)
            h0T = A.alloc([4, 16], F32)
            r_c = Res()
            p.dma("sp", lv, lruvec, key="c0", writes=[r_c])
            p.dma("sp", ones, c_ones, key="c0", writes=[r_c])
            p.dma("sp", h0T, st_h, key="c0", writes=[r_c])
            load_gain("mix")
            cst = A.alloc([4, 8], F32)
            r_cst = Res()
            p.op("act", lambda e: e.activation(out=cst[:, :, 0], in_=lv[:, :, 7], func=AF.Exp, scale=-1.0),
                 reads=[r_c], writes=[r_cst])
            p.op("act", lambda e: e.activation(out=cst[:, :, 1], in_=cst[:, :, 0], func=AF.Ln, bias=1.0),
                 writes=[r_cst])
            p.op("dve", lambda e: e.tensor_scalar(out=cst[:, :, 2], in0=cst[:, :, 1], scalar1=-8.0, scalar2=None,
                                                  op0=ALU.mult), writes=[r_cst])
            p.op("dve", lambda e: e.tensor_scalar(out=cst[:, :, 3], in0=cst[:, :, 1], scalar1=-4.0, scalar2=None,
                                                  op0=ALU.mult), writes=[r_cst])
            p.op("dve", lambda e: e.tensor_scalar(out=cst[:, :, 4], in0=lv[:, :, 5], scalar1=0.5, scalar2=None,
                                                  op0=ALU.mult), reads=[r_c], writes=[r_cst])
            p.op("dve", lambda e: e.tensor_scalar(out=cst[:, :, 5], in0=lv[:, :, 6], scalar1=0.5, scalar2=None,
                                                  op0=ALU.mult), reads=[r_c], writes=[r_cst])
            xnTb = A.alloc([8, 512], BF16)
            r_xnTb = Res()
            ubuf = A.alloc([4, 3 + 512], F32)
            r_ub = [Res() for _ in range(4)]
            hprev = A.alloc([4], F32)
            r_hp = [Res() for _ in range(4)]
            p.op("dve", lambda e: e.memset(hprev, 0.0), writes=r_hp)
            p.op("dve", lambda e: e.memset(ubuf[:, :, 0:3], 0.0), writes=r_ub)
            hs = A.alloc([4, 512], F32)
            r_hs = [Res() for _ in range(4)]
            gg = A.alloc([4, 512], BF16)
            r_gg = [Res() for _ in range(4)]
            TS = []
            for _ in range(2):
                d = {}
                for nm in ("ta", "tb", "xc", "tr", "ti", "av", "a2", "sv", "gx"):
                    d[nm] = A.alloc([512], F32)
                    d["r_" + nm] = Res()
                d["xcb"] = A.alloc([512], BF16)
                d["r_xcb"] = Res()
                TS.append(d)
            sqb = [A.alloc([512], F32) for _ in range(2)]
            rl = A.alloc([512], F32)
            mixL = A.alloc([4, 512], BF16)
            t16 = A.alloc([16], F32)
            hsl = A.alloc([4, 16], F32)
            r_hsl = Res()
            r_rl, r_lo, r_t16 = Res(), Res(), Res()
            r_sqb = [Res(), Res()]
            r_mixL = Res()
            GC = 0.7978845608028654

            for (t0, nb) in BLOCKS:
                is_s = (t0 == 2048)
                nt = nb // 128
                last_p = (t0 == 1536)
                norm_block(t0, nb, xnTb, r_xnTb)
                if is_s:
                    us = ubuf[:, :, 0:176].rearrange("p c (s j) -> p c s j", j=11)
                    for c in range(4):
                        p.dma("sp", us[:, c, :, 0:3], st_conv[:, c, :, :], key="c0", writes=[r_ub[c]])
                for c in range(4):
                    T_ = TS[c % 2]
                    ta, tb, xc, xcb, tr, ti, av, a2, sv, gx = (T_[k] for k in
                                                               ("ta", "tb", "xc", "xcb", "tr", "ti", "av", "a2", "sv", "gx"))
                    r_ta, r_tb, r_xc, r_xcb, r_tr, r_ti, r_av, r_a2, r_sv, r_gx = (
                        T_["r_" + k] for k in ("ta", "tb", "xc", "xcb", "tr", "ti", "av", "a2", "sv", "gx"))
                    for k in range(8):
                        p.op("pe", lambda e, k=k: e.matmul(
                            bank[0][:, 0:nb], wug[:, k, c * 128:(c + 1) * 128], xnTb[:, k, 0:nb],
                            start=(k == 0), stop=(k == 7)), reads=[r_wug, r_xnTb], writes=[bres[0]])
                    if not is_s:
                        p.op("act", lambda e: e.activation(out=ubuf[:, c, 3:3 + nb], in_=bank[0][:, 0:nb],
                                                           func=AF.Copy), reads=[bres[0]], writes=[r_ub[c]])
                    else:
                        p.op("act", lambda e: e.activation(
                            out=us[:, c, :, 3:11], in_=bank[0][:, 0:128].rearrange("p (s t) -> p s t", t=8),
                            func=AF.Copy), reads=[bres[0]], writes=[r_ub[c]])
                    for k in range(8):
                        p.op("pe", lambda e, k=k: e.matmul(
                            bank[1][:, 0:nb], wug[:, k, 512 + c * 128:512 + (c + 1) * 128], xnTb[:, k, 0:nb],
                            start=(k == 0), stop=(k == 7)), reads=[r_wug, r_xnTb], writes=[bres[1]])
                    p.op("act", lambda e: e.activation(out=ta[:, 0:nb], in_=bank[1][:, 0:nb], func=AF.Square),
                         reads=[bres[1]], writes=[r_ta])
                    p.op("pool", lambda e: e.tensor_scalar(out=ta[:, 0:nb], in0=ta[:, 0:nb], scalar1=0.044715,
                                                           scalar2=1.0, op0=ALU.mult, op1=ALU.add), writes=[r_ta])
                    p.op("dve", lambda e: e.tensor_tensor(out=ta[:, 0:nb], in0=ta[:, 0:nb], in1=bank[1][:, 0:nb],
                                                          op=ALU.mult), reads=[bres[1]], writes=[r_ta])
                    p.op("act", lambda e: e.activation(out=tb[:, 0:nb], in_=ta[:, 0:nb], func=AF.Tanh, scale=GC),
                         reads=[r_ta], writes=[r_tb])
                    p.op("dve", lambda e: e.scalar_tensor_tensor(
                        out=gg[:, c, 0:nb], in0=tb[:, 0:nb], scalar=1.0, in1=bank[1][:, 0:nb],
                        op0=ALU.add, op1=ALU.mult), reads=[r_tb, bres[1]], writes=[r_gg[c]])
                    if not is_s:
                        uin = lambda j: ubuf[:, c, j:j + nb]
                        xcv = xc[:, 0:nb]
                    else:
                        uin = lambda j: us[:, c, :, j:j + 8]
                        xcv = xc[:, 0:128].rearrange("p (s t) -> p s t", t=8)
                    p.op("pool", lambda e: e.tensor_scalar(out=xcv, in0=uin(0), scalar1=lv[:, c, 0:1],
                                                           scalar2=lv[:, c, 4:5], op0=ALU.mult, op1=ALU.add),
                         reads=[r_ub[c], r_c], writes=[r_xc])
                    for j in range(1, 4):
                        p.op("dve", lambda e, j=j: e.scalar_tensor_tensor(
                            out=xcv, in0=uin(j), scalar=lv[:, c, j:j + 1], in1=xcv, op0=ALU.mult, op1=ALU.add),
                            reads=[r_ub[c], r_c], writes=[r_xc])
                    if last_p:
                        (p.dma("sp", o_convp[:, c, :], ubuf[:, c, nb:nb + 3], key="osm",
                                             reads=[r_ub[c]]))
                    if is_s:
                        (p.dma("sp", o_convs[:, c, :, :], us[:, c, :, 8:11], key="osm",
                                             reads=[r_ub[c]]))
                    elif not last_p:
                        p.op("dve", lambda e: e.tensor_copy(out=ubuf[:, c, 0:3], in_=ubuf[:, c, nb:nb + 3]),
                             writes=[r_ub[c]])
                    p.op("act", lambda e: e.activation(out=xcb[:, 0:nb], in_=xc[:, 0:nb], func=AF.Copy),
                         reads=[r_xc], writes=[r_xcb])
                    p.op("pe", lambda e: e.matmul(bank[2][:, 0:nb], wa_sb[:, c, :], xcb[:, 0:nb],
                                                  start=True, stop=True),
                         reads=[r_wax, r_xcb], writes=[bres[2]])
                    p.op("pe", lambda e: e.matmul(bank[3][:, 0:nb], wx_sb[:, c, :], xcb[:, 0:nb],
                                                  start=True, stop=True),
                         reads=[r_wax, r_xcb], writes=[bres[3]])
                    p.op("act", lambda e: e.activation(out=tr[:, 0:nb], in_=bank[2][:, 0:nb], func=AF.Tanh,
                                                       scale=0.5, bias=cst[:, c, 4:5]),
                         reads=[bres[2], r_cst], writes=[r_tr])
                    p.op("act", lambda e: e.activation(out=ti[:, 0:nb], in_=bank[3][:, 0:nb], func=AF.Tanh,
                                                       scale=0.5, bias=cst[:, c, 5:6]),
                         reads=[bres[3], r_cst], writes=[r_ti])
                    p.op("act", lambda e: e.activation(out=av[:, 0:nb], in_=tr[:, 0:nb], func=AF.Exp,
                                                       scale=cst[:, c, 3:4], bias=cst[:, c, 3:4]),
                         reads=[r_tr, r_cst], writes=[r_av])
                    p.op("act", lambda e: e.activation(out=a2[:, 0:nb], in_=tr[:, 0:nb], func=AF.Exp,
                                                       scale=cst[:, c, 2:3], bias=cst[:, c, 2:3]),
                         reads=[r_tr, r_cst], writes=[r_a2])
                    p.op("act", lambda e: e.activation(out=sv[:, 0:nb], in_=a2[:, 0:nb], func=AF.Sqrt,
                                                       scale=-1.0, bias=1.0),
                         reads=[r_a2], writes=[r_sv])
                    p.op("dve", lambda e: e.scalar_tensor_tensor(
                        out=gx[:, 0:nb], in0=ti[:, 0:nb], scalar=1.0, in1=xc[:, 0:nb], op0=ALU.add, op1=ALU.mult),
                        reads=[r_ti, r_xc], writes=[r_gx])
                    p.op("dve", lambda e: e.scalar_tensor_tensor(
                        out=gx[:, 0:nb], in0=gx[:, 0:nb], scalar=0.5, in1=sv[:, 0:nb], op0=ALU.mult, op1=ALU.mult),
                        reads=[r_sv], writes=[r_gx])
                    if not is_s:
                        p.op("dve", lambda e: e.tensor_tensor_scan(
                            out=hs[:, c, 0:nb], data0=av[:, 0:nb], data1=gx[:, 0:nb], initial=hprev[:, c:c + 1],
                            op0=ALU.mult, op1=ALU.add),
                            reads=[r_av, r_gx, r_hp[c]], writes=[r_hs[c]])
                        p.op("dve", lambda e: e.tensor_copy(out=hprev[:, c:c + 1], in_=hs[:, c, nb - 1:nb]),
                             reads=[r_hs[c]], writes=[r_hp[c]])
                        if last_p and c == 3:
                            (p.dma("sp", o_hp, hprev, key="osm", reads=r_hp))
                    else:
                        a3 = av[:, 0:128].rearrange("p (s t) -> p s t", t=8)
                        g3 = gx[:, 0:128].rearrange("p (s t) -> p s t", t=8)
                        p.op("dve", lambda e: e.tensor_tensor(out=t16, in0=a3[:, :, 0], in1=h0T[:, c, :],
                                                              op=ALU.mult), reads=[r_av, r_c], writes=[r_t16])
                        p.op("dve", lambda e: e.tensor_tensor(out=g3[:, :, 0], in0=g3[:, :, 0], in1=t16,
                                                              op=ALU.add), reads=[r_t16], writes=[r_gx])
                        p.op("dve", lambda e: e.memset(a3[:, :, 0], 0.0), writes=[r_av])
                        p.op("dve", lambda e: e.tensor_tensor_scan(
                            out=hs[:, c, 0:128], data0=av[:, 0:128], data1=gx[:, 0:128], initial=0.0,
                            op0=ALU.mult, op1=ALU.add), reads=[r_av, r_gx], writes=[r_hs[c]])
                        p.op("dve", lambda e: e.tensor_copy(
                            out=hsl[:, c, :], in_=hs[:, c, 0:128].rearrange("p (s t) -> p s t", t=8)[:, :, 7]),
                            reads=[r_hs[c]], writes=[r_hsl])
                        if c == 3:
                            (p.dma("sp", o_hs, hsl, key="osm", reads=[r_hsl]))
                    sb_ = sqb[c % 2]
                    p.op("act", lambda e: e.activation(out=sb_[:, 0:nb], in_=hs[:, c, 0:nb], func=AF.Square),
                         reads=[r_hs[c]], writes=[r_sqb[c % 2]])
                    p.op("pe", lambda e: e.matmul(bank[4][:, 0:nb], ones, sb_[:, 0:nb], start=(c == 0),
                                                  stop=(c == 3)),
                         reads=[r_sqb[c % 2], r_c], writes=[bres[4]])
                p.op("act", lambda e: e.activation(out=rl[:, 0:nb], in_=bank[4][:, 0:nb], func=AF.Sqrt,
                                                   scale=1.0 / 512, bias=EPS),
                     reads=[bres[4]], writes=[r_rl])
                p.op("dve", lambda e: e.reciprocal(out=rl[:, 0:nb], in_=rl[:, 0:nb]), writes=[r_rl])
                for c in range(4):
                    lo, r_lo_ = TS[c % 2]["ta"], TS[c % 2]["r_ta"]
                    p.op("dve", lambda e: e.scalar_tensor_tensor(
                        out=lo[:, 0:nb], in0=hs[:, c, 0:nb], scalar=lv[:, c, 8:9], in1=rl[:, 0:nb],
                        op0=ALU.mult, op1=ALU.mult), reads=[r_hs[c], r_rl, r_c], writes=[r_lo_])
                    p.op("dve", lambda e: e.scalar_tensor_tensor(
                        out=mixL[:, c, 0:nb], in0=lo[:, 0:nb], scalar=0.5, in1=gg[:, c, 0:nb],
                        op0=ALU.mult, op1=ALU.mult), reads=[r_lo_, r_gg[c]], writes=[r_mixL])
                for tt in range(nt):
                    gi = t0 // 128 + tt
                    for kk in range(8):
                        for hc in range(2):
                            if kk < 4:
                                lhs = retT[:, kk, gi * 128:(gi + 1) * 128]
                                rr = r_retT[gi]
                            else:
                                lhs = mixL[:, kk - 4, tt * 128:(tt + 1) * 128]
                                rr = r_mixL
                            p.op("pe", lambda e: e.matmul(bank[6 + hc], lhs, wo_sb[:, kk, hc * 512:(hc + 1) * 512],
                                                          start=(kk == 0), stop=(kk == 7)),
                                 reads=[rr, r_wo], writes=[bres[6 + hc]])
                    p.op("dve", lambda e: e.tensor_tensor(out=xres[:, gi, :], in0=bank2(6), in1=xres[:, gi, :],
                                                          op=ALU.add),
                         reads=[bres[6], bres[7]], writes=[xr[gi]])
            A.release(m0)
            p.barrier()

        def transpose_2x1024(src, r_src, dst, r_dst, tb):
            for rnd in range(2):
                for cc in range(4):
                    c = rnd * 4 + cc
                    for t in range(2):
                        p.op("pe", lambda e: e.transpose(
                            bank_bf[tb][:, cc * 256 + t * 128:cc * 256 + (t + 1) * 128],
                            src[:, t, c * 128:(c + 1) * 128], ident),
                            reads=[r_src, r_ident], writes=[bres[tb]])
                p.op("act", lambda e: e.activation(
                    out=dst[:, rnd * 4:(rnd + 1) * 4, :],
                    in_=bank_bf[tb].rearrange("p (c m) -> p c m", c=4), func=AF.Copy),
                    reads=[bres[tb]], writes=[r_dst])

        def xattn_kv(KTp, r_KTp, Vp, r_Vp):
            m0 = A.mark()
            wk_sb = A.alloc([8, 1024], BF16)
            wv_sb = A.alloc([8, 1024], BF16)
            r_wk, r_wv = Res(), Res()
            p.dma("pool", wk_sb, wk.rearrange("(k p) n -> p k n", p=128), key="wA", writes=[r_wk])
            p.dma("pool", wv_sb, wv.rearrange("(k p) n -> p k n", p=128), key="wB", writes=[r_wv])
            memb = A.alloc([2, 1024], BF16)
            r_memb = Res()
            p.dma("pool", memb, memp.rearrange("(t p) d -> p t d", p=128), key="wC", writes=[r_memb])
            memT = A.alloc([8, 256], BF16)
            r_memT = Res()
            kbp = A.alloc([2, 1024], BF16)
            r_kbp = Res()
            kf = [A.alloc([1024], F32) for _ in range(2)]
            r_kf = [Res(), Res()]
            transpose_2x1024(memb, r_memb, memT, r_memT, 5)
            n = 0
            for (wsb, r_w, dstb, r_dstb, oap, tag) in ((wk_sb, r_wk, kbp, r_kbp, o_mk, "k"),
                                                       (wv_sb, r_wv, Vp, r_Vp, o_mv, "v")):
                ov = oap.rearrange("(t p) d -> p t d", p=128)
                for t in range(2):
                    f = kf[n % 2]
                    r_f = r_kf[n % 2]
                    n += 1
                    for hc in range(2):
                        for k in range(8):
                            p.op("pe", lambda e: e.matmul(
                                bank[hc], memT[:, k, t * 128:(t + 1) * 128], wsb[:, k, hc * 512:(hc + 1) * 512],
                                start=(k == 0), stop=(k == 7)), reads=[r_memT, r_w], writes=[bres[hc]])
                    p.op("act", lambda e: e.activation(out=f, in_=bank2(0), func=AF.Copy),
                         reads=[bres[0], bres[1]], writes=[r_f])
                    p.op("dve", lambda e: e.tensor_copy(out=dstb[:, t, :], in_=f),
                         reads=[r_f], writes=[r_dstb])
                    (p.dma("sp", ov[:, t, :], f, key="okv%d" % (n % 2), reads=[r_f]))
            transpose_2x1024(kbp, r_kbp, KTp, r_KTp, 5)
            A.release(m0)
            p.barrier()

        def xattn_main(KTp, r_KTp, Vp, r_Vp):
            m0 = A.mark()
            wq_sb = A.alloc([8, 1024], BF16)
            wo_sb = A.alloc([8, 1024], BF16)
            r_wq, r_wo = Res(), Res()
            p.dma("pool", wq_sb, wq.rearrange("(k p) n -> p k n", p=128), key="wA", writes=[r_wq])
            p.dma("pool", wo_sb, wo.rearrange("(k p) n -> p k n", p=128), key="wB", writes=[r_wo])
            load_gain("xattn")
            bmask = A.alloc([16, 128], BF16)
            r_c = Res()
            p.dma("sp", bmask, c_bmask, key="c0", writes=[r_c])
            SC = 1.0 / 16.0

            def mkset():
                d = dict(Pm=A.alloc([4, 256], BF16), PT=A.alloc([8, 128], BF16), attn=A.alloc([1024], BF16),
                         attnT=A.alloc([8, 128], BF16), sst=A.alloc([4, 4], F32))
                for k in list(d):
                    d["r_" + k] = Res()
                return d

            def softmax(B, sc_all, sc_h, sc_res, tb=5):
                sst, Pm, PT = B["sst"], B["Pm"], B["PT"]
                p.op("dve", lambda e: e.tensor_reduce(out=sst[:, 0, :], in_=sc_all, axis=AX.X, op=ALU.max),
                     reads=list(set(sc_res)), writes=[B["r_sst"]])
                p.op("dve", lambda e: e.tensor_scalar(out=sst[:, 1, :], in0=sst[:, 0, :], scalar1=-SC, scalar2=None,
                                                      op0=ALU.mult), writes=[B["r_sst"]])
                for h in range(4):
                    p.op("act", lambda e: e.activation(out=Pm[:, h, :], in_=sc_h[h], func=AF.Exp, scale=SC,
                                                       bias=sst[:, 1, h:h + 1], accum_out=sst[:, 2, h:h + 1]),
                         reads=[sc_res[h], B["r_sst"]], writes=[B["r_Pm"], B["r_sst"]])
                p.op("dve", lambda e: e.reciprocal(out=sst[:, 3, :], in_=sst[:, 2, :]), writes=[B["r_sst"]])
                for h in range(4):
                    for mc in range(2):
                        j = h * 2 + mc
                        p.op("pe", lambda e: e.transpose(bank_bf[tb][:, j * 128:(j + 1) * 128],
                                                         Pm[:, h, mc * 128:(mc + 1) * 128], ident),
                             reads=[B["r_Pm"], r_ident], writes=[bres[tb]])
                p.op("act", lambda e: e.activation(out=PT, in_=bank_bf[tb].rearrange("p (j n) -> p j n", j=8),
                                                   func=AF.Copy), reads=[bres[tb]], writes=[B["r_PT"]])

            def out_proj(B, gi, pv_all, pv_res, yb, tb=5):
                sst, attn, attnT = B["sst"], B["attn"], B["attnT"]
                p.op("dve", lambda e: e.tensor_tensor(
                    out=attn.rearrange("p (h d) -> p h d", h=4), in0=pv_all,
                    in1=sst[:, 3, :].unsqueeze(2).to_broadcast([128, 4, 256]), op=ALU.mult),
                    reads=list(set(pv_res)) + [B["r_sst"]], writes=[B["r_attn"]])
                for k in range(8):
                    p.op("pe", lambda e: e.transpose(bank_bf[tb][:, k * 128:(k + 1) * 128],
                                                     attn[:, k * 128:(k + 1) * 128], ident),
                         reads=[B["r_attn"], r_ident], writes=[bres[tb]])
                p.op("act", lambda e: e.activation(out=attnT, in_=bank_bf[tb].rearrange("p (j n) -> p j n", j=8),
                                                   func=AF.Copy), reads=[bres[tb]], writes=[B["r_attnT"]])
                for k in range(8):
                    for hc in range(2):
                        p.op("pe", lambda e: e.matmul(bank[yb + hc], attnT[:, k, :],
                                                      wo_sb[:, k, hc * 512:(hc + 1) * 512],
                                                      start=(k == 0), stop=(k == 7)),
                             reads=[B["r_attnT"], r_wo], writes=[bres[yb + hc]])
                p.op("dve", lambda e: e.tensor_tensor(out=xres[:, gi, :], in0=bank2(yb), in1=xres[:, gi, :],
                                                      op=ALU.add),
                     reads=[bres[yb], bres[yb + 1]], writes=[xr[gi]])

            def q_proj(nb, xnT_, r_xnT_, qT_, r_qT_, qb_=5):
                for c in range(8):
                    for k in range(8):
                        p.op("pe", lambda e: e.matmul(bank[qb_][:, 0:nb], wq_sb[:, k, c * 128:(c + 1) * 128],
                                                      xnT_[:, k, 0:nb], start=(k == 0), stop=(k == 7)),
                             reads=[r_wq, r_xnT_], writes=[bres[qb_]])
                    p.op("act", lambda e: e.activation(out=qT_[:, c, 0:nb], in_=bank[qb_][:, 0:nb], func=AF.Copy),
                         reads=[bres[qb_]], writes=[r_qT_])

            def pair(b0):
                v = psum[:, b0 * 512:(b0 + 2) * 512]
                return dict(all=v.rearrange("p (h m) -> p h m", h=4),
                            h=[v[:, h * 256:(h + 1) * 256] for h in range(4)],
                            res=[bres[b0], bres[b0], bres[b0 + 1], bres[b0 + 1]], flat=v)
            pairs = [pair(0), pair(2)]

            xnTb = [A.alloc([8, 512], BF16)] * 2
            r_xnTb = [Res()] * 2
            qTb = [A.alloc([8, 512], BF16) for _ in range(2)]
            r_qTb = [Res(), Res()]
            sets = [mkset(), mkset()]

            def prompt_stream():
                for bi, (t0, nb) in enumerate(BLOCKS[:4]):
                    xb, rxb, qb, rqb = xnTb[bi % 2], r_xnTb[bi % 2], qTb[bi % 2], r_qTb[bi % 2]
                    norm_block(t0, nb, xb, rxb)
                    q_proj(nb, xb, rxb, qb, rqb)
                    yield
                    for tt in range(nb // 128):
                        gi = t0 // 128 + tt
                        B = sets[gi % 2]
                        PP = pairs[gi % 2]
                        tl = slice(tt * 128, (tt + 1) * 128)
                        for h in range(4):
                            for kk in range(2):
                                p.op("pe", lambda e: e.matmul(PP["h"][h], qb[:, 2 * h + kk, tl], KTp[:, 2 * h + kk, :],
                                                              start=(kk == 0), stop=(kk == 1)),
                                     reads=[rqb, r_KTp], writes=[PP["res"][h]])
                        softmax(B, PP["all"], PP["h"], PP["res"])
                        for h in range(4):
                            for mc in range(2):
                                p.op("pe", lambda e: e.matmul(PP["h"][h], B["PT"][:, h * 2 + mc, :],
                                                              Vp[:, mc, h * 256:(h + 1) * 256],
                                                              start=(mc == 0), stop=(mc == 1)),
                                     reads=[B["r_PT"], r_Vp], writes=[PP["res"][h]])
                        out_proj(B, gi, PP["all"], PP["res"], 6)
                        yield

            def sample_stream():
                xs = A.alloc([8, 128], BF16)
                qs = A.alloc([8, 128], BF16)
                r_xs, r_qs = Res(), Res()
                B = mkset()
                msk = [A.alloc([8, 2, 128], BF16) for _ in range(2)]
                r_msk = [Res(), Res()]
                kb = [A.alloc([2, 1024], BF16) for _ in range(2)]
                r_kb = [Res(), Res()]
                KTs = [A.alloc([8, 256], BF16) for _ in range(2)]
                r_KTs = [Res(), Res()]
                acc = A.alloc([1024], F32)
                r_acc = Res()
                norm_tile(16, xs, r_xs, 0, 4)
                q_proj(128, xs, r_xs, qs, r_qs, 4)
                p.op("dve", lambda e: e.memset(acc, 0.0), writes=[r_acc])
                yield
                for s_ in range(16):
                    b2 = s_ % 2
                    g = s_ // 2
                    g2 = g % 2
                    if s_ % 2 == 0:
                        for c in range(8):
                            p.op("pool", lambda e: e.tensor_tensor(
                                out=msk[g2][:, c, :, :],
                                in0=qs[:, c, :].unsqueeze(1).to_broadcast([128, 2, 128]),
                                in1=bmask[:, 2 * g:2 * g + 2, :], op=ALU.mult),
                                reads=[r_qs, r_c], writes=[r_msk[g2]])
                    p.dma("pool", kb[b2], ck[s_].rearrange("(t p) d -> p t d", p=128), key="kv%d" % b2,
                          writes=[r_kb[b2]])
                    transpose_2x1024(kb[b2], r_kb[b2], KTs[b2], r_KTs[b2], 4)
                    for hp in range(2):
                        for hh in range(2):
                            h = hp * 2 + hh
                            for kk in range(2):
                                p.op("pe", lambda e: e.matmul(
                                    bank[4][:, hh * 256:(hh + 1) * 256], msk[g2][:, 2 * h + kk, s_ % 2, :],
                                    KTs[b2][:, 2 * h + kk, :], start=(kk == 0), stop=(kk == 1)),
                                    reads=[r_msk[g2], r_KTs[b2]], writes=[bres[4]])
                        p.op("dve", lambda e: e.tensor_tensor(out=acc[:, hp * 512:(hp + 1) * 512], in0=bank[4],
                                                              in1=acc[:, hp * 512:(hp + 1) * 512], op=ALU.add),
                             reads=[bres[4]], writes=[r_acc])
                    yield
                acc_h = [acc[:, h * 256:(h + 1) * 256] for h in range(4)]
                softmax(B, acc.rearrange("p (h m) -> p h m", h=4), acc_h, [r_acc] * 4, 4)
                acc2 = acc
                r_acc2 = r_acc
                p.op("dve", lambda e: e.memset(acc2, 0.0), writes=[r_acc2])
                yield
                for s_ in range(16):
                    b2 = s_ % 2
                    g = s_ // 2
                    g2 = g % 2
                    if s_ % 2 == 0:
                        for j in range(8):
                            p.op("pool", lambda e: e.tensor_tensor(
                                out=msk[g2][:, j, :, :],
                                in0=B["PT"][:, j, :].unsqueeze(1).to_broadcast([128, 2, 128]),
                                in1=bmask[:, 2 * g:2 * g + 2, :], op=ALU.mult),
                                reads=[B["r_PT"], r_c], writes=[r_msk[g2]])
                    p.dma("pool", kb[b2], cv[s_].rearrange("(t p) d -> p t d", p=128), key="kv%d" % b2,
                          writes=[r_kb[b2]])
                    for hp in range(2):
                        for hh in range(2):
                            h = hp * 2 + hh
                            for mc in range(2):
                                p.op("pe", lambda e: e.matmul(
                                    bank[4][:, hh * 256:(hh + 1) * 256], msk[g2][:, h * 2 + mc, s_ % 2, :],
                                    kb[b2][:, mc, h * 256:(h + 1) * 256], start=(mc == 0), stop=(mc == 1)),
                                    reads=[r_msk[g2], r_kb[b2]], writes=[bres[4]])
                        p.op("dve", lambda e: e.tensor_tensor(out=acc2[:, hp * 512:(hp + 1) * 512], in0=bank[4],
                                                              in1=acc2[:, hp * 512:(hp + 1) * 512], op=ALU.add),
                             reads=[bres[4]], writes=[r_acc2])
                    yield
                out_proj(B, 16, acc2.rearrange("p (h m) -> p h m", h=4), [r_acc2] * 4, 6, 4)
                yield

            interleave(prompt_stream(), sample_stream())
            A.release(m0)
            p.barrier()

        def dump_ret(retT, r_retT):
            pass

        def final_out():
            load_gain("final")
            m0 = A.mark()
            ybuf = [A.alloc([D], F32) for _ in range(4)]
            r_yb = [Res() for _ in range(4)]
            for i in range(NT):
                nb = i % 4
                p.op("act", lambda e, i=i: e.activation(out=junk, in_=xres[:, i, :], func=AF.Square,
                                                        accum_out=ssq[:, i:i + 1]),
                     reads=[xr[i]], writes=[r_junk, r_ssq[i]])
                p.op("dve", lambda e, i=i: e.tensor_scalar(out=tmp17[:, i:i + 1], in0=ssq[:, i:i + 1],
                                                           scalar1=1.0 / D, scalar2=EPS, op0=ALU.mult, op1=ALU.add),
                     writes=[r_ssq[i]])
                p.op("pool", lambda e, i=i: e.tensor_tensor(out=rstd[:, i:i + 1], in0=tmp17[:, i:i + 1],
                                                            in1=cm05[:, 0:1], op=ALU.pow),
                     reads=[r_cm], writes=[r_ssq[i]])
                p.op("dve", lambda e, i=i, nb=nb: e.scalar_tensor_tensor(
                    out=ybuf[nb], in0=xres[:, i, :], scalar=rstd[:, i:i + 1], in1=gbc, op0=ALU.mult, op1=ALU.mult),
                    reads=[xr[i], r_gbc, r_ssq[i]], writes=[r_yb[nb]])
                (p.dma("sp", yv_out[:, i, :], ybuf[nb], key="yo%d" % (i % 4), reads=[r_yb[nb]]))
            A.release(m0)
            p.barrier()

        def dump_x():
            for i in range(NT):
                (p.dma("sp", yv_out[:, i, :], xres[:, i, :], key="yo%d" % (i % 2), reads=[xr[i]]))

        ffn("ffn1")
        if stop_after == "ffn1":
            dump_x()
        else:
            retT = A.alloc([4, TOK], BF16)
            r_retT = [Res() for _ in range(NT)]
            mix_ret(retT, r_retT)
            if stop_after == "ret_dbg":
                dump_x()
            elif stop_after == "ret":
                dbg = dout("dbg_retT", [128, 4, TOK], BF16)
                (p.dma("sp", dbg, retT, key="dbg", reads=r_retT))
                dump_x()
            else:
                mix_lru(retT, r_retT)
                A.release(base_mark)
                if stop_after == "mix":
                    dump_x()
                else:
                    KTp = A.alloc([8, 256], BF16)
                    Vp = A.alloc([2, 1024], BF16)
                    r_KTp, r_Vp = Res(), Res()
                    xattn_kv(KTp, r_KTp, Vp, r_Vp)
                    if stop_after != "xattn_kv":
                        xattn_main(KTp, r_KTp, Vp, r_Vp)
                    A.release(base_mark)
                    if stop_after in ("xattn", "xattn_kv", "xattn_p", "xattn_s0", "xattn_s1", "xattn_s2"):
                        dump_x()
                    else:
                        ffn("ffn2")
                        if stop_after == "ffn2":
                            dump_x()
                        else:
                            final_out()
        p.emit()
    return nc


def _consts():
    c = {}
    c["c_ident"] = np.eye(128, dtype=np.float32).astype(ml_dtypes.bfloat16)
    c["c_ones"] = np.ones((128, 128), np.float32)
    half = 64
    inv = (10000.0 ** (-np.arange(half, dtype=np.float32) / half)).astype(np.float32)
    pos = np.concatenate([np.arange(2048), 16384 + (np.arange(128) % 8)]).astype(np.float32)
    ang = (pos[None, :] * inv[:, None]).astype(np.float32)
    cos = np.cos(ang.astype(np.float64)).astype(np.float32)
    sin = np.sin(ang.astype(np.float64)).astype(np.float32)
    c["c_cos"] = np.concatenate([cos, cos], 0)
    c["c_sin"] = np.concatenate([-sin, sin], 0)
    lg = np.log(1.0 - 2.0 ** (-5.0 - np.arange(4, dtype=np.float64)))
    scale = 128.0 ** -0.5
    idx = np.arange(128)
    dm = np.zeros((2, 128, 4, 128), np.float64)
    qd = np.zeros((2, 128, 4, 128), np.float64)
    kd = np.zeros((2, 128, 4), np.float64)
    diff = idx[None, :] - idx[:, None]
    s_id = idx // 8
    t_id = idx % 8
    same = (s_id[:, None] == s_id[None, :])
    dts = t_id[None, :] - t_id[:, None]
    for h in range(4):
        dm[0, :, h, :] = np.where(diff >= 0, np.exp(lg[h] * np.maximum(diff, 0)), 0.0) * scale
        dm[1, :, h, :] = np.where(same & (dts >= 0), np.exp(lg[h] * np.maximum(dts, 0)), 0.0) * scale
        qd[0, :, h, :] = np.exp(lg[h] * (idx + 1.0))[None, :]
        qd[1, :, h, :] = np.exp(lg[h] * (t_id + 1.0))[None, :]
        kd[0, :, h] = np.exp(lg[h] * (127.0 - idx)) * scale
        kd[1, :, h] = np.exp(lg[h] * (7.0 - t_id)) * scale
    c["c_dmask"] = dm.astype(np.float32)
    c["c_qdec"] = qd.astype(np.float32)
    c["c_kdec"] = kd.astype(np.float32)
    bm = (s_id[None, :] == np.arange(16)[:, None]).astype(np.float32)
    c["c_bmask"] = np.broadcast_to(bm[None], (128, 16, 128)).astype(ml_dtypes.bfloat16)
    c["c_rmask"] = np.ascontiguousarray(bm.T)
    return c


_NC_CACHE = {}


def _prep_inputs(inp, stop_after=None):
    f = lambda a: np.ascontiguousarray(np.asarray(a, dtype=np.float32))
    shared = {}
    for n in ("ffn1", "ffn2"):
        shared[n + "_wg"] = f(inp[n + "_wg"][0])
        shared[n + "_wu"] = f(inp[n + "_wu"][0])
        shared[n + "_wd"] = f(inp[n + "_wd"][0])
    shared["w_in"] = f(inp["w_in"][0])
    shared["w_out"] = f(inp["w_out"][0])
    shared["wq"] = f(inp["xattn_wq"][0])
    shared["wk"] = f(inp["xattn_wk"][0])
    shared["wv"] = f(inp["xattn_wv"][0])
    shared["wo"] = f(inp["xattn_wo"][0])
    bc = lambda v, n: np.ascontiguousarray(np.broadcast_to(f(v).reshape(1, n), (128, n)))
    shared["g_ffn1"] = bc(inp["ffn1_norm"][0], D)
    shared["g_mix"] = bc(inp["mix_norm"][0], D)
    shared["g_xattn"] = bc(inp["xattn_norm"][0], D)
    shared["g_ffn2"] = bc(inp["ffn2_norm"][0], D)
    shared["g_final"] = bc(inp["final_norm"], D)
    shared["gn_gain"] = bc(inp["ret_gn_gain"][0], 512)
    fm = lambda v: f(v).reshape(4, 128).T
    lv = np.zeros((128, 4, 9), np.float32)
    cw = f(inp["conv_w"][0])
    for j in range(4):
        lv[:, :, j] = fm(cw[j])
    lv[:, :, 4] = fm(inp["conv_b"][0])
    lv[:, :, 5] = fm(inp["lru_ba"][0])
    lv[:, :, 6] = fm(inp["lru_bx"][0])
    lv[:, :, 7] = fm(inp["lru_lambda"][0])
    lv[:, :, 8] = fm(inp["lru_norm"][0])
    shared["lruvec"] = lv

    def bd(w):
        w = f(w)
        o = np.zeros((128, 4, 128), np.float32)
        for c in range(4):
            for b in range(2):
                o[b * 64:(b + 1) * 64, c, b * 64:(b + 1) * 64] = w[2 * c + b]
        return o
    shared["wabd"] = bd(inp["lru_wa"][0])
    shared["wxbd"] = bd(inp["lru_wx"][0])
    shared.update(_consts())
    maps = []
    for c in range(NCORES):
        m = dict(shared)
        xs = f(inp["x_sample"][16 * c:16 * c + 16]).reshape(128, D)
        m["x"] = np.ascontiguousarray(np.concatenate([f(inp["x_prompt"][c]), xs], 0))
        m["st_ret"] = f(inp["state_ret"][0, 16 * c:16 * c + 16])
        h0 = f(inp["state_lru_h"][0, 16 * c:16 * c + 16])
        m["st_h"] = np.ascontiguousarray(h0.reshape(16, 4, 128).transpose(2, 1, 0))
        cv0 = f(inp["state_lru_conv"][0, 16 * c:16 * c + 16])
        m["st_conv"] = np.ascontiguousarray(cv0.reshape(16, 3, 4, 128).transpose(3, 2, 0, 1))
        m["ck"] = f(inp["cache_mem_k"][0, 16 * c:16 * c + 16]).reshape(16, 256, D)
        m["cv"] = f(inp["cache_mem_v"][0, 16 * c:16 * c + 16]).reshape(16, 256, D)
        m["memp"] = f(inp["mem_prompt"][c])
        maps.append(m)
    return maps


def _run(inp, stop_after=None, cores=None):
    key = stop_after
    if key not in _NC_CACHE:
        _NC_CACHE[key] = build_program(stop_after)
    nc = _NC_CACHE[key]
    maps = _prep_inputs(inp)
    if cores is not None:
        maps = [maps[c] for c in cores]
    res = run_bass_kernel_spmd(nc, maps, core_ids=list(range(len(maps))))
    return res.results


def kernel(**inp):
    rs = _run(inp)
    g = lambda n: [np.asarray(r[n], dtype=np.float32) for r in rs]
    y = g("y")
    y_prompt = np.stack([a[0:2048] for a in y], 0)
    y_sample = np.concatenate([a[2048:].reshape(16, 8, D) for a in y], 0)
    new_ret_p = np.stack(g("o_retp"), 0)[None]
    new_h_p = np.stack([a.T.reshape(512) for a in g("o_hp")], 0)[None]
    new_conv_p = np.stack([a.transpose(2, 1, 0).reshape(3, 512) for a in g("o_convp")], 0)[None]
    mk = np.stack([a.reshape(256, 4, 256) for a in g("o_mk")], 0)[None]
    mv = np.stack([a.reshape(256, 4, 256) for a in g("o_mv")], 0)[None]
    new_ret_s = np.concatenate(g("o_rets"), 0)[None]
    new_h_s = np.concatenate([a.transpose(2, 1, 0).reshape(16, 512) for a in g("o_hs")], 0)[None]
    new_conv_s = np.concatenate([a.transpose(2, 3, 1, 0).reshape(16, 3, 512) for a in g("o_convs")], 0)[None]
    return (y_prompt, y_sample, new_ret_p, new_h_p, new_conv_p, mk, mv, new_ret_s, new_h_s, new_conv_s)
```
